# Optimizing a Trainium2 kernel written in Bass

```python
import math
import jax, jax.numpy as jnp
from jax import lax
import numpy as np

D_MODEL = 1024
BATCH = 32
SEQ = 256
DEPTH = 4
DEC_BATCH = 2
DEC_SEQ = 1024
PAST_LEN = 256

GRID_W = 64
N_MIXERS = 4
N_GLA_L = len(range(0, DEPTH, N_MIXERS))
N_NAT_L = len(range(1, DEPTH, N_MIXERS))
N_GM_L = len(range(2, DEPTH, N_MIXERS))
N_SSD_L = len(range(3, DEPTH, N_MIXERS))
N_SUB = 3
N_MOD = 3 * N_SUB
D_FF = 2816
EPS = 1e-6
NEG_INF = -1e30
ROPE_THETA = 10000.0
GLA_H = 4
GLA_DK = 128
GLA_DV = 256
GLA_RANK = 16
GLA_TAU = 16.0
GLA_CHUNK = 16
GLA_IN = 2 * GLA_H * GLA_DK + 2 * GLA_H * GLA_DV
NAT_H = 16
NAT_HD = 64
NAT_WH = 8
NAT_WW = 16
NAT_QB = 16
NAT_KB = 32
GM_DH = 1024
GM_G = 8
GM_CHUNK = 128
GM_CG = GM_DH // GM_G
SSD_DI = 2 * D_MODEL
SSD_P = 64
SSD_H = SSD_DI // SSD_P
SSD_N = 128
SSD_G = 4
SSD_CONV = 3
SSD_CHUNK = 64
SSD_XBC = SSD_DI + 2 * SSD_G * SSD_N
SSD_IN = SSD_DI + SSD_XBC + 2 * SSD_H

kernel_name = "hybrid_flow_trunk_prefix_ctx"


def rmsnorm(x, g):
    x32 = x.astype(jnp.float32)
    y = x32 * lax.rsqrt(jnp.mean(x32 * x32, axis=-1, keepdims=True) + EPS)
    return y.astype(x.dtype) * g


def layernorm(x, g, b):
    x32 = x.astype(jnp.float32)
    xc = x32 - jnp.mean(x32, axis=-1, keepdims=True)
    y = xc * lax.rsqrt(jnp.mean(xc * xc, axis=-1, keepdims=True) + EPS)
    return y.astype(x.dtype) * g + b


def modulation(cond, w_ada, b_ada):
    m = jax.nn.silu(cond) @ w_ada + b_ada
    return m.reshape(cond.shape[0], N_MOD, D_MODEL)


def pre_mod(x, g, mod, k):
    shift = mod[:, 3 * k][:, None]
    scale = mod[:, 3 * k + 1][:, None]
    gate = mod[:, 3 * k + 2][:, None]
    return rmsnorm(x, g) * (1.0 + scale) + shift, gate


def swiglu(h, w_in, w_out):
    a, u = jnp.split(h @ w_in, 2, axis=-1)
    return (jax.nn.silu(a) * u) @ w_out


def rope_2d(x):
    T, dh = x.shape[1], x.shape[-1]
    half = dh // 2
    t = jnp.arange(T)
    inv = ROPE_THETA ** (-jnp.arange(0, half, 2, dtype=jnp.float32) / half)

    def rot(xa, pos):
        ang = pos.astype(jnp.float32)[:, None] * inv
        cos = jnp.cos(ang)[None, :, None, :]
        sin = jnp.sin(ang)[None, :, None, :]
        x1, x2 = jnp.split(xa.astype(jnp.float32), 2, axis=-1)
        return jnp.concatenate([x1 * cos - x2 * sin, x1 * sin + x2 * cos], axis=-1)

    out = jnp.concatenate([rot(x[..., :half], t // GRID_W), rot(x[..., half:], t % GRID_W)], axis=-1)
    return out.astype(x.dtype)


def flip_t(t):
    return jnp.flip(t, axis=1)


def gla_chunked(q, k, v, log_a, s0):
    B_, T, H, _ = q.shape
    dv = v.shape[-1]
    C = GLA_CHUNK
    n = T // C

    def blk(t):
        return t.astype(jnp.float32).reshape(B_, n, C, H, -1).transpose(0, 1, 3, 2, 4)

    q, k, v, la = blk(q), blk(k), blk(v), blk(log_a)
    b = jnp.cumsum(la, axis=3)
    tril = jnp.tril(jnp.ones((C, C), dtype=bool))
    diff = b[:, :, :, :, None, :] - b[:, :, :, None, :, :]
    decay = jnp.exp(jnp.where(tril[:, :, None], diff, -jnp.inf))
    attn = jnp.einsum('bnhid,bnhjd,bnhijd->bnhij', q, k, decay)
    o_intra = jnp.einsum('bnhij,bnhjv->bnhiv', attn, v)
    b_last = b[:, :, :, -1:, :]
    q_dec = q * jnp.exp(b)
    u = jnp.einsum('bnhjd,bnhjv->bnhdv', k * jnp.exp(b_last - b), v)
    g = jnp.exp(b_last[:, :, :, 0])

    def step(s, xs):
        qd, un, gn = xs
        o = jnp.einsum('bhid,bhdv->bhiv', qd, s)
        return gn[..., None] * s + un, o

    s_fin, o_inter = lax.scan(step, s0.astype(jnp.float32),
                              (jnp.swapaxes(q_dec, 0, 1), jnp.swapaxes(u, 0, 1), jnp.swapaxes(g, 0, 1)))
    o = o_intra + jnp.swapaxes(o_inter, 0, 1)
    return o.transpose(0, 1, 3, 2, 4).reshape(B_, T, H, dv), s_fin


def gla_mixer(h, s0, w_in, w_a1, w_a2, b_a, norm_g, w_out, use_rope):
    B_, T, _ = h.shape
    q, k, v, r = jnp.split(h @ w_in, [GLA_H * GLA_DK, 2 * GLA_H * GLA_DK,
                                      2 * GLA_H * GLA_DK + GLA_H * GLA_DV], axis=-1)
    q = q.reshape(B_, T, GLA_H, GLA_DK) * (GLA_DK ** -0.5)
    k = k.reshape(B_, T, GLA_H, GLA_DK)
    v = v.reshape(B_, T, GLA_H, GLA_DV)
    if use_rope:
        q, k = rope_2d(q), rope_2d(k)
    z = jnp.einsum('btd,edr->bter', h, w_a1)
    z = jnp.einsum('bter,erk->btek', z, w_a2) + b_a
    log_a = (jax.nn.log_sigmoid(z.astype(jnp.float32)) / GLA_TAU).reshape(B_, T, 2, GLA_H, GLA_DK)
    o_f, s_f = gla_chunked(q, k, v, log_a[:, :, 0], s0[:, 0])
    o_b, s_b = gla_chunked(flip_t(q), flip_t(k), flip_t(v), flip_t(log_a[:, :, 1]), s0[:, 1])
    o = o_f + flip_t(o_b)
    o = rmsnorm(o, norm_g.reshape(GLA_H, GLA_DV)).astype(h.dtype)
    o = o.reshape(B_, T, GLA_H * GLA_DV) * jax.nn.silu(r)
    return o @ w_out, jnp.stack([s_f, s_b], axis=1)


def nat_tables(rows):
    wh = min(NAT_WH, rows)
    r = np.arange(rows)
    row_idx = np.clip(r - wh // 2, 0, rows - wh)[:, None] + np.arange(wh)
    ncb = GRID_W // NAT_QB
    col_idx = np.clip(np.arange(ncb) * NAT_QB - (NAT_KB - NAT_QB) // 2, 0,
                      GRID_W - NAT_KB)[:, None] + np.arange(NAT_KB)
    qcol = np.arange(ncb)[:, None] * NAT_QB + np.arange(NAT_QB)
    c_start = np.clip(qcol - NAT_WW // 2, 0, GRID_W - NAT_WW)
    kc = col_idx[:, None, :]
    col_ok = (kc >= c_start[..., None]) & (kc < c_start[..., None] + NAT_WW)
    dc = kc - qcol[..., None]
    dr = row_idx - r[:, None]
    full = (rows, ncb, NAT_QB, wh, NAT_KB)
    flat = (rows, ncb, NAT_QB, wh * NAT_KB)
    dr_i = np.broadcast_to(dr[:, None, None, :, None] + NAT_WH - 1, full).reshape(flat)
    dc_i = np.broadcast_to(np.clip(dc + NAT_WW - 1, 0, 2 * NAT_WW - 2)[None, :, :, None, :], full).reshape(flat)
    ok = np.broadcast_to(col_ok[None, :, :, None, :], full).reshape(flat)
    return row_idx, col_idx, dr_i, dc_i, ok


def nat_context(h, w_qkv, w_out):
    B_, S, _ = h.shape
    q, k, v = jnp.split(h @ w_qkv, 3, axis=-1)
    q = q.reshape(B_, S, NAT_H, NAT_HD)
    k = k.reshape(B_, S, NAT_H, NAT_HD)
    v = v.reshape(B_, S, NAT_H, NAT_HD)
    s = jnp.einsum('bqhd,bkhd->bhqk', q, k).astype(jnp.float32) * (NAT_HD ** -0.5)
    p = jax.nn.softmax(s, axis=-1).astype(h.dtype)
    o = jnp.einsum('bhqk,bkhd->bqhd', p, v).reshape(B_, S, D_MODEL)
    return o @ w_out, k.transpose(0, 2, 1, 3), v.transpose(0, 2, 1, 3)


def nat_latent(h, ck, cv, w_qkv, rpb, w_out):
    B_, T, _ = h.shape
    rows = T // GRID_W
    ncb = GRID_W // NAT_QB
    row_idx, col_idx, dr_i, dc_i, ok = nat_tables(rows)
    q, k, v = jnp.split(h @ w_qkv, 3, axis=-1)
    q = q.reshape(B_, rows, ncb, NAT_QB, NAT_H, NAT_HD)
    k = k.reshape(B_, rows, GRID_W, NAT_H, NAT_HD)
    v = v.reshape(B_, rows, GRID_W, NAT_H, NAT_HD)
    ri = row_idx[:, None, :, None]
    ci = col_idx[None, :, None, :]
    kb = k[:, ri, ci].reshape(B_, rows, ncb, -1, NAT_H, NAT_HD)
    vb = v[:, ri, ci].reshape(B_, rows, ncb, -1, NAT_H, NAT_HD)
    nk = kb.shape[3]
    scale = NAT_HD ** -0.5
    s_lat = jnp.einsum('brnqhd,brnkhd->bhrnqk', q, kb).astype(jnp.float32) * scale
    s_lat = s_lat + rpb[:, dr_i, dc_i].astype(jnp.float32)
    s_lat = jnp.where(ok, s_lat, NEG_INF)
    s_ctx = jnp.einsum('brnqhd,bhsd->bhrnqs', q, ck).astype(jnp.float32) * scale
    p = jax.nn.softmax(jnp.concatenate([s_lat, s_ctx], axis=-1), axis=-1).astype(h.dtype)
    o = (jnp.einsum('bhrnqk,brnkhd->brnqhd', p[..., :nk], vb)
         + jnp.einsum('bhrnqs,bhsd->brnqhd', p[..., nk:], cv))
    return o.reshape(B_, T, D_MODEL) @ w_out


def gmlp_mixer(h, w_in, ln_g, ln_b, w_s, b_s, w_out):
    B_, T, _ = h.shape
    u, v = jnp.split(jax.nn.gelu(h @ w_in), 2, axis=-1)
    v = layernorm(v, ln_g, ln_b).reshape(B_, T // GM_CHUNK, GM_CHUNK, GM_G, GM_CG)
    v = jnp.einsum('gpq,bnqgc->bnpgc', w_s, v) + b_s.T[None, None, :, :, None]
    return (u * v.reshape(B_, T, GM_DH)) @ w_out


def dwconv_centred(x, w, b):
    K = w.shape[0]
    T = x.shape[1]
    pad = K // 2
    xp = jnp.pad(x, ((0, 0), (pad, pad), (0, 0)))
    out = xp[:, 0:T] * w[0]
    for i in range(1, K):
        out = out + xp[:, i:i + T] * w[i]
    return out + b


def ssd_chunked(x, dt, a, bm, cm, s0):
    B_, T = x.shape[:2]
    L = SSD_CHUNK
    n = T // L
    E = SSD_H // SSD_G
    f32 = jnp.float32
    x = x.astype(f32).reshape(B_, n, L, SSD_G, E, SSD_P)
    dt = dt.reshape(B_, n, L, SSD_G, E)
    bm = bm.astype(f32).reshape(B_, n, L, SSD_G, SSD_N)
    cm = cm.astype(f32).reshape(B_, n, L, SSD_G, SSD_N)
    cum = jnp.cumsum(dt * a.reshape(SSD_G, E), axis=2)
    tril = jnp.tril(jnp.ones((L, L), dtype=bool))
    seg = cum[:, :, :, None] - cum[:, :, None, :]
    decay = jnp.exp(jnp.where(tril[:, :, None, None], seg, -jnp.inf))
    dtx = x * dt[..., None]
    cb = jnp.einsum('bnigs,bnjgs->bnijg', cm, bm)
    y_diag = jnp.einsum('bnijg,bnijge,bnjgep->bnigep', cb, decay, dtx)
    u = jnp.einsum('bnjgs,bnjge,bnjgep->bngeps', bm, jnp.exp(cum[:, :, -1:] - cum), dtx)
    chunk_decay = jnp.exp(cum[:, :, -1])
    q_decay = jnp.exp(cum)

    def step(s, xs):
        c_n, qd_n, u_n, g_n = xs
        y = jnp.einsum('bigs,bige,bgeps->bigep', c_n, qd_n, s)
        return g_n[..., None, None] * s + u_n, y

    sw = lambda t: jnp.swapaxes(t, 0, 1)
    s_fin, y_off = lax.scan(step, s0.astype(f32).reshape(B_, SSD_G, E, SSD_P, SSD_N),
                            (sw(cm), sw(q_decay), sw(u), sw(chunk_decay)))
    y = y_diag + sw(y_off)
    return y.reshape(B_, T, SSD_H, SSD_P), s_fin.reshape(B_, SSD_H, SSD_P, SSD_N)


def ssd_mixer(h, s0, w_in, conv_w, conv_b, dt_bias, a_log, d_skip, norm_g, w_out):
    B_, T, _ = h.shape
    z, xbc, dt = jnp.split(h @ w_in, [SSD_DI, SSD_DI + SSD_XBC], axis=-1)
    xbc = jax.nn.silu(dwconv_centred(xbc, conv_w, conv_b))
    x, bm, cm = jnp.split(xbc, [SSD_DI, SSD_DI + SSD_G * SSD_N], axis=-1)
    x = x.reshape(B_, T, SSD_H, SSD_P)
    bm = bm.reshape(B_, T, SSD_G, SSD_N)
    cm = cm.reshape(B_, T, SSD_G, SSD_N)
    dt = jax.nn.softplus(dt.astype(jnp.float32).reshape(B_, T, 2, SSD_H) + dt_bias)
    a = -jnp.exp(a_log.astype(jnp.float32))
    y_f, s_f = ssd_chunked(x, dt[:, :, 0], a[0], bm, cm, s0[:, 0])
    y_b, s_b = ssd_chunked(flip_t(x), flip_t(dt[:, :, 1]), a[1], flip_t(bm), flip_t(cm), s0[:, 1])
    y = y_f + flip_t(y_b) + d_skip[:, None] * x.astype(jnp.float32)
    y = rmsnorm(y.reshape(B_, T, SSD_DI) * jax.nn.silu(z.astype(jnp.float32)), norm_g).astype(h.dtype)
    return y @ w_out, jnp.stack([s_f, s_b], axis=1)


def setup_inputs(seed: int = 0) -> dict:
    key = jax.random.key(seed)
    keys = iter(jax.random.split(key, 64))

    def nrm(shape, scale):
        return scale * jax.random.normal(next(keys), shape, jnp.float32)

    def unif(shape, lo, hi):
        return jax.random.uniform(next(keys), shape, jnp.float32, lo, hi)

    D = D_MODEL
    dt0 = jnp.exp(unif((N_SSD_L, 2, SSD_H), math.log(1e-3), math.log(1e-1)))
    return {
        'x_prompt': nrm((BATCH, SEQ, D), 1.0),
        'x_sample': nrm((DEC_BATCH, DEC_SEQ, D), 1.0),
        'state_gla': nrm((DEC_BATCH, N_GLA_L, 2, GLA_H, GLA_DK, GLA_DV), 0.1),
        'cache_nat_k': nrm((DEC_BATCH, N_NAT_L, NAT_H, PAST_LEN, NAT_HD), 1.0),
        'cache_nat_v': nrm((DEC_BATCH, N_NAT_L, NAT_H, PAST_LEN, NAT_HD), 1.0),
        'state_ssd': nrm((DEC_BATCH, N_SSD_L, 2, SSD_H, SSD_P, SSD_N), 0.1),
        'c': nrm((DEC_BATCH, D), 1.0),
        'c_ctx': nrm((D,), 1.0),
        'norm_g': 1.0 + nrm((DEPTH, N_SUB, D), 0.02),
        'w_ada': nrm((DEPTH, D, N_MOD * D), 0.5 * D ** -0.5),
        'b_ada': nrm((DEPTH, N_MOD * D), 0.02),
        'w_ffn_in': nrm((DEPTH, 2, D, 2 * D_FF), D ** -0.5),
        'w_ffn_out': nrm((DEPTH, 2, D_FF, D), D_FF ** -0.5),
        'gla_w_in': nrm((N_GLA_L, D, GLA_IN), D ** -0.5),
        'gla_w_a1': nrm((N_GLA_L, 2, D, GLA_RANK), D ** -0.5),
        'gla_w_a2': nrm((N_GLA_L, 2, GLA_RANK, GLA_H * GLA_DK), GLA_RANK ** -0.5),
        'gla_b_a': nrm((N_GLA_L, 2, GLA_H * GLA_DK), 0.5),
        'gla_norm_g': 1.0 + nrm((N_GLA_L, GLA_H * GLA_DV), 0.02),
        'gla_w_out': nrm((N_GLA_L, GLA_H * GLA_DV, D), (GLA_H * GLA_DV) ** -0.5),
        'nat_w_qkv': nrm((N_NAT_L, D, 3 * D), D ** -0.5),
        'nat_rpb': nrm((N_NAT_L, NAT_H, 2 * NAT_WH - 1, 2 * NAT_WW - 1), 0.02),
        'nat_w_out': nrm((N_NAT_L, D, D), D ** -0.5),
        'gm_w_in': nrm((N_GM_L, D, 2 * GM_DH), D ** -0.5),
        'gm_ln_g': 1.0 + nrm((N_GM_L, GM_DH), 0.02),
        'gm_ln_b': nrm((N_GM_L, GM_DH), 0.02),
        'gm_w_s': nrm((N_GM_L, GM_G, GM_CHUNK, GM_CHUNK), GM_CHUNK ** -0.5),
        'gm_b_s': 1.0 + nrm((N_GM_L, GM_G, GM_CHUNK), 0.02),
        'gm_w_out': nrm((N_GM_L, GM_DH, D), GM_DH ** -0.5),
        'ssd_w_in': nrm((N_SSD_L, D, SSD_IN), D ** -0.5),
        'ssd_conv_w': nrm((N_SSD_L, SSD_CONV, SSD_XBC), SSD_CONV ** -0.5),
        'ssd_conv_b': nrm((N_SSD_L, SSD_XBC), 0.02),
        'ssd_dt_bias': dt0 + jnp.log(-jnp.expm1(-dt0)),
        'ssd_a_log': jnp.log(unif((N_SSD_L, 2, SSD_H), 1.0, 16.0)),
        'ssd_d': 1.0 + nrm((N_SSD_L, SSD_H), 0.1),
        'ssd_norm_g': 1.0 + nrm((N_SSD_L, SSD_DI), 0.02),
        'ssd_w_out': nrm((N_SSD_L, SSD_DI, D), SSD_DI ** -0.5),
        'final_g': 1.0 + nrm((D,), 0.02),
    }


def reference(x_prompt, x_sample, state_gla, cache_nat_k, cache_nat_v, state_ssd, c,
              c_ctx, norm_g, w_ada, b_ada, w_ffn_in, w_ffn_out,
              gla_w_in, gla_w_a1, gla_w_a2, gla_b_a, gla_norm_g, gla_w_out,
              nat_w_qkv, nat_rpb, nat_w_out,
              gm_w_in, gm_ln_g, gm_ln_b, gm_w_s, gm_b_s, gm_w_out,
              ssd_w_in, ssd_conv_w, ssd_conv_b, ssd_dt_bias, ssd_a_log, ssd_d, ssd_norm_g, ssd_w_out,
              final_g):
    xp, xs = x_prompt, x_sample
    new_gla, new_k, new_v, new_ssd = [], [], [], []
    for l in range(DEPTH):
        kind, j = l % N_MIXERS, l // N_MIXERS
        mp = modulation(c_ctx[None], w_ada[l], b_ada[l])
        ms = modulation(c, w_ada[l], b_ada[l])
        hp, gp = pre_mod(xp, norm_g[l, 0], mp, 0)
        hs, gs = pre_mod(xs, norm_g[l, 0], ms, 0)
        xp = xp + 0.5 * gp * swiglu(hp, w_ffn_in[l, 0], w_ffn_out[l, 0])
        xs = xs + 0.5 * gs * swiglu(hs, w_ffn_in[l, 0], w_ffn_out[l, 0])
        hp, gp = pre_mod(xp, norm_g[l, 1], mp, 1)
        hs, gs = pre_mod(xs, norm_g[l, 1], ms, 1)
        if kind == 0:
            zero_state = jnp.zeros((hp.shape[0], 2, GLA_H, GLA_DK, GLA_DV), jnp.float32)
            op, st = gla_mixer(hp, zero_state, gla_w_in[j], gla_w_a1[j], gla_w_a2[j], gla_b_a[j],
                               gla_norm_g[j], gla_w_out[j], False)
            os_, _ = gla_mixer(hs, state_gla[:, j], gla_w_in[j], gla_w_a1[j], gla_w_a2[j], gla_b_a[j],
                               gla_norm_g[j], gla_w_out[j], True)
            new_gla.append(st)
        elif kind == 1:
            op, kc, vc = nat_context(hp, nat_w_qkv[j], nat_w_out[j])
            os_ = nat_latent(hs, cache_nat_k[:, j], cache_nat_v[:, j], nat_w_qkv[j], nat_rpb[j], nat_w_out[j])
            new_k.append(kc)
            new_v.append(vc)
        elif kind == 2:
            op = gmlp_mixer(hp, gm_w_in[j], gm_ln_g[j], gm_ln_b[j], gm_w_s[j], gm_b_s[j], gm_w_out[j])
            os_ = gmlp_mixer(hs, gm_w_in[j], gm_ln_g[j], gm_ln_b[j], gm_w_s[j], gm_b_s[j], gm_w_out[j])
        else:
            zero_state = jnp.zeros((hp.shape[0], 2, SSD_H, SSD_P, SSD_N), jnp.float32)
            op, st = ssd_mixer(hp, zero_state, ssd_w_in[j], ssd_conv_w[j], ssd_conv_b[j], ssd_dt_bias[j],
                               ssd_a_log[j], ssd_d[j], ssd_norm_g[j], ssd_w_out[j])
            os_, _ = ssd_mixer(hs, state_ssd[:, j], ssd_w_in[j], ssd_conv_w[j], ssd_conv_b[j], ssd_dt_bias[j],
                               ssd_a_log[j], ssd_d[j], ssd_norm_g[j], ssd_w_out[j])
            new_ssd.append(st)
        xp = xp + gp * op
        xs = xs + gs * os_
        hp, gp = pre_mod(xp, norm_g[l, 2], mp, 2)
        hs, gs = pre_mod(xs, norm_g[l, 2], ms, 2)
        xp = xp + 0.5 * gp * swiglu(hp, w_ffn_in[l, 1], w_ffn_out[l, 1])
        xs = xs + 0.5 * gs * swiglu(hs, w_ffn_in[l, 1], w_ffn_out[l, 1])
    y_prompt = rmsnorm(xp, final_g)
    y_sample = rmsnorm(xs, final_g)
    return (y_prompt, y_sample, jnp.stack(new_gla, axis=1), jnp.stack(new_k, axis=1),
            jnp.stack(new_v, axis=1), jnp.stack(new_ssd, axis=1))
```

```python
import numpy as np
from contextlib import ExitStack
import concourse.bass as bass
import concourse.mybir as mybir
from concourse.bass_utils import run_bass_kernel_spmd

F32 = mybir.dt.float32
BF16 = mybir.dt.bfloat16
AF = mybir.ActivationFunctionType
ALU = mybir.AluOpType
AX = mybir.AxisListType

NCORES = 8
D = 1024
NTOK = 2048
TT = 4
DFF = 2816
NFC = 22
FGROUPS = [(0, 6), (6, 6), (12, 5), (17, 5)]
EPS = 1e-6


LAST_NAMES = []
NATV = 0


class Buf:
    __slots__ = ("t", "lw", "rd", "name", "psum")

    def __init__(self, t, name=""):
        self.psum = False
        self.t = t
        self.lw = None
        self.rd = {}
        self.name = name

    def __getitem__(self, k):
        return self.t[k]


class View:
    def __init__(self, parent, ap):
        self.parent = parent
        self.ap = ap
        self.psum = parent.psum
        self.name = parent.name

    lw = property(lambda self: self.parent.lw, lambda self, v: setattr(self.parent, "lw", v))
    rd = property(lambda self: self.parent.rd, lambda self, v: setattr(self.parent, "rd", v))
    t = property(lambda self: self.ap.tensor)

    def __getitem__(self, k):
        return self.ap[k]


class Eng:
    def __init__(self, name, e, sem):
        self.name = name
        self.e = e
        self.sem = sem
        self.cnt = 0
        self.seen = {}


class K:
    def __init__(self, nc, es):
        self.nc = nc
        self.es = es
        self.sems = {}
        self.engs = {}
        for name, e in (("pe", nc.tensor), ("act", nc.scalar), ("dve", nc.vector),
                        ("pool", nc.gpsimd), ("sp", nc.sync)):
            s = es.enter_context(nc.semaphore("s_" + name))
            self.sems[name] = s
            self.engs[name] = Eng(name, e, s)
        self.dma_sems = []
        self.pools = {}
        for pname, cnt in (("hw", 28), ("sw", 60)):
            lst = []
            for i in range(cnt):
                nm = "%s%d" % (pname, i)
                s = es.enter_context(nc.semaphore(nm))
                self.sems[nm] = s
                ent = [nm, 0]
                lst.append(ent)
                self.dma_sems.append(ent)
            self.pools[pname] = [lst, 0]
        self.n_inst = 0
        self.out_tokens = []
        self.uid = 0

    def sb(self, name, shape, dt=F32, es=None):
        self.uid += 1
        LAST_NAMES.append("%s_%d" % (name, self.uid + 0))
        t = (es or self.es).enter_context(self.nc.sbuf_tensor("%s_%d" % (name, self.uid), list(shape), dt))
        return Buf(t, name)

    def ps(self, name, shape, dt=F32, es=None):
        t = (es or self.es).enter_context(self.nc.psum_tensor(name, list(shape), dt))
        b = Buf(t, name)
        b.psum = True
        return b

    def _wait(self, eng, deps):
        need = {}
        for (k, v) in deps:
            if v > need.get(k, 0):
                need[k] = v
        for k, v in need.items():
            if eng.seen.get(k, 0) >= v:
                continue
            eng.e.wait_ge(self.sems[k], v)
            eng.seen[k] = v
            self.n_inst += 1

    def _deps(self, reads, writes, en=None):
        deps = []
        for b in reads:
            if b.lw is not None:
                deps.append(b.lw)
            if b.psum:
                for k_, v_ in b.rd.items():
                    if k_ != en:
                        deps.append((k_, v_))
        for b in writes:
            if b.lw is not None:
                deps.append(b.lw)
            for k, v in b.rd.items():
                deps.append((k, v))
        return deps

    def _mark(self, tok, reads, writes):
        for b in reads:
            if b.rd.get(tok[0], 0) < tok[1]:
                b.rd[tok[0]] = tok[1]
        for b in writes:
            b.lw = tok
            b.rd = {}

    def op(self, en, fn, reads=(), writes=()):
        eng = self.engs[en]
        self._wait(eng, self._deps(reads, writes, en))
        ins = fn(eng.e)
        eng.cnt += 1
        ins.then_inc(eng.sem, 1)
        tok = (en, eng.cnt)
        self._mark(tok, reads, writes)
        self.n_inst += 1
        return tok

    def mm(self, out_buf, mms, reads):
        eng = self.engs["pe"]
        self._wait(eng, self._deps(reads, [out_buf]))
        n = len(mms)
        for i, m in enumerate(mms):
            ins = eng.e.matmul(m[0], m[1], m[2], start=m[3], stop=m[4])
            self.n_inst += 1
            if i == n - 1:
                eng.cnt += 1
                ins.then_inc(eng.sem, 1)
        tok = ("pe", eng.cnt)
        self._mark(tok, reads, [out_buf])
        return tok

    def tr(self, out_buf, out_ap, in_ap, ident_ap, reads):
        eng = self.engs["pe"]
        self._wait(eng, self._deps(reads, [out_buf]))
        ins = eng.e.transpose(out_ap, in_ap, ident_ap)
        eng.cnt += 1
        ins.then_inc(eng.sem, 1)
        tok = ("pe", eng.cnt)
        self._mark(tok, reads, [out_buf])
        self.n_inst += 1
        return tok

    def dma(self, qn, out_ap, in_ap, reads=(), writes=(), is_output=False):
        eng = self.engs[qn]
        pl = self.pools["sw" if qn == "pool" else "hw"]
        slot = pl[0][pl[1]]
        pl[1] = (pl[1] + 1) % len(pl[0])
        deps = self._deps(reads, writes)
        if slot[1] > 0:
            deps.append((slot[0], slot[1]))
        self._wait(eng, deps)
        ins = eng.e.dma_start(out=out_ap, in_=in_ap)
        slot[1] += 16
        ins.then_inc(self.sems[slot[0]], 16)
        tok = (slot[0], slot[1])
        self._mark(tok, reads, writes)
        self.n_inst += 1
        if is_output:
            self.out_tokens.append(tok)
        return tok

    def all_tokens(self):
        deps = []
        for s in self.dma_sems:
            if s[1] > 0:
                deps.append((s[0], s[1]))
        for n in ("pe", "act", "dve", "pool"):
            if self.engs[n].cnt > 0:
                deps.append((n, self.engs[n].cnt))
        return deps

    def barrier(self):
        deps = self.all_tokens()
        for n in ("pe", "act", "dve", "pool", "sp"):
            self._wait(self.engs[n], deps)

    def finish(self):
        self._wait(self.engs["sp"], self.all_tokens() + list(self.out_tokens))


class Prog:
    def __init__(self, stop=None, debug=False, start=None):
        self.start = start
        self.lite = start is not None and start[0] == "mix" and stop == start
        self.stop = stop
        self.debug = debug
        self.nc = bass.Bass("TRN2", target_bir_lowering=False)
        self.din = {}
        self.dout = {}

    def I(self, name, shape):
        self.din[name] = self.nc.dram_tensor(name, list(shape), F32, kind="ExternalInput").ap()
        return self.din[name]

    def O(self, name, shape):
        self.dout[name] = self.nc.dram_tensor(name, list(shape), F32, kind="ExternalOutput").ap()
        return self.dout[name]

    def psum(self):
        b = self.pbanks[self.prr]
        self.prr = (self.prr + 1) % len(self.pbanks)
        return b

    def build(self):
        nc = self.nc
        xT_d = self.I("xT", [128, 8, NTOK])
        cond_d = self.I("cond", [128, 8, 2])
        w1_d = self.I("w1", [1 if self.lite else 8 * NFC, 128, 2048])
        w2_d = self.I("w2", [1 if self.lite else 8, 128, NFC * 1024])
        wada_d = self.I("wada", [1 if self.lite else 4, 18, 128, 4096])
        bada_d = self.I("bada", [128, 4, 72])
        ng_d = self.I("ng", [128, 4, 3, 8])
        fg_d = self.I("fg", [128, 8])
        yT_d = self.O("yT", [128, 8, NTOK])
        self.declare_mixer_io()

        with ExitStack() as es:
            k = K(nc, es)
            self.k = k
            self.pbanks = [k.ps("pb%d" % i, [128, 512]) for i in range(6)]
            self.psm = k.ps("psm", [128, 512])
            self.pyd = [self.psm, k.ps("psd2", [128, 512])]
            self.prr = 0
            self.x = [k.sb("x%d" % t, [128, 8, 512]) for t in range(TT)]
            self.ones_bf = k.sb("ones_bf", [128, 128], BF16)
            self.scT = k.sb("scT", [128, 8, 2], BF16)
            self.scT32 = k.sb("scT32", [128, 8, 2], F32)
            self.condt = k.sb("condt", [128, 8, 2])
            self.bada = k.sb("bada", [128, 4, 72])
            self.ng = k.sb("ng", [128, 4, 3, 8])
            self.fg = k.sb("fg", [128, 8])
            self.mod = [k.sb("mod%d" % l, [128, 72, 2]) for l in range(4)]
            self.Am = [k.sb("Am%d" % l, [128, 3, 8, 2]) for l in range(4)]
            self.Gm = [k.sb("Gm%d" % l, [128, 3, 8, 2]) for l in range(4)]
            self.alloc_mixer_persistent()

            for t in range(TT):
                k.dma("sp", self.x[t][:], xT_d[:, :, t * 512:(t + 1) * 512], writes=[self.x[t]])
            k.dma("sp", self.condt[:], cond_d, writes=[self.condt])
            k.dma("sp", self.bada[:], bada_d, writes=[self.bada])
            k.dma("sp", self.ng[:], ng_d, writes=[self.ng])
            k.dma("sp", self.fg[:], fg_d, writes=[self.fg])
            k.op("pool", lambda e: e.memset(self.ones_bf[:], 1.0), [], [self.ones_bf])
            k.op("act", lambda e: e.activation(self.scT[:], self.condt[:], AF.Silu), [self.condt], [self.scT])
            k.op("act", lambda e: e.activation(self.scT32[:], self.condt[:], AF.Silu), [self.condt], [self.scT32])
            self.load_mixer_consts()

            stages = []
            for l in range(4):
                stages += [("mod", l), ("ffn", l, 0), ("mix", l), ("ffn", l, 1)]
            done = False
            if self.start is not None:
                i0 = stages.index(self.start)
                stages = [("mod", self.start[1])] + stages[i0:]
            for st in stages:
                if st[0] == "mod":
                    self.modulation(st[1])
                elif st[0] == "ffn":
                    self.ffn(st[1], st[2])
                    if self.stop == ("ffn", st[1], st[2]):
                        done = True
                else:
                    self.mixer(st[1])
                    if self.stop == ("mix", st[1]):
                        done = True
                if done:
                    break
            self.final(yT_d, raw=self.debug)
            k.finish()
            self.n_inst = k.n_inst
        return nc

    def modulation(self, l):
        k = self.k
        wada_d = self.din["wada"]
        k.barrier()
        with ExitStack() as ph:
            wa = [k.sb("wa%d" % i, [128, 4096], F32, es=ph) for i in range(4)]
            psm = self.psm
            for t in range(18):
                w = wa[t % 4]
                k.dma("sp" if t % 2 == 0 else "act", w[:], wada_d[0 if self.lite else l, t], writes=[w])
                for cc in range(4):
                    ci = t * 4 + cc
                    mms = []
                    for kc in range(8):
                        mms.append((psm[:, ci * 2:ci * 2 + 2], w[:, kc * 512 + cc * 128: kc * 512 + cc * 128 + 128],
                                    self.scT32[:, kc, :], kc == 0, kc == 7))
                    k.mm(psm, mms, [w, self.scT32])
            mod = self.mod[l]
            bb = bass.AP(self.bada.t, l * 72, [[4 * 72, 128], [1, 72], [0, 2]])
            k.op("dve", lambda e: e.tensor_tensor(mod[:], psm[:, 0:144].rearrange("p (a b) -> p a b", b=2), bb, ALU.add),
                 [psm, self.bada], [mod])
            Am, Gm = self.Am[l], self.Gm[l]
            for s in range(3):
                sc = mod[:, (3 * s + 1) * 8:(3 * s + 2) * 8, :]
                gt = mod[:, (3 * s + 2) * 8:(3 * s + 3) * 8, :]
                ngb = bass.AP(self.ng.t, l * 24 + s * 8, [[96, 128], [1, 8], [0, 2]])
                k.op("dve", lambda e: e.scalar_tensor_tensor(Am[:, s], sc, 1.0, ngb, ALU.add, ALU.mult), [mod, self.ng], [Am])
                k.op("dve", lambda e: e.tensor_scalar(Gm[:, s], gt, 0.5 if s != 1 else 1.0, None, ALU.mult), [mod], [Gm])
            k.barrier()

    def norm_mod(self, l, s, hT, ph, tiles=None, temps=None):
        k = self.k
        tiles = list(range(TT)) if tiles is None else tiles
        if temps is None:
            sqb = [k.sb("sq%d" % i, [128, 8, 512], BF16, es=ph) for i in range(2)]
            rsb = [k.sb("rs%d" % i, [128, 512], F32, es=ph) for i in range(2)]
            t3b = [k.sb("t32%d" % i, [128, 512], F32, es=ph) for i in range(3)]
            temps = ([(b, b[:]) for b in sqb], [(b, b[:]) for b in rsb], [(b, b[:]) for b in t3b])
        sql, rsl, t3l = temps
        Am, mod = self.Am[l], self.mod[l]
        n = 0
        for ti, t in enumerate(tiles):
            j = 0 if t < 2 else 1
            x = self.x[t]
            qb, q = sql[ti % len(sql)]
            rb, r = rsl[ti % len(rsl)]
            k.op("act", lambda e: e.activation(q, x[:], AF.Square), [x], [qb])
            pb = self.psum()
            k.mm(pb, [(pb[:], self.ones_bf[:], q[:, c, :], c == 0, c == 7) for c in range(8)], [self.ones_bf, qb])
            k.op("act", lambda e: e.activation(r, pb[:], AF.Sqrt, bias=self.eps_t[:, 0:1], scale=1.0 / D), [pb, self.eps_t], [rb])
            k.op("dve", lambda e: e.reciprocal(r, r), [rb], [rb])
            for c in range(8):
                tbb, tb = t3l[n % len(t3l)]
                n += 1
                k.op("dve", lambda e: e.scalar_tensor_tensor(tb, x[:, c, :], Am[:, s, c, j:j + 1], r, ALU.mult, ALU.mult),
                     [x, Am, rb], [tbb])
                k.op("act", lambda e: e.activation(hT[ti][:, c, :], tb, AF.Identity, bias=mod[:, 3 * s * 8 + c, j:j + 1]),
                     [tbb, mod], [hT[ti]])

    def ffn(self, l, i):
        k = self.k
        w1_d, w2_d = self.din["w1"], self.din["w2"]
        s = 0 if i == 0 else 2
        li = l * 2 + i
        k.barrier()
        with ExitStack() as ph:
            hT = [k.sb("hT%d" % t, [128, 8, 512], BF16, es=ph) for t in range(TT)]
            gT = [k.sb("gT%d" % t, [128, 6, 512], BF16, es=ph) for t in range(TT)]
            win = [k.sb("win%d" % t, [128, 2048], BF16, es=ph) for t in range(4)]
            wout = [k.sb("wout%d" % t, [128, 6 * 1024], BF16, es=ph) for t in range(2)]
            sl = [k.sb("sl%d" % t, [128, 512], F32, es=ph) for t in range(3)]
            self.norm_mod(l, s, hT, ph)
            Gm = self.Gm[l]
            nw = 0
            ns = 0
            for g, (kc0, n) in enumerate(FGROUPS):
                wo = wout[g % 2]
                if g > 0:
                    k.dma("pool", wo[:, 0:n * 1024], w2_d[li, :, kc0 * 1024:(kc0 + n) * 1024], writes=[wo])
                for fi in range(n):
                    fc = kc0 + fi
                    w = win[nw % 4]
                    nw += 1
                    k.dma("pool", w[:], w1_d[li * NFC + fc], writes=[w])
                    if g == 0 and fi == 1:
                        k.dma("pool", wo[:, 0:n * 1024], w2_d[li, :, kc0 * 1024:(kc0 + n) * 1024], writes=[wo])
                    for t in range(TT):
                        pa = self.psum()
                        k.mm(pa, [(pa[:], w[:, kc * 128:(kc + 1) * 128], hT[t][:, kc, :], kc == 0, kc == 7) for kc in range(8)], [w, hT[t]])
                        pu = self.psum()
                        k.mm(pu, [(pu[:], w[:, 1024 + kc * 128:1024 + (kc + 1) * 128], hT[t][:, kc, :], kc == 0, kc == 7) for kc in range(8)], [w, hT[t]])
                        st = sl[ns % 3]
                        ns += 1
                        k.op("act", lambda e: e.activation(st[:], pa[:], AF.Silu), [pa], [st])
                        k.op("dve", lambda e: e.tensor_tensor(gT[t][:, fi, :], st[:], pu[:], ALU.mult), [st, pu], [gT[t]])
                for dc in range(8):
                    for t in range(TT):
                        j = 0 if t < 2 else 1
                        py = self.psum()
                        k.mm(py, [(py[:], wo[:, q * 1024 + dc * 128:q * 1024 + dc * 128 + 128], gT[t][:, q, :], q == 0, q == n - 1)
                                  for q in range(n)], [wo, gT[t]])
                        x = self.x[t]
                        k.op("dve", lambda e: e.scalar_tensor_tensor(x[:, dc, :], py[:], Gm[:, s, dc, j:j + 1], x[:, dc, :], ALU.mult, ALU.add),
                             [py, Gm, x], [x])
            k.barrier()

    def final(self, yT_d, raw=False):
        k = self.k
        k.barrier()
        with ExitStack() as ph:
            if raw:
                for t in range(TT):
                    k.dma("sp", yT_d[:, :, t * 512:(t + 1) * 512], self.x[t][:], reads=[self.x[t]], is_output=True)
                return
            sq = [k.sb("fsq%d" % i, [128, 8, 512], BF16, es=ph) for i in range(2)]
            rs = [k.sb("frs%d" % i, [128, 512], F32, es=ph) for i in range(2)]
            yo = [k.sb("fyo%d" % i, [128, 8, 512], F32, es=ph) for i in range(2)]
            for t in range(TT):
                x = self.x[t]
                q, r, y = sq[t % 2], rs[t % 2], yo[t % 2]
                k.op("act", lambda e: e.activation(q[:], x[:], AF.Square), [x], [q])
                pb = self.psum()
                k.mm(pb, [(pb[:], self.ones_bf[:], q[:, c, :], c == 0, c == 7) for c in range(8)], [self.ones_bf, q])
                k.op("act", lambda e: e.activation(r[:], pb[:], AF.Sqrt, bias=self.eps_t[:, 0:1], scale=1.0 / D), [pb, self.eps_t], [r])
                k.op("dve", lambda e: e.reciprocal(r[:], r[:]), [r], [r])
                for c in range(8):
                    k.op("dve", lambda e: e.scalar_tensor_tensor(y[:, c, :], x[:, c, :], self.fg[:, c:c + 1], r[:], ALU.mult, ALU.mult),
                         [x, self.fg, r], [y])
                k.dma("sp", yT_d[:, :, t * 512:(t + 1) * 512], y[:], reads=[y], is_output=True)

    def declare_mixer_io(self):
        I, O = self.I, self.O
        I("ident", [128, 128])
        I("gm_wu", [128, 8192]); I("gm_wv", [128, 8192]); I("gm_wo", [128, 8192])
        I("gm_wsT", [128, 1024]); I("gm_lng", [1, 1024]); I("gm_lnb", [1, 1024]); I("gm_bs", [1, 1024])
        I("nat_wqkv", [8, 128, 3072]); I("nat_wo", [128, 8192])
        I("nat_tbT", [8, 128, 960]); I("nat_maskT", [128, 64])
        I("nat_ckT", [8, 128, 256]); I("nat_cv", [8, 128, 256])
        O("kc_out", [128, 8, 1024]); O("vc_out", [128, 8, 8, 128])
        I("gla_win", [4, 128, 8, 1024]); I("gla_wo", [4, 128, 2, 1024]); I("gla_wa1", [128, 8, 32]); I("gla_wa2", [33, 4, 256])
        I("gla_tri", [128, 4, 128]); I("gla_msk", [128, 2, 128]); I("gla_cos", [128, 1024]); I("gla_sin", [128, 1024])
        I("gla_ng", [128, 8]); I("gla_s0", [128, 2, 4, 256])
        O("gla_out", [128, 4, 2, 4, 256])
        I("ssd_wx", [4, 128, 8, 512]); I("ssd_wz", [4, 128, 8, 512]); I("ssd_wbc", [4, 128, 8, 256]); I("ssd_wdt", [128, 8, 64])
        I("ssd_wo", [128, 16, 1024]); I("ssd_tri", [128, 4, 128]); I("ssd_cw", [128, 24, 3]); I("ssd_cb", [128, 24])
        I("ssd_dtb", [1, 64]); I("ssd_alog", [1, 64]); I("ssd_dsk", [1, 32]); I("ssd_ng", [128, 16]); I("ssd_s0", [128, 2, 32, 64])
        O("ssd_out", [128, 4, 2, 32, 64])

    def alloc_mixer_persistent(self):
        k = self.k
        self.eps_t = k.sb("eps_t", [128, 1])
        k.op("pool", lambda e: e.memset(self.eps_t[:], EPS), [], [self.eps_t])
        self.ident_bf = k.sb("ident_bf", [128, 128], BF16)
        self.ident_f = k.sb("ident_f", [128, 128], F32)

    def load_mixer_consts(self):
        k = self.k
        k.dma("pool", self.ident_bf[:], self.din["ident"], writes=[self.ident_bf])
        k.dma("sp", self.ident_f[:], self.din["ident"], writes=[self.ident_f])

    def mixer(self, l):
        k = self.k
        k.barrier()
        with ExitStack() as ph:
            if l == 3:
                self.mix_ssd(l, None, ph)
            else:
                hT = [k.sb("mhT%d" % t, [128, 8, 512], BF16, es=ph) for t in range(TT)]
                with ExitStack() as sub:
                    self.norm_mod(l, 1, hT, sub)
                    k.barrier()
                [self.mix_gla, self.mix_nat, self.mix_gmlp][l](l, hT, ph)
            k.barrier()

    def out_proj(self, l, Wo, srcT, only=None):
        k = self.k
        Gm = self.Gm[l]
        for dc in range(8):
            for t in (range(TT) if only is None else [only]):
                j = 0 if t < 2 else 1
                py = self.psum()
                k.mm(py, [(py[:], Wo[:, q * 1024 + dc * 128:q * 1024 + dc * 128 + 128], srcT[t][:, q, :], q == 0, q == 7)
                          for q in range(8)], [Wo, srcT[t]])
                x = self.x[t]
                k.op("dve", lambda e: e.scalar_tensor_tensor(x[:, dc, :], py[:], Gm[:, 1, dc, j:j + 1], x[:, dc, :], ALU.mult, ALU.add),
                     [py, Gm, x], [x])

    def gelu_tanh(self, out_ap, out_buf, p, tmps, n):
        k = self.k
        t1, t3 = tmps
        k.op("act", lambda e: e.activation(t1[:, 0:n], p[:, 0:n], AF.Square, scale=0.044715 ** 0.5), [p], [t1])
        k.op("dve", lambda e: e.scalar_tensor_tensor(t3[:, 0:n], t1[:, 0:n], 1.0, p[:, 0:n], ALU.add, ALU.mult), [t1, p], [t3])
        k.op("act", lambda e: e.activation(t3[:, 0:n], t3[:, 0:n], AF.Sigmoid, scale=1.5957691216), [t3], [t3])
        k.op("dve", lambda e: e.tensor_tensor(out_ap, t3[:, 0:n], p[:, 0:n], ALU.mult), [t3, p], [out_buf])

    def mix_gmlp(self, l, hT, ph):
        k = self.k
        d = self.din
        Wu = k.sb("gWu", [128, 8192], BF16, es=ph)
        Wv = k.sb("gWv", [128, 8192], BF16, es=ph)
        Wo = k.sb("gWo", [128, 8192], BF16, es=ph)
        wsT = k.sb("gwsT", [128, 1024], BF16, es=ph)
        lng = k.sb("glng", [128, 1024], F32, es=ph)
        lnb = k.sb("glnb", [128, 1024], F32, es=ph)
        bsb = k.sb("gbsb", [128, 1024], F32, es=ph)
        for (w, nm) in ((Wu, "gm_wu"), (Wv, "gm_wv"), (Wo, "gm_wo")):
            for h in range(2):
                k.dma("pool", w[:, h * 4096:(h + 1) * 4096], d[nm][:, h * 4096:(h + 1) * 4096], writes=[w])
        k.dma("pool", wsT[:], d["gm_wsT"], writes=[wsT])
        k.dma("sp", lng[:], d["gm_lng"].partition_broadcast(128), writes=[lng])
        k.dma("sp", lnb[:], d["gm_lnb"].partition_broadcast(128), writes=[lnb])
        k.dma("sp", bsb[:], d["gm_bs"].partition_broadcast(128), writes=[bsb])
        uT = k.sb("guT", [128, 8, 512], BF16, es=ph)
        gm2 = [k.sb("ggmT%d" % t, [128, 8, 512], BF16, es=ph) for t in range(1)]
        gmT = [gm2[0], gm2[0], gm2[0], gm2[0]]
        vt = k.sb("gvt", [128, 1024], F32, es=ph)
        vsq = k.sb("gvsq", [128, 1024], F32, es=ph)
        vnb = k.sb("gvnb", [128, 4, 1024], BF16, es=ph)
        st = k.sb("gst", [128, 8], F32, es=ph)
        tmpa = [(k.sb("gt1%d" % i, [128, 512], F32, es=ph), k.sb("gt3%d" % i, [128, 512], F32, es=ph)) for i in range(2)]
        zz = [k.sb("gzz%d" % i, [128, 512], F32, es=ph) for i in range(1)] * 2
        ng = 0
        for t in range(TT):
            for fc in range(8):
                pu = self.psum()
                k.mm(pu, [(pu[:], Wu[:, kc * 1024 + fc * 128:kc * 1024 + fc * 128 + 128], hT[t][:, kc, :], kc == 0, kc == 7) for kc in range(8)], [Wu, hT[t]])
                self.gelu_tanh(uT[:, fc, :], uT, pu, tmpa[ng % 2], 512)
                ng += 1
            for q4 in range(4):
                for half in range(2):
                    pv = self.psum()
                    k.mm(pv, [(pv[:], hT[t][:, kc, q4 * 128:(q4 + 1) * 128], Wv[:, kc * 1024 + half * 512:kc * 1024 + half * 512 + 512], kc == 0, kc == 7)
                              for kc in range(8)], [Wv, hT[t]])
                    self.gelu_tanh(vt[:, half * 512:(half + 1) * 512], vt, pv, tmpa[ng % 2], 512)
                    ng += 1
                k.op("dve", lambda e: e.tensor_reduce(st[:, 0:1], vt[:], AX.X, ALU.add), [vt], [st])
                k.op("pool", lambda e: e.tensor_tensor(vsq[:], vt[:], vt[:], ALU.mult), [vt], [vsq])
                k.op("dve", lambda e: e.tensor_reduce(st[:, 1:2], vsq[:], AX.X, ALU.add), [vsq], [st])
                k.op("dve", lambda e: e.tensor_scalar(st[:, 2:3], st[:, 0:1], 1.0 / 1024, None, ALU.mult), [st], [st])
                k.op("dve", lambda e: e.tensor_tensor(st[:, 3:4], st[:, 2:3], st[:, 2:3], ALU.mult), [st], [st])
                k.op("dve", lambda e: e.scalar_tensor_tensor(st[:, 4:5], st[:, 1:2], 1.0 / 1024, st[:, 3:4], ALU.mult, ALU.subtract), [st], [st])
                k.op("act", lambda e: e.activation(st[:, 5:6], st[:, 4:5], AF.Sqrt, bias=self.eps_t[:, 0:1]), [st, self.eps_t], [st])
                k.op("dve", lambda e: e.reciprocal(st[:, 6:7], st[:, 5:6]), [st], [st])
                k.op("dve", lambda e: e.tensor_scalar(vsq[:], vt[:], st[:, 2:3], st[:, 6:7], ALU.subtract, ALU.mult), [vt, st], [vsq])
                k.op("pool", lambda e: e.tensor_tensor(vsq[:], vsq[:], lng[:], ALU.mult), [vsq, lng], [vsq])
                k.op("pool", lambda e: e.tensor_tensor(vnb[:, q4, :], vsq[:], lnb[:], ALU.add), [vsq, lnb], [vnb])
            for g in range(8):
                pz = self.psum()
                for q4 in range(4):
                    k.mm(pz, [(pz[:, q4 * 128:(q4 + 1) * 128], vnb[:, q4, g * 128:(g + 1) * 128], wsT[:, g * 128:(g + 1) * 128], True, True)], [vnb, wsT])
                z = zz[g % 2]
                bb = bass.AP(bsb.t, g * 128, [[1024, 128], [0, 4], [1, 128]])
                k.op("dve", lambda e: e.tensor_tensor(z[:].rearrange("p (a b) -> p a b", a=4), pz[:].rearrange("p (a b) -> p a b", a=4), bb, ALU.add), [pz, bsb], [z])
                k.op("dve", lambda e: e.tensor_tensor(gmT[t][:, g, :], z[:], uT[:, g, :], ALU.mult), [z, uT], [gmT[t]])
            self.out_proj(l, Wo, gmT, only=t)

    def mix_nat(self, l, hT, ph):
        k = self.k
        d = self.din
        Wo = k.sb("nWo", [128, 8192], BF16, es=ph)
        for h in range(2):
            k.dma("pool", Wo[:, h * 4096:(h + 1) * 4096], d["nat_wo"][:, h * 4096:(h + 1) * 4096], writes=[Wo])
        oT = [k.sb("noT%d" % t, [128, 8, 512], BF16, es=ph) for t in range(TT)]
        Wp = [k.sb("nWp%d" % i, [128, 3072], BF16, es=ph) for i in range(2)]
        maskT = k.sb("nmask", [128, 64], F32, es=ph)
        if not (NATV & 64):
            k.dma("sp", maskT[:], d["nat_maskT"], writes=[maskT])
        tbr = [k.sb("ntbr%d" % i, [128, 960], F32, es=ph) for i in range(1)] * 2
        tbT = [k.sb("ntbT%d" % i, [128, 15, 64], BF16, es=ph) for i in range(2)]
        v2 = k.sb("nv2", [128, 7, 128], BF16, es=ph)
        ckT = [k.sb("nckT%d" % i, [128, 256], BF16, es=ph) for i in range(2)]
        cv = [k.sb("ncv%d" % i, [128, 256], BF16, es=ph) for i in range(2)]
        qT = [k.sb("nqT%d" % i, [128, 1024], BF16, es=ph) for i in range(2)]
        kT = [k.sb("nkT%d" % i, [128, 1024], BF16, es=ph) for i in range(2)]
        vtk = [k.sb("nvt%d" % i, [128, 8, 128], BF16, es=ph) for i in range(2)]
        k32 = [k.sb("nk32%d" % i, [128, 256], F32, es=ph) for i in range(2)]
        v32 = [k.sb("nv32%d" % i, [128, 2, 128], F32, es=ph) for i in range(2)]
        PT = [k.sb("nPT%d" % i, [128, 512], BF16, es=ph) for i in range(4)]
        rdl = [k.sb("nrdl%d" % i, [128, 64], F32, es=ph) for i in range(4)]
        rd = [k.sb("nrd%d" % i, [128, 256], F32, es=ph) for i in range(2)]
        kc_out, vc_out = self.dout["kc_out"], self.dout["vc_out"]
        nseq = 0
        npt = 0
        for pr in range(8):
            W = Wp[pr % 2]
            k.dma("pool", W[:], d["nat_wqkv"][pr], writes=[W])
            tr_, tb = tbr[pr % 2], tbT[pr % 2]
            if not (NATV & 64):
                k.dma("sp", tr_[:], d["nat_tbT"][pr], writes=[tr_])
            mb = bass.AP(maskT.t, 0, [[64, 128], [0, 15], [1, 64]])
            if not (NATV & 16):
                k.op("dve", lambda e: e.tensor_tensor(tb[:], tr_[:].rearrange("p (b c) -> p b c", b=15), mb, ALU.add), [tr_, maskT], [tb])
            ck, cvv = ckT[pr % 2], cv[pr % 2]
            if not (NATV & 128):
                k.dma("pool", ck[:], d["nat_ckT"][pr], writes=[ck])
                k.dma("pool", cvv[:], d["nat_cv"][pr], writes=[cvv])
            for sq in range(5):
                if NATV & 32:
                    continue
                T = 256 if sq < 4 else 1024
                t0 = sq * 256
                q_, k_, v_ = qT[nseq % 2], kT[nseq % 2], vtk[nseq % 2]
                kk32, vv32 = k32[nseq % 2], v32[nseq % 2]
                nseq += 1
                for c0 in range(0, T, 512):
                    n = min(512, T - c0)
                    tt, off = (t0 + c0) // 512, (t0 + c0) % 512
                    pq = self.psum()
                    if not (NATV & 256):
                        k.mm(pq, [(pq[:, 0:n], W[:, kc * 128:(kc + 1) * 128], hT[tt][:, kc, off:off + n], kc == 0, kc == 7) for kc in range(8)], [W, hT[tt]])
                        k.op("act", lambda e: e.mul(q_[:, c0:c0 + n], pq[:, 0:n], 0.125), [pq], [q_])
                    if NATV & 512:
                        continue
                    pk = self.psum()
                    k.mm(pk, [(pk[:, 0:n], W[:, 1024 + kc * 128:1024 + (kc + 1) * 128], hT[tt][:, kc, off:off + n], kc == 0, kc == 7) for kc in range(8)], [W, hT[tt]])
                    if not (NATV & 2048):
                        k.op("dve", lambda e: e.tensor_copy(k_[:, c0:c0 + n], pk[:, 0:n]), [pk], [k_])
                    if sq < 4 and not (NATV & 4096):
                        k.op("act", lambda e: e.copy(kk32[:, 0:n], pk[:, 0:n]), [pk], [kk32])
                        if not (NATV & 8):
                            k.dma("sp", kc_out[:, pr, t0:t0 + 256], kk32[:], reads=[kk32], is_output=True)
                for ch in range(T // 128):
                    if NATV & 1024:
                        continue
                    tt, off = (t0 + ch * 128) // 512, (t0 + ch * 128) % 512
                    pv = self.psum()
                    k.mm(pv, [(pv[:, 0:128], hT[tt][:, kc, off:off + 128], W[:, 2048 + kc * 128:2048 + (kc + 1) * 128], kc == 0, kc == 7) for kc in range(8)], [W, hT[tt]])
                    k.op("dve", lambda e: e.tensor_copy(v_[:, ch, :], pv[:, 0:128]), [pv], [v_])
                    if sq < 4:
                        k.op("act", lambda e: e.copy(vv32[:, ch, :], pv[:, 0:128]), [pv], [vv32])
                if sq < 4:
                    if not (NATV & 8):
                        k.dma("sp", vc_out[:, pr, sq * 2:sq * 2 + 2, :], vv32[:], reads=[vv32], is_output=True)
                elif not (NATV & 1):
                    k.dma("sp", v2[0:64, :, :], v_[64:128, 0:7, :], reads=[v_], writes=[v2])
                    k.dma("sp", v2[64:128, :, :], v_[0:64, 1:8, :], reads=[v_], writes=[v2])
                def drive(gens):
                    while gens:
                        for gn in list(gens):
                            try:
                                next(gn)
                            except StopIteration:
                                gens.remove(gn)

                def unit_ctx(hh, ci_):
                    hb = hh * 64
                    pS = self.psum()
                    for ch in range(2):
                        k.mm(pS, [(pS[:, ch * 256:(ch + 1) * 256], k_[hb:hb + 64, ch * 128:(ch + 1) * 128], q_[hb:hb + 64, 0:256], True, True)], [k_, q_])
                    P = PT[ci_]
                    yield
                    k.op("act", lambda e: e.activation(P[:], pS[:], AF.Exp), [pS], [P])
                    yield
                    po = self.psum()
                    k.mm(po, [(po[:, 0:256], v_[:, ch, :], P[:, ch * 256:(ch + 1) * 256], ch == 0, ch == 1) for ch in range(2)], [v_, P])
                    k.mm(po, [(po[:, 256:512], self.ones_bf[:], P[:, ch * 256:(ch + 1) * 256], ch == 0, ch == 1) for ch in range(2)], [self.ones_bf, P])
                    r_ = rd[ci_ % 2]
                    k.op("dve", lambda e: e.reciprocal(r_[hb:hb + 64, 0:256], po[hb:hb + 64, 256:512]), [po], [r_])
                    tt, off = t0 // 512, t0 % 512
                    k.op("dve", lambda e: e.tensor_tensor(oT[tt][hb:hb + 64, pr, off:off + 256], po[hb:hb + 64, 0:256], r_[hb:hb + 64, 0:256], ALU.mult), [po, r_], [oT[tt]])

                def unit_lat(hh, r, ci_):
                    hb = hh * 64
                    row0 = min(max(r - 4, 0), 8)
                    nch = 6
                    pS = self.psum()
                    qs = q_[hb:hb + 64, r * 64:(r + 1) * 64]
                    for ci in range(4):
                        sl = row0 + 2 * ci - r + 7
                        lb = tb[hb:hb + 64, sl:sl + 2, :].rearrange("p a b -> p (a b)")
                        kc0 = 64 * row0 + 128 * ci
                        k.mm(pS, [(pS[:, ci * 64:(ci + 1) * 64], k_[hb:hb + 64, kc0:kc0 + 128], qs, True, False),
                                  (pS[:, ci * 64:(ci + 1) * 64], lb, self.ident_bf[hb:hb + 64, hb:hb + 64], False, True)], [k_, q_, tb, self.ident_bf])
                    for cj in range(2):
                        ci = 4 + cj
                        k.mm(pS, [(pS[:, ci * 64:(ci + 1) * 64], ck[hb:hb + 64, cj * 128:(cj + 1) * 128], qs, True, True)], [ck, q_])
                    P = PT[ci_]
                    yield
                    k.op("act", lambda e: e.activation(P[:, 0:nch * 64], pS[:, 0:nch * 64], AF.Exp), [pS], [P])
                    yield
                    po = self.psum()
                    if row0 % 2 == 0:
                        vs = [v_[:, row0 // 2 + ci, :] for ci in range(4)]
                    else:
                        vs = [v2[:, (row0 - 1) // 2 + ci, :] for ci in range(4)]
                    vs += [cvv[:, cj * 128:(cj + 1) * 128] for cj in range(2)]
                    k.mm(po, [(po[:, 0:64], vs[ci], P[:, ci * 64:(ci + 1) * 64], ci == 0, ci == nch - 1) for ci in range(nch)], [v_, v2, cvv, P])
                    k.mm(po, [(po[:, 64:128], self.ones_bf[:], P[:, ci * 64:(ci + 1) * 64], ci == 0, ci == nch - 1) for ci in range(nch)], [self.ones_bf, P])
                    r_ = rdl[ci_]
                    k.op("dve", lambda e: e.reciprocal(r_[hb:hb + 64, 0:64], po[hb:hb + 64, 64:128]), [po], [r_])
                    tt, off = (t0 + r * 64) // 512, (t0 + r * 64) % 512
                    k.op("dve", lambda e: e.tensor_tensor(oT[tt][hb:hb + 64, pr, off:off + 64], po[hb:hb + 64, 0:64], r_[hb:hb + 64, 0:64], ALU.mult), [po, r_], [oT[tt]])

                if sq < 4:
                    drive([unit_ctx(0, 0), unit_ctx(1, 1)])
                else:
                    for r2 in range(0, 16, 2):
                        drive([unit_lat(0, r2, 0), unit_lat(1, r2, 1), unit_lat(0, r2 + 1, 2), unit_lat(1, r2 + 1, 3)])
        self.out_proj(l, Wo, oT)

    def mix_gla(self, l, hT, ph):
        k = self.k
        d = self.din
        SC = 128 ** -0.5
        W = k.sb("aW", [128, 8, 1024], BF16, es=ph)
        Wo = k.sb("aWo", [128, 2, 1024], BF16, es=ph)
        wa1 = k.sb("awa1", [128, 8, 32], BF16, es=ph)
        wa2 = k.sb("awa2", [33, 4, 256], BF16, es=ph)
        z1a = k.sb("az1a", [33, NTOK], BF16, es=ph)
        tri = k.sb("atri", [128, 4, 128], F32, es=ph)
        msk = k.sb("amsk", [128, 2, 128], F32, es=ph)
        cos = k.sb("acos", [128, 1024], F32, es=ph)
        sin = k.sb("asin", [128, 1024], F32, es=ph)
        ng = k.sb("ang", [128, 8], F32, es=ph)
        q32 = k.sb("aq32", [128, 1024], F32, es=ph)
        k32 = k.sb("ak32", [128, 1024], F32, es=ph)
        kt32 = k.sb("akt32", [128, 8, 128], F32, es=ph)
        vtk = k.sb("avtk", [128, 8, 256], BF16, es=ph)
        srT = k.sb("asrT", [128, 2, 1024], BF16, es=ph)
        Ltok = k.sb("aLtok", [128, 8, 256], F32, es=ph)
        oF = k.sb("aoF", [128, 2, 1024], F32, es=ph)
        GT = k.sb("aGT", [128, 2, 1024], BF16, es=ph)
        sqn = k.sb("asqn", [128, 2, 512], BF16, es=ph)
        S32s = [k.sb("aS32%d" % i, [128, 256], F32, es=ph) for i in range(2)]
        Sbfs = [k.sb("aSbf%d" % i, [128, 256], BF16, es=ph) for i in range(2)]
        Ebt = [k.sb("aEb%d" % i, [128, 128], F32, es=ph) for i in range(2)]
        Ent = [k.sb("aEn%d" % i, [128, 128], F32, es=ph) for i in range(2)]
        EDt = [k.sb("aED%d" % i, [128, 128], F32, es=ph) for i in range(2)]
        qd = [k.sb("aqd%d" % i, [128, 128], BF16, es=ph) for i in range(2)]
        kd = [k.sb("akd%d" % i, [128, 128], BF16, es=ph) for i in range(2)]
        Am = [k.sb("aAm%d" % i, [128, 128], BF16, es=ph) for i in range(2)]
        kl = [k.sb("akl%d" % i, [128, 128], BF16, es=ph) for i in range(2)]
        t1 = k.sb("at1", [128, 512], F32, es=ph)
        t2 = k.sb("at2", [128, 512], F32, es=ph)
        rsd = k.sb("arsd", [128, 512], F32, es=ph)
        k.dma("pool", wa1[:], d["gla_wa1"], writes=[wa1])
        k.dma("pool", wa2[:], d["gla_wa2"], writes=[wa2])
        k.dma("sp", tri[:], d["gla_tri"], writes=[tri])
        k.dma("sp", msk[:], d["gla_msk"], writes=[msk])
        k.dma("sp", cos[:], d["gla_cos"], writes=[cos])
        k.dma("sp", sin[:], d["gla_sin"], writes=[sin])
        k.dma("sp", ng[:], d["gla_ng"], writes=[ng])
        gla_out = self.dout["gla_out"]
        k.op("pool", lambda e: e.memset(z1a[:], 1.0), [], [z1a])
        for t in range(TT):
            pz = self.psum()
            k.mm(pz, [(pz[0:32, :], wa1[:, kc, :], hT[t][:, kc, :], kc == 0, kc == 7) for kc in range(8)], [wa1, hT[t]])
            k.op("act", lambda e: e.copy(z1a[0:32, t * 512:(t + 1) * 512], pz[0:32, :]), [pz], [z1a])
        Gm = self.Gm[l]
        n2 = 0
        for h in range(4):
            for hf in range(2):
                k.dma("pool", W[:, hf * 4:(hf + 1) * 4, :], d["gla_win"][h, :, hf * 4:(hf + 1) * 4, :], writes=[W])
            k.dma("pool", Wo[:], d["gla_wo"][h], writes=[Wo])
            for sq in range(5):
                T = 256 if sq < 4 else 1024
                t0 = sq * 256
                nch = T // 128
                rope = sq == 4
                for c0 in range(0, T, 512):
                    n = min(512, T - c0)
                    tt, off = (t0 + c0) // 512, (t0 + c0) % 512
                    for (dst, cb) in ((q32, 0), (k32, 128)):
                        pq = self.psum()
                        k.mm(pq, [(pq[:, 0:n], W[:, kc, cb:cb + 128], hT[tt][:, kc, off:off + n], kc == 0, kc == 7) for kc in range(8)], [W, hT[tt]])
                        if rope:
                            ps2 = self.psum()
                            k.mm(ps2, [(ps2[:, 0:n], W[:, kc, 256 + cb:256 + cb + 128], hT[tt][:, kc, off:off + n], kc == 0, kc == 7) for kc in range(8)], [W, hT[tt]])
                            k.op("dve", lambda e: e.tensor_tensor(t1[:, 0:n], pq[:, 0:n], cos[:, c0:c0 + n], ALU.mult), [pq, cos], [t1])
                            k.op("dve", lambda e: e.tensor_tensor(t2[:, 0:n], ps2[:, 0:n], sin[:, c0:c0 + n], ALU.mult), [ps2, sin], [t2])
                            k.op("pool", lambda e: e.tensor_tensor(dst[:, c0:c0 + n], t1[:, 0:n], t2[:, 0:n], ALU.add), [t1, t2], [dst])
                        else:
                            k.op("act", lambda e: e.copy(dst[:, c0:c0 + n], pq[:, 0:n]), [pq], [dst])
                    for dvc in range(2):
                        pr = self.psum()
                        k.mm(pr, [(pr[:, 0:n], W[:, kc, 768 + dvc * 128:768 + (dvc + 1) * 128], hT[tt][:, kc, off:off + n], kc == 0, kc == 7) for kc in range(8)], [W, hT[tt]])
                        k.op("act", lambda e: e.activation(srT[:, dvc, c0:c0 + n], pr[:, 0:n], AF.Silu), [pr], [srT])
                for ch in range(nch):
                    tt, off = (t0 + ch * 128) // 512, (t0 + ch * 128) % 512
                    pv = self.psum()
                    k.mm(pv, [(pv[:, 0:256], hT[tt][:, kc, off:off + 128], W[:, kc, 512:768], kc == 0, kc == 7) for kc in range(8)], [W, hT[tt]])
                    k.op("act", lambda e: e.copy(vtk[:, ch, :], pv[:, 0:256]), [pv], [vtk])
                    pl = self.psum()
                    k.mm(pl, [(pl[:, 0:256], z1a[0:33, t0 + ch * 128:t0 + (ch + 1) * 128], wa2[0:33, h, :], True, True)], [z1a, wa2])
                    k.op("act", lambda e: e.activation(Ltok[:, ch, :], pl[:, 0:256], AF.Exp, scale=-1.0), [pl], [Ltok])
                    k.op("act", lambda e: e.activation(Ltok[:, ch, :], Ltok[:, ch, :], AF.Ln, bias=1.0), [Ltok], [Ltok])
                    pt = self.psum()
                    k.tr(pt, pt[:, 0:128], k32[:, ch * 128:(ch + 1) * 128], self.ident_f[:], [k32, self.ident_f])
                    k.op("dve", lambda e: e.tensor_copy(kt32[:, ch, :], pt[:, 0:128]), [pt], [kt32])
                for e_ in range(2):
                    if rope:
                        k.dma("sp", S32s[e_][:], d["gla_s0"][:, e_, h, :], writes=[S32s[e_]])
                    else:
                        k.op("pool", lambda e: e.memset(S32s[e_][:], 0.0), [], [S32s[e_]])
                    k.op("act", lambda e: e.copy(Sbfs[e_][:], S32s[e_][:]), [S32s[e_]], [Sbfs[e_]])
                def unit(e_, ch, first):
                    S32, Sbf = S32s[e_], Sbfs[e_]
                    gcol = 127 if e_ == 0 else 0
                    i2 = e_
                    cs = slice(ch * 128, (ch + 1) * 128)
                    Lc = Ltok[:, ch, e_ * 128:(e_ + 1) * 128]
                    pb = self.psum()
                    k.mm(pb, [(pb[:, 0:128], Lc, tri[:, 2 * e_, :], True, True)], [Ltok, tri])
                    Eb, En, ED = Ebt[i2], Ent[i2], EDt[i2]
                    pD = self.psum()
                    k.mm(pD, [(pD[:, 0:128], tri[:, 2 * e_ + 1, :], Lc, True, True)], [Ltok, tri])
                    yield
                    k.op("act", lambda e: e.activation(Eb[:], pb[:, 0:128], AF.Exp), [pb], [Eb])
                    k.op("act", lambda e: e.activation(En[:], pb[:, 0:128], AF.Exp, scale=-1.0), [pb], [En])
                    k.op("dve", lambda e: e.scalar_tensor_tensor(qd[i2][:], q32[:, cs], SC, Eb[:], ALU.mult, ALU.mult), [q32, Eb], [qd[i2]])
                    k.op("dve", lambda e: e.tensor_tensor(kd[i2][:], k32[:, cs], En[:], ALU.mult), [k32, En], [kd[i2]])
                    yield
                    pA = self.psum()
                    k.mm(pA, [(pA[:, 0:128], kd[i2][:], qd[i2][:], True, True)], [kd[i2], qd[i2]])
                    k.op("dve", lambda e: e.tensor_tensor(Am[i2][:], pA[:, 0:128], msk[:, e_, :], ALU.mult), [pA, msk], [Am[i2]])
                    k.op("act", lambda e: e.activation(ED[:], pD[:, 0:128], AF.Exp), [pD], [ED])
                    k.op("dve", lambda e: e.tensor_tensor(kl[i2][:], kt32[:, ch, :], ED[:], ALU.mult), [kt32, ED], [kl[i2]])
                    yield
                    po = self.psum()
                    for dvc in range(2):
                        k.mm(po, [(po[:, dvc * 128:(dvc + 1) * 128], vtk[:, ch, dvc * 128:(dvc + 1) * 128], Am[i2][:], True, False),
                                  (po[:, dvc * 128:(dvc + 1) * 128], Sbf[:, dvc * 128:(dvc + 1) * 128], qd[i2][:], False, True)], [vtk, Am[i2], Sbf, qd[i2]])
                    if first:
                        k.op("act", lambda e: e.copy(oF[:, :, cs], po[:, 0:256].rearrange("p (a b) -> p a b", a=2)), [po], [oF])
                    else:
                        k.op("dve", lambda e: e.tensor_tensor(oF[:, :, cs], oF[:, :, cs], po[:, 0:256].rearrange("p (a b) -> p a b", a=2), ALU.add), [po, oF], [oF])
                    yield
                    pU = self.psum()
                    k.mm(pU, [(pU[:, 0:256], kl[i2][:], vtk[:, ch, :], True, True)], [kl[i2], vtk])
                    k.op("dve", lambda e: e.scalar_tensor_tensor(S32[:], S32[:], Eb[:, gcol:gcol + 1], pU[:, 0:256], ALU.mult, ALU.add), [S32, Eb, pU], [S32])
                    k.op("act", lambda e: e.copy(Sbf[:], S32[:]), [S32], [Sbf])

                written = set()
                for step in range(nch):
                    gens = []
                    for e_ in range(2):
                        ch = step if e_ == 0 else nch - 1 - step
                        gens.append(unit(e_, ch, ch not in written))
                        written.add(ch)
                    while gens:
                        for gn in list(gens):
                            try:
                                next(gn)
                            except StopIteration:
                                gens.remove(gn)
                if not rope:
                    for e_ in range(2):
                        k.dma("sp", gla_out[:, sq, e_, h, :], S32s[e_][:], reads=[S32s[e_]], is_output=True)
                for c0 in range(0, T, 512):
                    n = min(512, T - c0)
                    tt, off = (t0 + c0) // 512, (t0 + c0) % 512
                    k.op("act", lambda e: e.activation(sqn[:, :, 0:n], oF[:, :, c0:c0 + n], AF.Square), [oF], [sqn])
                    pn = self.psum()
                    k.mm(pn, [(pn[:, 0:n], self.ones_bf[:], sqn[:, dvc, 0:n], dvc == 0, dvc == 1) for dvc in range(2)], [self.ones_bf, sqn])
                    k.op("act", lambda e: e.activation(rsd[:, 0:n], pn[:, 0:n], AF.Sqrt, bias=self.eps_t[:, 0:1], scale=1.0 / 256), [pn, self.eps_t], [rsd])
                    k.op("dve", lambda e: e.reciprocal(rsd[:, 0:n], rsd[:, 0:n]), [rsd], [rsd])
                    for dvc in range(2):
                        k.op("dve", lambda e: e.scalar_tensor_tensor(t1[:, 0:n], oF[:, dvc, c0:c0 + n], ng[:, h * 2 + dvc:h * 2 + dvc + 1], rsd[:, 0:n], ALU.mult, ALU.mult), [oF, ng, rsd], [t1])
                        k.op("pool", lambda e: e.tensor_tensor(GT[:, dvc, c0:c0 + n], t1[:, 0:n], srT[:, dvc, c0:c0 + n], ALU.mult), [t1, srT], [GT])
                    j = 0 if tt < 2 else 1
                    for dc in range(8):
                        py = self.psum()
                        k.mm(py, [(py[:, 0:n], Wo[:, dvc, dc * 128:(dc + 1) * 128], GT[:, dvc, c0:c0 + n], dvc == 0, dvc == 1) for dvc in range(2)], [Wo, GT])
                        x = self.x[tt]
                        k.op("dve", lambda e: e.scalar_tensor_tensor(x[:, dc, off:off + n], py[:, 0:n], Gm[:, 1, dc, j:j + 1], x[:, dc, off:off + n], ALU.mult, ALU.add),
                             [py, Gm, x], [x])

    def mix_ssd(self, l, hT_unused, ph):
        k = self.k
        d = self.din
        hTb = [k.sb("shT%d" % i, [128, 8, 512], BF16, es=ph) for i in range(2)]
        yT = k.sb("syT", [128, 16, 1024], BF16, es=ph)
        Wp = k.sb("sWp", [128, 8, 512], BF16, es=ph)
        Wo = View(Wp, Wp[:, 0:4, :].rearrange("p a (b c) -> p (a b) c", c=128))
        Wdt = k.sb("sWdt", [128, 8, 64], BF16, es=ph)
        dt_t = k.sb("sdt", [128, 8, 64], F32, es=ph)
        dtA = None
        ecum = k.sb("secum", [128, 8, 64], F32, es=ph)
        edec = k.sb("sedec", [128, 8, 64], F32, es=ph)
        gdec = k.sb("sgdec", [128, 8, 64], F32, es=ph)
        raw = k.sb("sraw", [128, 1024], F32, es=ph)
        dtA = View(raw, raw[:, 0:512].rearrange("p (a b) -> p a b", a=8))
        cv = k.sb("scv", [128, 1024], F32, es=ph)
        xtok = k.sb("sxtok", [128, 8, 512], BF16, es=ph)
        dtx = [k.sb("sdtx%d" % i, [128, 512], BF16, es=ph) for i in range(2)]
        dtxd = [k.sb("sdtxd%d" % i, [128, 512], BF16, es=ph) for i in range(2)]
        BcT = k.sb("sBcT", [128, 1024], BF16, es=ph)
        CcT = k.sb("sCcT", [128, 1024], BF16, es=ph)
        Btok = k.sb("sBtok", [128, 8, 128], BF16, es=ph)
        yacc = k.sb("syacc", [128, 8, 512], F32, es=ph)
        STs = [k.sb("sST32%d" % i, [128, 512], F32, es=ph) for i in range(2)]
        STb = [k.sb("sSTbf%d" % i, [128, 512], BF16, es=ph) for i in range(2)]
        CBm = [k.sb("sCBm%d" % i, [128, 128], F32, es=ph) for i in range(2)]
        lhs = [(k.sb("slhh%d" % i, [128, 8, 128], BF16, es=ph), k.sb("slhl%d" % i, [128, 8, 128], BF16, es=ph)) for i in range(2)]
        dtAh = k.sb("sdtAh", [128, 8, 64], BF16, es=ph)
        dtAl = k.sb("sdtAl", [128, 8, 64], BF16, es=ph)
        trib = k.sb("strib", [128, 4, 128], BF16, es=ph)
        Dec = [k.sb("sDec%d" % i, [128, 512], F32, es=ph) for i in range(2)]
        Mt = [k.sb("sMt%d" % i, [128, 512], BF16, es=ph) for i in range(2)]
        tA = [k.sb("stA%d" % i, [128, 512], F32, es=ph) for i in range(1)] + [View(cv, cv[:, 512:1024]), View(cv, cv[:, 0:512])]
        rbc = raw
        tri = k.sb("stri", [128, 4, 128], F32, es=ph)
        ones_f = k.sb("sones", [128, 128], F32, es=ph)
        cw = k.sb("scw", [128, 24, 3], F32, es=ph)
        cb = k.sb("scb", [128, 24], F32, es=ph)
        dtb = k.sb("sdtb", [128, 64], F32, es=ph)
        abc = k.sb("sabc", [128, 64], F32, es=ph)
        dsk = k.sb("sdsk", [128, 32], F32, es=ph)
        ng = k.sb("sng", [128, 16], F32, es=ph)
        ssqp = k.sb("sssqp", [128, 8, 4], F32, es=ph)
        ssq = k.sb("sssq", [128, 8], F32, es=ph)
        k.dma("pool", Wdt[:], d["ssd_wdt"], writes=[Wdt])
        k.dma("sp", tri[:], d["ssd_tri"], writes=[tri])
        k.dma("pool", trib[:], d["ssd_tri"], writes=[trib])
        k.dma("sp", cw[:], d["ssd_cw"], writes=[cw])
        k.dma("sp", cb[:], d["ssd_cb"], writes=[cb])
        k.dma("sp", dtb[:], d["ssd_dtb"].partition_broadcast(128), writes=[dtb])
        k.dma("sp", abc[:], d["ssd_alog"].partition_broadcast(128), writes=[abc])
        k.dma("sp", dsk[:], d["ssd_dsk"].partition_broadcast(128), writes=[dsk])
        k.dma("sp", ng[:], d["ssd_ng"], writes=[ng])
        k.op("pool", lambda e: e.memset(ones_f[:], 1.0), [], [ones_f])
        k.op("act", lambda e: e.activation(abc[:], abc[:], AF.Exp), [abc], [abc])
        k.op("dve", lambda e: e.tensor_scalar(abc[:], abc[:], -1.0, None, ALU.mult), [abc], [abc])
        ssd_out = self.dout["ssd_out"]
        Gm = self.Gm[l]
        bf_view = yacc.t.bitcast(BF16)
        n3 = 0
        ntr = 0

        def bc8(buf, c, col0):
            return bass.AP(buf.t, c * 64 + col0, [[8 * 64, 128], [1, 8], [0, 64]])

        for blk in range(2):
            tiles = [2 * blk, 2 * blk + 1]
            seqs = [(i * 256, 256) for i in range(4)] if blk == 0 else [(0, 1024)]
            ns = len(seqs)
            temps = ([(yacc, bf_view[:, :, 0:512])], [(raw, raw[:, 0:512])], [(raw, raw[:, 512:1024]), (cv, cv[:, 0:512]), (cv, cv[:, 512:1024])])
            self.norm_mod(l, 1, hTb, ph, tiles=tiles, temps=temps)
            for c in range(8):
                hsl = hTb[c // 4]
                o_ = (c % 4) * 128
                pd = self.psum()
                k.mm(pd, [(pd[:, 0:64], hsl[:, kc, o_:o_ + 128], Wdt[:, kc, :], kc == 0, kc == 7) for kc in range(8)], [hsl, Wdt])
                k.op("dve", lambda e: e.tensor_tensor(dt_t[:, c, :], pd[:, 0:64], dtb[:], ALU.add), [pd, dtb], [dt_t])
                k.op("act", lambda e: e.activation(dt_t[:, c, :], dt_t[:, c, :], AF.Exp), [dt_t], [dt_t])
                k.op("act", lambda e: e.activation(dt_t[:, c, :], dt_t[:, c, :], AF.Ln, bias=1.0), [dt_t], [dt_t])
                k.op("dve", lambda e: e.tensor_tensor(dtA[:, c, :], dt_t[:, c, :], abc[:], ALU.mult), [dt_t, abc], [dtA])
                k.op("act", lambda e: e.copy(dtAh[:, c, :], dtA[:, c, :]), [dtA], [dtAh])
                k.op("dve", lambda e: e.tensor_tensor(dtAl[:, c, :], dtA[:, c, :], dtAh[:, c, :], ALU.subtract), [dtA, dtAh], [dtAl])
                pc = self.psum()
                k.mm(pc, [(pc[:, 0:32], tri[:, 0, :], dtA[:, c, 0:32], True, True)], [tri, dtA])
                k.mm(pc, [(pc[:, 32:64], tri[:, 1, :], dtA[:, c, 32:64], True, True)], [tri, dtA])
                k.op("act", lambda e: e.activation(ecum[:, c, :], pc[:, 0:64], AF.Exp), [pc], [ecum])
                pe_ = self.psum()
                k.mm(pe_, [(pe_[:, 0:32], tri[:, 2, :], dtA[:, c, 0:32], True, True)], [tri, dtA])
                k.mm(pe_, [(pe_[:, 32:64], tri[:, 3, :], dtA[:, c, 32:64], True, True)], [tri, dtA])
                k.op("act", lambda e: e.activation(edec[:, c, :], pe_[:, 0:64], AF.Exp), [pe_], [edec])
                pg = self.psum()
                k.mm(pg, [(pg[:, 0:64], ones_f[:], dtA[:, c, :], True, True)], [ones_f, dtA])
                k.op("act", lambda e: e.activation(gdec[:, c, :], pg[:, 0:64], AF.Exp), [pg], [gdec])

            def conv_chunk(Wt, col0, cidx, dst):
                pps = []
                for ti in range(2):
                    pp = self.psum()
                    k.mm(pp, [(pp[:], Wt[:, kc, col0:col0 + 128], hTb[ti][:, kc, :], kc == 0, kc == 7) for kc in range(8)], [Wt, hTb[ti]])
                    pps.append(pp)
                    k.op("act", lambda e: e.activation(dst[:, ti * 512:(ti + 1) * 512], pp[:], AF.Identity, bias=cb[:, cidx:cidx + 1], scale=cw[:, cidx, 1:2]),
                         [pp, cw, cb], [dst])
                nseg = ns // 2 if ns > 1 else 1
                T = 512 // nseg
                for ti in range(2):
                    pp = pps[ti]
                    d3 = dst[:, ti * 512:(ti + 1) * 512].rearrange("p (a b) -> p a b", a=nseg)
                    p3 = pp[:].rearrange("p (a b) -> p a b", a=nseg)
                    k.op("dve", lambda e: e.scalar_tensor_tensor(d3[:, :, 1:T], p3[:, :, 0:T - 1], cw[:, cidx, 0:1], d3[:, :, 1:T], ALU.mult, ALU.add), [pp, cw, dst], [dst])
                    k.op("dve", lambda e: e.scalar_tensor_tensor(d3[:, :, 0:T - 1], p3[:, :, 1:T], cw[:, cidx, 2:3], d3[:, :, 0:T - 1], ALU.mult, ALU.add), [pp, cw, dst], [dst])
                if ns == 1:
                    k.op("dve", lambda e: e.scalar_tensor_tensor(dst[:, 512:513], pps[0][:, 511:512], cw[:, cidx, 0:1], dst[:, 512:513], ALU.mult, ALU.add), [pps[0], cw, dst], [dst])
                    k.op("dve", lambda e: e.scalar_tensor_tensor(dst[:, 511:512], pps[1][:, 0:1], cw[:, cidx, 2:3], dst[:, 511:512], ALU.mult, ALU.add), [pps[1], cw, dst], [dst])
                k.op("act", lambda e: e.activation(dst[:], dst[:], AF.Silu), [dst], [dst])

            for g in range(4):
                for hf in range(2):
                    k.dma("pool", Wp[:, hf * 4:(hf + 1) * 4, :], d["ssd_wx"][g, :, hf * 4:(hf + 1) * 4, :], writes=[Wp])
                for xc in range(4):
                    cvb = (raw, cv)[xc % 2]
                    conv_chunk(Wp, xc * 128, 4 * g + xc, cvb)
                    for c in range(8):
                        pt = self.psum()
                        k.tr(pt, pt[:, 0:128], cvb[:, c * 128:(c + 1) * 128], self.ident_f[:], [cvb, self.ident_f])
                        ntr += 1
                        if ntr % 2:
                            k.op("act", lambda e: e.copy(xtok[:, c, xc * 128:(xc + 1) * 128], pt[:, 0:128]), [pt], [xtok])
                        else:
                            k.op("dve", lambda e: e.tensor_copy(xtok[:, c, xc * 128:(xc + 1) * 128], pt[:, 0:128]), [pt], [xtok])
                k.dma("pool", Wp[:, :, 0:256], d["ssd_wbc"][g], writes=[Wp])
                conv_chunk(Wp, 0, 16 + g, raw)
                k.op("pool", lambda e: e.tensor_copy(BcT[:], raw[:]), [raw], [BcT])
                for c in range(8):
                    pt = self.psum()
                    k.tr(pt, pt[:, 0:128], raw[:, c * 128:(c + 1) * 128], self.ident_f[:], [raw, self.ident_f])
                    k.op("act", lambda e: e.copy(Btok[:, c, :], pt[:, 0:128]), [pt], [Btok])
                conv_chunk(Wp, 128, 20 + g, cv)
                k.op("pool", lambda e: e.tensor_copy(CcT[:], cv[:]), [cv], [CcT])
                for hf in range(2):
                    k.dma("pool", Wp[:, hf * 4:(hf + 1) * 4, :], d["ssd_wz"][g, :, hf * 4:(hf + 1) * 4, :], writes=[Wp])
                for si, (s0, T) in enumerate(seqs):
                    nch = T // 128
                    c_first = s0 // 128
                    for e_ in range(2):
                        if blk == 1:
                            k.dma("sp", STs[e_][:].rearrange("p (a b) -> p a b", a=8), d["ssd_s0"][:, e_, 8 * g:8 * g + 8, :], writes=[STs[e_]])
                        else:
                            k.op("pool", lambda e: e.memset(STs[e_][:], 0.0), [], [STs[e_]])
                        k.op("act", lambda e: e.copy(STb[e_][:], STs[e_][:]), [STs[e_]], [STb[e_]])
                    def unit(e_, c, first, u):
                        ST32, STbf = STs[e_], STb[e_]
                        hc0 = e_ * 32 + 8 * g
                        cs = slice(c * 128, (c + 1) * 128)
                        hc0 = e_ * 32 + 8 * g
                        pcb = self.psum()
                        k.mm(pcb, [(pcb[:, 0:128], BcT[:, cs], CcT[:, cs], True, True)], [BcT, CcT])
                        cbm = CBm[e_]
                        yield
                        k.op("dve", lambda e: e.tensor_tensor(cbm[:], pcb[:, 0:128], tri[:, e_, :], ALU.mult), [pcb, tri], [cbm])
                        dx = dtx[e_]
                        k.op("pool", lambda e: e.tensor_tensor(dx[:].rearrange("p (a b) -> p a b", a=8), xtok[:, c, :].rearrange("p (a b) -> p a b", a=8),
                                                               bc8(dt_t, c, hc0), ALU.mult), [xtok, dt_t], [dx])
                        pyd = self.pyd[e_]
                        tri_b = bass.AP(trib.t, e_ * 128, [[512, 128], [0, 8], [1, 128]])
                        lhh, lhl = lhs[e_]
                        k.op("dve", lambda e: e.tensor_tensor(lhh[:], tri_b, bass.AP(dtAh.t, c * 64 + hc0, [[512, 128], [1, 8], [0, 128]]), ALU.mult), [trib, dtAh], [lhh])
                        k.op("dve", lambda e: e.tensor_tensor(lhl[:], tri_b, bass.AP(dtAl.t, c * 64 + hc0, [[512, 128], [1, 8], [0, 128]]), ALU.mult), [trib, dtAl], [lhl])
                        cbm_b = bass.AP(cbm.t, 0, [[128, 128], [0, 4], [1, 128]])
                        for half in range(2):
                            i3 = e_
                            psg = self.psum()
                            mms = [(psg[:], trib[:, 2 + e_, :], lhh[:, half * 4:(half + 1) * 4, :].rearrange("p a b -> p (a b)"), True, False),
                                   (psg[:], trib[:, 2 + e_, :], lhl[:, half * 4:(half + 1) * 4, :].rearrange("p a b -> p (a b)"), False, True)]
                            yield
                            k.mm(psg, mms, [lhh, lhl, trib])
                            yield
                            k.op("act", lambda e: e.activation(Dec[i3][:], psg[:], AF.Exp), [psg], [Dec[i3]])
                            k.op("dve", lambda e: e.tensor_tensor(Mt[i3][:].rearrange("p (a b) -> p a b", a=4), Dec[i3][:].rearrange("p (a b) -> p a b", a=4), cbm_b, ALU.mult),
                                 [Dec[i3], cbm], [Mt[i3]])
                            yield
                            k.mm(pyd, [(pyd[:, (half * 4 + q) * 64:(half * 4 + q + 1) * 64], Mt[i3][:, q * 128:(q + 1) * 128], dx[:, (half * 4 + q) * 64:(half * 4 + q + 1) * 64], True, True)
                                       for q in range(4)], [Mt[i3], dx])
                        yield
                        pyo = self.psum()
                        k.mm(pyo, [(pyo[:], CcT[:, cs], STbf[:], True, True)], [CcT, STbf])
                        t_ = tA[(2 * u) % 3]
                        k.op("dve", lambda e: e.tensor_tensor(t_[:].rearrange("p (a b) -> p a b", a=8), pyo[:].rearrange("p (a b) -> p a b", a=8),
                                                              bc8(ecum, c, hc0), ALU.mult), [pyo, ecum], [t_])
                        if first:
                            k.op("dve", lambda e: e.tensor_tensor(yacc[:, c, :], t_[:], pyd[:], ALU.add), [t_, pyd], [yacc])
                        else:
                            k.op("pool", lambda e: e.tensor_tensor(yacc[:, c, :], yacc[:, c, :], t_[:], ALU.add), [t_, yacc], [yacc])
                            k.op("dve", lambda e: e.tensor_tensor(yacc[:, c, :], yacc[:, c, :], pyd[:], ALU.add), [pyd, yacc], [yacc])
                        dxd = dtxd[e_]
                        k.op("pool", lambda e: e.tensor_tensor(dxd[:].rearrange("p (a b) -> p a b", a=8), dx[:].rearrange("p (a b) -> p a b", a=8),
                                                               bc8(edec, c, hc0), ALU.mult), [dx, edec], [dxd])
                        yield
                        pU = self.psum()
                        k.mm(pU, [(pU[:], Btok[:, c, :], dxd[:], True, True)], [Btok, dxd])
                        t2 = tA[(2 * u + 1) % 3]
                        k.op("dve", lambda e: e.tensor_tensor(t2[:].rearrange("p (a b) -> p a b", a=8), ST32[:].rearrange("p (a b) -> p a b", a=8),
                                                              bc8(gdec, c, hc0), ALU.mult), [ST32, gdec], [t2])
                        k.op("dve", lambda e: e.tensor_tensor(ST32[:], t2[:], pU[:], ALU.add), [t2, pU], [ST32])
                        k.op("act", lambda e: e.copy(STbf[:], ST32[:]), [ST32], [STbf])

                    written = set()
                    for step in range(nch):
                        gens = []
                        for e_ in range(2):
                            c = c_first + step if e_ == 0 else c_first + nch - 1 - step
                            gens.append(unit(e_, c, c not in written, n3))
                            written.add(c)
                            n3 += 1
                        while gens:
                            for gn in list(gens):
                                try:
                                    next(gn)
                                except StopIteration:
                                    gens.remove(gn)
                    if blk == 0:
                        for e_ in range(2):
                            k.dma("sp", ssd_out[:, si, e_, 8 * g:8 * g + 8, :], STs[e_][:].rearrange("p (a b) -> p a b", a=8), reads=[STs[e_]], is_output=True)
                dskb = bass.AP(dsk.t, 8 * g, [[32, 128], [1, 8], [0, 64]])
                tsets = [(tA[0], View(cv, cv[:, 0:512]), View(cv, cv[:, 512:1024])),
                         (Dec[0], View(raw, raw[:, 0:512]), View(raw, raw[:, 512:1024]))]

                def unit_d(c, ts):
                    t_, t2, t3 = ts
                    k.op("pool", lambda e: e.tensor_tensor(t_[:].rearrange("p (a b) -> p a b", a=8), xtok[:, c, :].rearrange("p (a b) -> p a b", a=8), dskb, ALU.mult), [xtok, dsk], [t_])
                    k.op("pool", lambda e: e.tensor_tensor(yacc[:, c, :], yacc[:, c, :], t_[:], ALU.add), [t_, yacc], [yacc])
                    hsl = hTb[c // 4]
                    o_ = (c % 4) * 128
                    pz = self.psum()
                    k.mm(pz, [(pz[:], hsl[:, kc, o_:o_ + 128], Wp[:, kc, :], kc == 0, kc == 7) for kc in range(8)], [hsl, Wp])
                    yield
                    k.op("act", lambda e: e.activation(t2[:], pz[:], AF.Silu), [pz], [t2])
                    k.op("dve", lambda e: e.tensor_tensor(t2[:], t2[:], yacc[:, c, :], ALU.mult), [t2, yacc], [t2])
                    k.op("pool", lambda e: e.tensor_tensor(t3[:], t2[:], t2[:], ALU.mult), [t2], [t3])
                    k.op("dve", lambda e: e.tensor_reduce(ssqp[:, c, g:g + 1], t3[:], AX.X, ALU.add), [t3], [ssqp])
                    yield
                    for q in range(4):
                        pt = self.psum()
                        k.tr(pt, pt[:, 0:128], t2[:, q * 128:(q + 1) * 128], self.ident_f[:], [t2, self.ident_f])
                        fc = 4 * g + q
                        if q % 2:
                            k.op("act", lambda e: e.activation(yT[:, fc, c * 128:(c + 1) * 128], pt[:, 0:128], AF.Identity, scale=ng[:, fc:fc + 1]), [pt, ng], [yT])
                        else:
                            k.op("dve", lambda e: e.tensor_scalar(yT[:, fc, c * 128:(c + 1) * 128], pt[:, 0:128], ng[:, fc:fc + 1], None, ALU.mult), [pt, ng], [yT])
                        yield

                for c2 in range(0, 8, 2):
                    gens = [unit_d(c2, tsets[0]), unit_d(c2 + 1, tsets[1])]
                    while gens:
                        for gn in list(gens):
                            try:
                                next(gn)
                            except StopIteration:
                                gens.remove(gn)
            k.op("dve", lambda e: e.tensor_reduce(ssq[:], ssqp[:], AX.X, ALU.add), [ssqp], [ssq])
            k.op("act", lambda e: e.activation(ssq[:], ssq[:], AF.Sqrt, bias=self.eps_t[:, 0:1], scale=1.0 / 2048), [ssq, self.eps_t], [ssq])
            k.op("dve", lambda e: e.reciprocal(ssq[:], ssq[:]), [ssq], [ssq])
            for c in range(8):
                i3 = c % 3
                k.op("pool", lambda e: e.tensor_scalar(Dec[0][:, i3 * 128:(i3 + 1) * 128], ones_f[:], ssq[:, c:c + 1], None, ALU.mult), [ones_f, ssq], [Dec[0]])
                pr_ = self.psum()
                k.mm(pr_, [(pr_[:, 0:128], Dec[0][:, i3 * 128:(i3 + 1) * 128], self.ident_f[:], True, True)], [Dec[0], self.ident_f])
                k.op("act", lambda e: e.copy(rbc[:, c * 128:(c + 1) * 128], pr_[:, 0:128]), [pr_], [rbc])
            for dp in range(8):
                k.dma("pool", Wo[:], d["ssd_wo"][:, :, dp * 128:(dp + 1) * 128], writes=[Wo])
                for d2 in range(1):
                    dc = dp
                    for ti, t in enumerate(tiles):
                        j = 0 if t < 2 else 1
                        py = self.psum()
                        k.mm(py, [(py[:], Wo[:, kc, d2 * 128:(d2 + 1) * 128], yT[:, kc, ti * 512:(ti + 1) * 512], kc == 0, kc == 15) for kc in range(16)], [Wo, yT])
                        t_ = tA[(dc + ti) % 3]
                        k.op("dve", lambda e: e.tensor_tensor(t_[:], py[:], rbc[:, ti * 512:(ti + 1) * 512], ALU.mult), [py, rbc], [t_])
                        x = self.x[t]
                        k.op("dve", lambda e: e.scalar_tensor_tensor(x[:, dc, :], t_[:], Gm[:, 1, dc, j:j + 1], x[:, dc, :], ALU.mult, ALU.add), [t_, Gm, x], [x])


def host_layout(inp):
    f = lambda a: np.ascontiguousarray(a, dtype=np.float32)
    sh = {}
    w = inp["w_ffn_in"].reshape(8, 8, 128, 2, NFC, 128)
    sh["w1"] = f(w.transpose(0, 4, 2, 3, 1, 5).reshape(8 * NFC, 128, 2048))
    w = inp["w_ffn_out"].reshape(8, NFC, 128, 1024)
    sh["w2"] = f(w.transpose(0, 2, 1, 3).reshape(8, 128, NFC * 1024))
    w = inp["w_ada"].reshape(4, 8, 128, 18, 512)
    sh["wada"] = f(w.transpose(0, 3, 2, 1, 4).reshape(4, 18, 128, 4096))
    sh["bada"] = f(inp["b_ada"].reshape(4, 72, 128).transpose(2, 0, 1))
    sh["ng"] = f(inp["norm_g"].reshape(4, 3, 8, 128).transpose(3, 0, 1, 2))
    sh["fg"] = f(inp["final_g"].reshape(8, 128).T)
    sh["ident"] = np.eye(128, dtype=np.float32)
    kcm = lambda w: f(w.reshape(8, 128, -1).transpose(1, 0, 2).reshape(128, -1))
    sh["gm_wu"] = kcm(inp["gm_w_in"][0][:, :1024]); sh["gm_wv"] = kcm(inp["gm_w_in"][0][:, 1024:]); sh["gm_wo"] = kcm(inp["gm_w_out"][0])
    sh["gm_wsT"] = f(inp["gm_w_s"][0].transpose(2, 0, 1).reshape(128, 1024))
    sh["gm_lng"] = f(inp["gm_ln_g"][0].reshape(1, 1024)); sh["gm_lnb"] = f(inp["gm_ln_b"][0].reshape(1, 1024))
    sh["gm_bs"] = f(inp["gm_b_s"][0].reshape(1, 1024))
    w = inp["nat_w_qkv"][0].reshape(8, 128, 3, 8, 128)
    sh["nat_wqkv"] = f(w.transpose(3, 1, 2, 0, 4).reshape(8, 128, 3072))
    sh["nat_wo"] = kcm(inp["nat_w_out"][0])
    kc_ = np.arange(64)[:, None]; qc_ = np.arange(64)[None, :]
    idx = np.clip(kc_ - qc_ + 15, 0, 30)
    tb = inp["nat_rpb"][0][:, :, idx]
    tb = tb.reshape(8, 2, 15, 64, 64).transpose(0, 1, 4, 2, 3)
    sh["nat_tbT"] = f(tb.reshape(8, 128, 960))
    cs = np.clip(np.arange(64) - 8, 0, 48)[None, :]
    valid = (kc_ >= cs) & (kc_ < cs + 16)
    mk = np.where(valid.T, 0.0, -30000.0).astype(np.float32)
    sh["nat_maskT"] = np.ascontiguousarray(np.concatenate([mk, mk], 0))
    wi = inp["gla_w_in"][0]
    dk = np.arange(128)
    partner = np.where(dk % 64 < 32, dk + 32, dk - 32)
    gw = np.zeros((4, 1024, 1024), np.float32)
    for h in range(4):
        q = wi[:, h * 128:(h + 1) * 128]; kk = wi[:, 512 + h * 128:512 + (h + 1) * 128]
        gw[h, :, 0:128] = q; gw[h, :, 128:256] = kk; gw[h, :, 256:384] = q[:, partner]; gw[h, :, 384:512] = kk[:, partner]
        gw[h, :, 512:768] = wi[:, 1024 + h * 256:1024 + (h + 1) * 256]
        gw[h, :, 768:1024] = wi[:, 2048 + h * 256:2048 + (h + 1) * 256]
    sh["gla_win"] = f(gw.reshape(4, 8, 128, 1024).transpose(0, 2, 1, 3))
    sh["gla_wo"] = f(inp["gla_w_out"][0].reshape(4, 2, 128, 1024).transpose(0, 2, 1, 3))
    sh["gla_wa1"] = f(inp["gla_w_a1"][0].transpose(1, 0, 2).reshape(8, 128, 32).transpose(1, 0, 2))
    a2 = np.zeros((33, 4, 2, 128), np.float32)
    for e in range(2):
        a2[e * 16:(e + 1) * 16, :, e, :] = inp["gla_w_a2"][0][e].reshape(16, 4, 128)
        a2[32, :, e, :] = inp["gla_b_a"][0][e].reshape(4, 128)
    sh["gla_wa2"] = f(a2.reshape(33, 4, 256))
    jj = np.arange(128)[:, None]; ii = np.arange(128)[None, :]
    c16 = -1.0 / 16.0
    tri = np.stack([(jj <= ii), (jj > ii), (jj >= ii), (jj < ii)], 1).astype(np.float32) * c16
    sh["gla_tri"] = f(tri)
    sh["gla_msk"] = f(np.stack([(jj <= ii), (jj >= ii)], 1).astype(np.float32))
    tpos = np.arange(1024)
    inv = 10000.0 ** (-np.arange(0, 64, 2, dtype=np.float64) / 64.0)
    pos = np.where((dk < 64)[:, None], (tpos // 64)[None, :], (tpos % 64)[None, :]).astype(np.float64)
    ang = pos * inv[dk % 32][:, None]
    sgn = np.where(dk % 64 < 32, -1.0, 1.0)[:, None]
    sh["gla_cos"] = f(np.cos(ang)); sh["gla_sin"] = f(np.sin(ang) * sgn)
    sh["gla_ng"] = f(inp["gla_norm_g"][0].reshape(8, 128).T)
    wi = inp["ssd_w_in"][0]
    kc3 = lambda w: f(w.reshape(8, 128, -1).transpose(1, 0, 2))
    sh["ssd_wz"] = f(np.stack([kc3(wi[:, 512 * g:512 * (g + 1)]) for g in range(4)]))
    sh["ssd_wx"] = f(np.stack([kc3(wi[:, 2048 + 512 * g:2048 + 512 * (g + 1)]) for g in range(4)]))
    sh["ssd_wbc"] = f(np.stack([kc3(np.concatenate([wi[:, 4096 + 128 * g:4096 + 128 * (g + 1)], wi[:, 4608 + 128 * g:4608 + 128 * (g + 1)]], 1)) for g in range(4)]))
    sh["ssd_wdt"] = kc3(wi[:, 5120:5184])
    sh["ssd_wo"] = f(inp["ssd_w_out"][0].reshape(16, 128, 1024).transpose(1, 0, 2))
    sh["ssd_tri"] = f(np.stack([(jj <= ii), (jj >= ii), (jj > ii), (jj < ii)], 1).astype(np.float32))
    sh["ssd_cw"] = f(inp["ssd_conv_w"][0].reshape(3, 24, 128).transpose(2, 1, 0))
    sh["ssd_cb"] = f(inp["ssd_conv_b"][0].reshape(24, 128).T)
    sh["ssd_dtb"] = f(inp["ssd_dt_bias"][0].reshape(1, 64)); sh["ssd_alog"] = f(inp["ssd_a_log"][0].reshape(1, 64))
    sh["ssd_dsk"] = f(inp["ssd_d"][0].reshape(1, 32)); sh["ssd_ng"] = f(inp["ssd_norm_g"][0].reshape(16, 128).T)
    return sh


def core_inputs(c, inp, sh):
    b = c // 4
    xp = inp["x_prompt"][4 * c:4 * c + 4].reshape(1024, D)
    xs = inp["x_sample"][b]
    xall = np.concatenate([xp, xs], 0)
    m = dict(sh)
    m["xT"] = np.ascontiguousarray(xall.T.reshape(8, 128, NTOK).transpose(1, 0, 2), dtype=np.float32)
    cond = np.stack([inp["c_ctx"], inp["c"][b]], 1)
    m["cond"] = np.ascontiguousarray(cond.reshape(8, 128, 2).transpose(1, 0, 2), dtype=np.float32)
    ck = inp["cache_nat_k"][b, 0].reshape(8, 2, 256, 64)
    m["nat_ckT"] = np.ascontiguousarray(ck.transpose(0, 1, 3, 2).reshape(8, 128, 256), dtype=np.float32)
    m["ssd_s0"] = np.ascontiguousarray(inp["state_ssd"][b, 0].transpose(3, 0, 1, 2), dtype=np.float32)
    m["gla_s0"] = np.ascontiguousarray(inp["state_gla"][b, 0].transpose(2, 0, 1, 3), dtype=np.float32)
    cv = inp["cache_nat_v"][b, 0].reshape(8, 2, 2, 128, 64)
    m["nat_cv"] = np.ascontiguousarray(cv.transpose(0, 3, 2, 1, 4).reshape(8, 128, 256), dtype=np.float32)
    return m


_CACHE = {}


def run(inputs, stop=None, debug=False, start=None, xover=None):
    inp = {k: np.asarray(v) for k, v in inputs.items()}
    key = (stop, debug, start)
    if key not in _CACHE:
        p = Prog(stop=stop, debug=debug, start=start)
        p.build()
        _CACHE[key] = p
    p = _CACHE[key]
    sh = host_layout(inp)
    in_maps = [core_inputs(c, inp, sh) for c in range(NCORES)]
    if xover is not None:
        for c in range(NCORES):
            in_maps[c]["xT"] = xover[c]
    if p.lite:
        for c in range(NCORES):
            in_maps[c]["w1"] = sh["w1"][:1]; in_maps[c]["w2"] = sh["w2"][:1]
            in_maps[c]["wada"] = sh["wada"][start[1]:start[1] + 1]
    res = run_bass_kernel_spmd(p.nc, in_maps, core_ids=list(range(NCORES)))
    return p, res.results


def assemble_y(results):
    yp = np.zeros((32, 256, D), np.float32)
    ys = np.zeros((2, 1024, D), np.float32)
    for c in range(NCORES):
        yT = results[c]["yT"]
        y = yT.transpose(2, 1, 0).reshape(NTOK, D)
        yp[4 * c:4 * c + 4] = y[:1024].reshape(4, 256, D)
        q = c % 4
        ys[c // 4, q * 256:(q + 1) * 256] = y[1024 + q * 256:1024 + (q + 1) * 256]
    return yp, ys


def kernel(**inputs):
    p, results = run(inputs)
    yp, ys = assemble_y(results)
    gla = np.zeros((32, 1, 2, 4, 128, 256), np.float32)
    ck = np.zeros((32, 1, 16, 256, 64), np.float32)
    cvv = np.zeros((32, 1, 16, 256, 64), np.float32)
    ssd = np.zeros((32, 1, 2, 32, 64, 128), np.float32)
    for c in range(NCORES):
        r = results[c]
        gla[4 * c:4 * c + 4, 0] = r["gla_out"].transpose(1, 2, 3, 0, 4)
        kc = r["kc_out"].reshape(2, 64, 8, 4, 256)
        ck[4 * c:4 * c + 4, 0] = kc.transpose(3, 2, 0, 4, 1).reshape(4, 16, 256, 64)
        vc = r["vc_out"].reshape(128, 8, 4, 2, 2, 64)
        cvv[4 * c:4 * c + 4, 0] = vc.transpose(2, 1, 4, 3, 0, 5).reshape(4, 16, 256, 64)
        ssd[4 * c:4 * c + 4, 0] = r["ssd_out"].transpose(1, 2, 3, 4, 0)
    return yp, ys, gla, ck, cvv, ssd
```

```python
import numpy as np
from contextlib import ExitStack
import concourse.bass as bass
import concourse.mybir as mybir
from concourse.bass_utils import run_bass_kernel_spmd

F32 = mybir.dt.float32
BF16 = mybir.dt.bfloat16
AF = mybir.ActivationFunctionType
ALU = mybir.AluOpType
AX = mybir.AxisListType

NCORES = 8
D = 1024
NTOK = 2048
TT = 4
DFF = 2816
NFC = 22
FGROUPS = [(0, 6), (6, 6), (12, 5), (17, 5)]
EPS = 1e-6


LAST_NAMES = []
NATV = 0


class Buf:
    __slots__ = ("t", "lw", "rd", "name", "psum")

    def __init__(self, t, name=""):
        self.psum = False
        self.t = t
        self.lw = None
        self.rd = {}
        self.name = name

    def __getitem__(self, k):
        return self.t[k]


class View:
    def __init__(self, parent, ap):
        self.parent = parent
        self.ap = ap
        self.psum = parent.psum
        self.name = parent.name

    lw = property(lambda self: self.parent.lw, lambda self, v: setattr(self.parent, "lw", v))
    rd = property(lambda self: self.parent.rd, lambda self, v: setattr(self.parent, "rd", v))
    t = property(lambda self: self.ap.tensor)

    def __getitem__(self, k):
        return self.ap[k]


class Eng:
    def __init__(self, name, e, sem):
        self.name = name
        self.e = e
        self.sem = sem
        self.cnt = 0
        self.seen = {}


class K:
    def __init__(self, nc, es):
        self.nc = nc
        self.es = es
        self.sems = {}
        self.engs = {}
        for name, e in (("pe", nc.tensor), ("act", nc.scalar), ("dve", nc.vector),
                        ("pool", nc.gpsimd), ("sp", nc.sync)):
            s = es.enter_context(nc.semaphore("s_" + name))
            self.sems[name] = s
            self.engs[name] = Eng(name, e, s)
        self.dma_sems = []
        self.pools = {}
        for pname, cnt in (("hw", 28), ("sw", 60)):
            lst = []
            for i in range(cnt):
                nm = "%s%d" % (pname, i)
                s = es.enter_context(nc.semaphore(nm))
                self.sems[nm] = s
                ent = [nm, 0]
                lst.append(ent)
                self.dma_sems.append(ent)
            self.pools[pname] = [lst, 0]
        self.n_inst = 0
        self.out_tokens = []
        self.uid = 0

    def sb(self, name, shape, dt=F32, es=None):
        self.uid += 1
        LAST_NAMES.append("%s_%d" % (name, self.uid + 0))
        t = (es or self.es).enter_context(self.nc.sbuf_tensor("%s_%d" % (name, self.uid), list(shape), dt))
        return Buf(t, name)

    def ps(self, name, shape, dt=F32, es=None):
        t = (es or self.es).enter_context(self.nc.psum_tensor(name, list(shape), dt))
        b = Buf(t, name)
        b.psum = True
        return b

    def _wait(self, eng, deps):
        need = {}
        for (k, v) in deps:
            if v > need.get(k, 0):
                need[k] = v
        for k, v in need.items():
            if eng.seen.get(k, 0) >= v:
                continue
            eng.e.wait_ge(self.sems[k], v)
            eng.seen[k] = v
            self.n_inst += 1

    def _deps(self, reads, writes, en=None):
        deps = []
        for b in reads:
            if b.lw is not None:
                deps.append(b.lw)
            if b.psum:
                for k_, v_ in b.rd.items():
                    if k_ != en:
                        deps.append((k_, v_))
        for b in writes:
            if b.lw is not None:
                deps.append(b.lw)
            for k, v in b.rd.items():
                deps.append((k, v))
        return deps

    def _mark(self, tok, reads, writes):
        for b in reads:
            if b.rd.get(tok[0], 0) < tok[1]:
                b.rd[tok[0]] = tok[1]
        for b in writes:
            b.lw = tok
            b.rd = {}

    def op(self, en, fn, reads=(), writes=()):
        eng = self.engs[en]
        self._wait(eng, self._deps(reads, writes, en))
        ins = fn(eng.e)
        eng.cnt += 1
        ins.then_inc(eng.sem, 1)
        tok = (en, eng.cnt)
        self._mark(tok, reads, writes)
        self.n_inst += 1
        return tok

    def mm(self, out_buf, mms, reads):
        eng = self.engs["pe"]
        self._wait(eng, self._deps(reads, [out_buf]))
        n = len(mms)
        for i, m in enumerate(mms):
            ins = eng.e.matmul(m[0], m[1], m[2], start=m[3], stop=m[4])
            self.n_inst += 1
            if i == n - 1:
                eng.cnt += 1
                ins.then_inc(eng.sem, 1)
        tok = ("pe", eng.cnt)
        self._mark(tok, reads, [out_buf])
        return tok

    def tr(self, out_buf, out_ap, in_ap, ident_ap, reads):
        eng = self.engs["pe"]
        self._wait(eng, self._deps(reads, [out_buf]))
        ins = eng.e.transpose(out_ap, in_ap, ident_ap)
        eng.cnt += 1
        ins.then_inc(eng.sem, 1)
        tok = ("pe", eng.cnt)
        self._mark(tok, reads, [out_buf])
        self.n_inst += 1
        return tok

    def dma(self, qn, out_ap, in_ap, reads=(), writes=(), is_output=False):
        eng = self.engs[qn]
        pl = self.pools["sw" if qn == "pool" else "hw"]
        slot = pl[0][pl[1]]
        pl[1] = (pl[1] + 1) % len(pl[0])
        deps = self._deps(reads, writes)
        if slot[1] > 0:
            deps.append((slot[0], slot[1]))
        self._wait(eng, deps)
        ins = eng.e.dma_start(out=out_ap, in_=in_ap)
        slot[1] += 16
        ins.then_inc(self.sems[slot[0]], 16)
        tok = (slot[0], slot[1])
        self._mark(tok, reads, writes)
        self.n_inst += 1
        if is_output:
            self.out_tokens.append(tok)
        return tok

    def all_tokens(self):
        deps = []
        for s in self.dma_sems:
            if s[1] > 0:
                deps.append((s[0], s[1]))
        for n in ("pe", "act", "dve", "pool"):
            if self.engs[n].cnt > 0:
                deps.append((n, self.engs[n].cnt))
        return deps

    def barrier(self):
        deps = self.all_tokens()
        for n in ("pe", "act", "dve", "pool", "sp"):
            self._wait(self.engs[n], deps)

    def finish(self):
        self._wait(self.engs["sp"], self.all_tokens() + list(self.out_tokens))


class Prog:
    def __init__(self, stop=None, debug=False, start=None):
        self.start = start
        self.lite = start is not None and start[0] == "mix" and stop == start
        self.stop = stop
        self.debug = debug
        self.nc = bass.Bass("TRN2", target_bir_lowering=False)
        self.din = {}
        self.dout = {}

    def I(self, name, shape):
        self.din[name] = self.nc.dram_tensor(name, list(shape), F32, kind="ExternalInput").ap()
        return self.din[name]

    def O(self, name, shape):
        self.dout[name] = self.nc.dram_tensor(name, list(shape), F32, kind="ExternalOutput").ap()
        return self.dout[name]

    def psum(self):
        b = self.pbanks[self.prr]
        self.prr = (self.prr + 1) % len(self.pbanks)
        return b

    def build(self):
        nc = self.nc
        xT_d = self.I("xT", [128, 8, NTOK])
        cond_d = self.I("cond", [128, 8, 2])
        w1_d = self.I("w1", [1 if self.lite else 8 * NFC, 128, 2048])
        w2_d = self.I("w2", [1 if self.lite else 8, 128, NFC * 1024])
        wada_d = self.I("wada", [1 if self.lite else 4, 18, 128, 4096])
        bada_d = self.I("bada", [128, 4, 72])
        ng_d = self.I("ng", [128, 4, 3, 8])
        fg_d = self.I("fg", [128, 8])
        yT_d = self.O("yT", [128, 8, NTOK])
        self.declare_mixer_io()

        with ExitStack() as es:
            k = K(nc, es)
            self.k = k
            self.pbanks = [k.ps("pb%d" % i, [128, 512]) for i in range(6)]
            self.psm = k.ps("psm", [128, 512])
            self.pyd = [self.psm, k.ps("psd2", [128, 512])]
            self.prr = 0
            self.x = [k.sb("x%d" % t, [128, 8, 512]) for t in range(TT)]
            self.ones_bf = k.sb("ones_bf", [128, 128], BF16)
            self.scT = k.sb("scT", [128, 8, 2], BF16)
            self.condt = k.sb("condt", [128, 8, 2])
            self.bada = k.sb("bada", [128, 4, 72])
            self.ng = k.sb("ng", [128, 4, 3, 8])
            self.fg = k.sb("fg", [128, 8])
            self.mod = [k.sb("mod%d" % l, [128, 72, 2]) for l in range(4)]
            self.Am = [k.sb("Am%d" % l, [128, 3, 8, 2]) for l in range(4)]
            self.Gm = [k.sb("Gm%d" % l, [128, 3, 8, 2]) for l in range(4)]
            self.alloc_mixer_persistent()

            for t in range(TT):
                k.dma("sp", self.x[t][:], xT_d[:, :, t * 512:(t + 1) * 512], writes=[self.x[t]])
            k.dma("sp", self.condt[:], cond_d, writes=[self.condt])
            k.dma("sp", self.bada[:], bada_d, writes=[self.bada])
            k.dma("sp", self.ng[:], ng_d, writes=[self.ng])
            k.dma("sp", self.fg[:], fg_d, writes=[self.fg])
            k.op("pool", lambda e: e.memset(self.ones_bf[:], 1.0), [], [self.ones_bf])
            k.op("act", lambda e: e.activation(self.scT[:], self.condt[:], AF.Silu), [self.condt], [self.scT])
            self.load_mixer_consts()

            stages = []
            for l in range(4):
                stages += [("mod", l), ("ffn", l, 0), ("mix", l), ("ffn", l, 1)]
            done = False
            if self.start is not None:
                i0 = stages.index(self.start)
                stages = [("mod", self.start[1])] + stages[i0:]
            for st in stages:
                if st[0] == "mod":
                    self.modulation(st[1])
                elif st[0] == "ffn":
                    self.ffn(st[1], st[2])
                    if self.stop == ("ffn", st[1], st[2]):
                        done = True
                else:
                    self.mixer(st[1])
                    if self.stop == ("mix", st[1]):
                        done = True
                if done:
                    break
            self.final(yT_d, raw=self.debug)
            k.finish()
            self.n_inst = k.n_inst
        return nc

    def modulation(self, l):
        k = self.k
        wada_d = self.din["wada"]
        k.barrier()
        with ExitStack() as ph:
            wf = [k.sb("waf%d" % i, [128, 8, 512], F32, es=ph) for i in range(4)]
            wbs = [(k.sb("wbA%d" % i, [128, 3, 512], BF16, es=ph), k.sb("wbB%d" % i, [128, 3, 512], BF16, es=ph),
                    k.sb("wbC%d" % i, [128, 2, 512], BF16, es=ph)) for i in range(2)]
            psm = self.psm
            for t in range(18):
                w = wf[t % 4]
                k.dma("sp", w[:], wada_d[0 if self.lite else l, t].rearrange("p (a b) -> p a b", a=8), writes=[w])
                wA, wB, wC = wbs[t % 2]
                k.op("act", lambda e: e.copy(wA[:], w[:, 0:3, :]), [w], [wA])
                k.op("dve", lambda e: e.tensor_copy(wB[:], w[:, 3:6, :]), [w], [wB])
                k.op("pool", lambda e: e.tensor_copy(wC[:], w[:, 6:8, :]), [w], [wC])
                parts = [wA] * 3 + [wB] * 3 + [wC] * 2
                for cc in range(4):
                    ci = t * 4 + cc
                    mms = []
                    for kc in range(8):
                        wb = parts[kc]
                        kk = kc if kc < 3 else (kc - 3 if kc < 6 else kc - 6)
                        mms.append((psm[:, ci * 2:ci * 2 + 2], wb[:, kk, cc * 128:(cc + 1) * 128], self.scT[:, kc, :], kc == 0, kc == 7))
                    k.mm(psm, mms, [wA, wB, wC, self.scT])
            mod = self.mod[l]
            bb = bass.AP(self.bada.t, l * 72, [[4 * 72, 128], [1, 72], [0, 2]])
            k.op("dve", lambda e: e.tensor_tensor(mod[:], psm[:, 0:144].rearrange("p (a b) -> p a b", b=2), bb, ALU.add),
                 [psm, self.bada], [mod])
            Am, Gm = self.Am[l], self.Gm[l]
            for s in range(3):
                sc = mod[:, (3 * s + 1) * 8:(3 * s + 2) * 8, :]
                gt = mod[:, (3 * s + 2) * 8:(3 * s + 3) * 8, :]
                ngb = bass.AP(self.ng.t, l * 24 + s * 8, [[96, 128], [1, 8], [0, 2]])
                k.op("dve", lambda e: e.scalar_tensor_tensor(Am[:, s], sc, 1.0, ngb, ALU.add, ALU.mult), [mod, self.ng], [Am])
                k.op("dve", lambda e: e.tensor_scalar(Gm[:, s], gt, 0.5 if s != 1 else 1.0, None, ALU.mult), [mod], [Gm])
            k.barrier()

    def norm_mod(self, l, s, hT, ph, tiles=None, temps=None):
        k = self.k
        tiles = list(range(TT)) if tiles is None else tiles
        if temps is None:
            sqb = [k.sb("sq%d" % i, [128, 8, 512], BF16, es=ph) for i in range(2)]
            rsb = [k.sb("rs%d" % i, [128, 512], F32, es=ph) for i in range(2)]
            t3b = [k.sb("t32%d" % i, [128, 512], F32, es=ph) for i in range(3)]
            temps = ([(b, b[:]) for b in sqb], [(b, b[:]) for b in rsb], [(b, b[:]) for b in t3b])
        sql, rsl, t3l = temps
        Am, mod = self.Am[l], self.mod[l]
        n = 0
        for ti, t in enumerate(tiles):
            j = 0 if t < 2 else 1
            x = self.x[t]
            qb, q = sql[ti % len(sql)]
            rb, r = rsl[ti % len(rsl)]
            k.op("act", lambda e: e.activation(q, x[:], AF.Square), [x], [qb])
            pb = self.psum()
            k.mm(pb, [(pb[:], self.ones_bf[:], q[:, c, :], c == 0, c == 7) for c in range(8)], [self.ones_bf, qb])
            k.op("act", lambda e: e.activation(r, pb[:], AF.Sqrt, bias=self.eps_t[:, 0:1], scale=1.0 / D), [pb, self.eps_t], [rb])
            k.op("dve", lambda e: e.reciprocal(r, r), [rb], [rb])
            for c in range(8):
                tbb, tb = t3l[n % len(t3l)]
                n += 1
                k.op("dve", lambda e: e.scalar_tensor_tensor(tb, x[:, c, :], Am[:, s, c, j:j + 1], r, ALU.mult, ALU.mult),
                     [x, Am, rb], [tbb])
                k.op("act", lambda e: e.activation(hT[ti][:, c, :], tb, AF.Identity, bias=mod[:, 3 * s * 8 + c, j:j + 1]),
                     [tbb, mod], [hT[ti]])

    def ffn(self, l, i):
        k = self.k
        w1_d, w2_d = self.din["w1"], self.din["w2"]
        s = 0 if i == 0 else 2
        li = l * 2 + i
        k.barrier()
        with ExitStack() as ph:
            hT = [k.sb("hT%d" % t, [128, 8, 512], BF16, es=ph) for t in range(TT)]
            gT = [k.sb("gT%d" % t, [128, 6, 512], BF16, es=ph) for t in range(TT)]
            win = [k.sb("win%d" % t, [128, 2048], BF16, es=ph) for t in range(4)]
            wout = [k.sb("wout%d" % t, [128, 6 * 1024], BF16, es=ph) for t in range(2)]
            sl = [k.sb("sl%d" % t, [128, 512], F32, es=ph) for t in range(3)]
            self.norm_mod(l, s, hT, ph)
            Gm = self.Gm[l]
            nw = 0
            ns = 0
            for g, (kc0, n) in enumerate(FGROUPS):
                wo = wout[g % 2]
                if g > 0:
                    k.dma("pool", wo[:, 0:n * 1024], w2_d[li, :, kc0 * 1024:(kc0 + n) * 1024], writes=[wo])
                for fi in range(n):
                    fc = kc0 + fi
                    w = win[nw % 4]
                    nw += 1
                    k.dma("pool", w[:], w1_d[li * NFC + fc], writes=[w])
                    if g == 0 and fi == 1:
                        k.dma("pool", wo[:, 0:n * 1024], w2_d[li, :, kc0 * 1024:(kc0 + n) * 1024], writes=[wo])
                    for t in range(TT):
                        pa = self.psum()
                        k.mm(pa, [(pa[:], w[:, kc * 128:(kc + 1) * 128], hT[t][:, kc, :], kc == 0, kc == 7) for kc in range(8)], [w, hT[t]])
                        pu = self.psum()
                        k.mm(pu, [(pu[:], w[:, 1024 + kc * 128:1024 + (kc + 1) * 128], hT[t][:, kc, :], kc == 0, kc == 7) for kc in range(8)], [w, hT[t]])
                        st = sl[ns % 3]
                        ns += 1
                        k.op("act", lambda e: e.activation(st[:], pa[:], AF.Silu), [pa], [st])
                        k.op("dve", lambda e: e.tensor_tensor(gT[t][:, fi, :], st[:], pu[:], ALU.mult), [st, pu], [gT[t]])
                for dc in range(8):
                    for t in range(TT):
                        j = 0 if t < 2 else 1
                        py = self.psum()
                        k.mm(py, [(py[:], wo[:, q * 1024 + dc * 128:q * 1024 + dc * 128 + 128], gT[t][:, q, :], q == 0, q == n - 1)
                                  for q in range(n)], [wo, gT[t]])
                        x = self.x[t]
                        k.op("dve", lambda e: e.scalar_tensor_tensor(x[:, dc, :], py[:], Gm[:, s, dc, j:j + 1], x[:, dc, :], ALU.mult, ALU.add),
                             [py, Gm, x], [x])
            k.barrier()

    def final(self, yT_d, raw=False):
        k = self.k
        k.barrier()
        with ExitStack() as ph:
            if raw:
                for t in range(TT):
                    k.dma("sp", yT_d[:, :, t * 512:(t + 1) * 512], self.x[t][:], reads=[self.x[t]], is_output=True)
                return
            sq = [k.sb("fsq%d" % i, [128, 8, 512], BF16, es=ph) for i in range(2)]
            rs = [k.sb("frs%d" % i, [128, 512], F32, es=ph) for i in range(2)]
            yo = [k.sb("fyo%d" % i, [128, 8, 512], F32, es=ph) for i in range(2)]
            for t in range(TT):
                x = self.x[t]
                q, r, y = sq[t % 2], rs[t % 2], yo[t % 2]
                k.op("act", lambda e: e.activation(q[:], x[:], AF.Square), [x], [q])
                pb = self.psum()
                k.mm(pb, [(pb[:], self.ones_bf[:], q[:, c, :], c == 0, c == 7) for c in range(8)], [self.ones_bf, q])
                k.op("act", lambda e: e.activation(r[:], pb[:], AF.Sqrt, bias=self.eps_t[:, 0:1], scale=1.0 / D), [pb, self.eps_t], [r])
                k.op("dve", lambda e: e.reciprocal(r[:], r[:]), [r], [r])
                for c in range(8):
                    k.op("dve", lambda e: e.scalar_tensor_tensor(y[:, c, :], x[:, c, :], self.fg[:, c:c + 1], r[:], ALU.mult, ALU.mult),
                         [x, self.fg, r], [y])
                k.dma("sp", yT_d[:, :, t * 512:(t + 1) * 512], y[:], reads=[y], is_output=True)

    def declare_mixer_io(self):
        I, O = self.I, self.O
        I("ident", [128, 128])
        I("gm_wu", [128, 8192]); I("gm_wv", [128, 8192]); I("gm_wo", [128, 8192])
        I("gm_wsT", [128, 1024]); I("gm_lng", [1, 1024]); I("gm_lnb", [1, 1024]); I("gm_bs", [1, 1024])
        I("nat_wqkv", [8, 128, 3072]); I("nat_wo", [128, 8192])
        I("nat_tbT", [8, 128, 960]); I("nat_maskT", [128, 64])
        I("nat_ckT", [8, 128, 256]); I("nat_cv", [8, 128, 256])
        O("kc_out", [128, 8, 1024]); O("vc_out", [128, 8, 8, 128])
        I("gla_win", [4, 128, 8, 1024]); I("gla_wo", [4, 128, 2, 1024]); I("gla_wa1", [128, 8, 32]); I("gla_wa2", [33, 4, 256])
        I("gla_tri", [128, 4, 128]); I("gla_msk", [128, 2, 128]); I("gla_cos", [128, 1024]); I("gla_sin", [128, 1024])
        I("gla_ng", [128, 8]); I("gla_s0", [128, 2, 4, 256])
        O("gla_out", [128, 4, 2, 4, 256])
        I("ssd_wx", [4, 128, 8, 512]); I("ssd_wz", [4, 128, 8, 512]); I("ssd_wbc", [4, 128, 8, 256]); I("ssd_wdt", [128, 8, 64])
        I("ssd_wo", [128, 16, 1024]); I("ssd_tri", [128, 4, 128]); I("ssd_cw", [128, 24, 3]); I("ssd_cb", [128, 24])
        I("ssd_dtb", [1, 64]); I("ssd_alog", [1, 64]); I("ssd_dsk", [1, 32]); I("ssd_ng", [128, 16]); I("ssd_s0", [128, 2, 32, 64])
        O("ssd_out", [128, 4, 2, 32, 64])

    def alloc_mixer_persistent(self):
        k = self.k
        self.eps_t = k.sb("eps_t", [128, 1])
        k.op("pool", lambda e: e.memset(self.eps_t[:], EPS), [], [self.eps_t])
        self.ident_bf = k.sb("ident_bf", [128, 128], BF16)
        self.ident_f = k.sb("ident_f", [128, 128], F32)

    def load_mixer_consts(self):
        k = self.k
        k.dma("pool", self.ident_bf[:], self.din["ident"], writes=[self.ident_bf])
        k.dma("sp", self.ident_f[:], self.din["ident"], writes=[self.ident_f])

    def mixer(self, l):
        k = self.k
        k.barrier()
        with ExitStack() as ph:
            if l == 3:
                self.mix_ssd(l, None, ph)
            else:
                hT = [k.sb("mhT%d" % t, [128, 8, 512], BF16, es=ph) for t in range(TT)]
                with ExitStack() as sub:
                    self.norm_mod(l, 1, hT, sub)
                    k.barrier()
                [self.mix_gla, self.mix_nat, self.mix_gmlp][l](l, hT, ph)
            k.barrier()

    def out_proj(self, l, Wo, srcT, only=None):
        k = self.k
        Gm = self.Gm[l]
        for dc in range(8):
            for t in (range(TT) if only is None else [only]):
                j = 0 if t < 2 else 1
                py = self.psum()
                k.mm(py, [(py[:], Wo[:, q * 1024 + dc * 128:q * 1024 + dc * 128 + 128], srcT[t][:, q, :], q == 0, q == 7)
                          for q in range(8)], [Wo, srcT[t]])
                x = self.x[t]
                k.op("dve", lambda e: e.scalar_tensor_tensor(x[:, dc, :], py[:], Gm[:, 1, dc, j:j + 1], x[:, dc, :], ALU.mult, ALU.add),
                     [py, Gm, x], [x])

    def gelu_tanh(self, out_ap, out_buf, p, tmps, n):
        k = self.k
        t1, t3 = tmps
        k.op("act", lambda e: e.activation(t1[:, 0:n], p[:, 0:n], AF.Square, scale=0.044715 ** 0.5), [p], [t1])
        k.op("dve", lambda e: e.scalar_tensor_tensor(t3[:, 0:n], t1[:, 0:n], 1.0, p[:, 0:n], ALU.add, ALU.mult), [t1, p], [t3])
        k.op("act", lambda e: e.activation(t3[:, 0:n], t3[:, 0:n], AF.Sigmoid, scale=1.5957691216), [t3], [t3])
        k.op("dve", lambda e: e.tensor_tensor(out_ap, t3[:, 0:n], p[:, 0:n], ALU.mult), [t3, p], [out_buf])

    def mix_gmlp(self, l, hT, ph):
        k = self.k
        d = self.din
        Wu = k.sb("gWu", [128, 8192], BF16, es=ph)
        Wv = k.sb("gWv", [128, 8192], BF16, es=ph)
        Wo = k.sb("gWo", [128, 8192], BF16, es=ph)
        wsT = k.sb("gwsT", [128, 1024], BF16, es=ph)
        lng = k.sb("glng", [128, 1024], F32, es=ph)
        lnb = k.sb("glnb", [128, 1024], F32, es=ph)
        bsb = k.sb("gbsb", [128, 1024], F32, es=ph)
        for (w, nm) in ((Wu, "gm_wu"), (Wv, "gm_wv"), (Wo, "gm_wo")):
            for h in range(2):
                k.dma("pool", w[:, h * 4096:(h + 1) * 4096], d[nm][:, h * 4096:(h + 1) * 4096], writes=[w])
        k.dma("pool", wsT[:], d["gm_wsT"], writes=[wsT])
        k.dma("sp", lng[:], d["gm_lng"].partition_broadcast(128), writes=[lng])
        k.dma("sp", lnb[:], d["gm_lnb"].partition_broadcast(128), writes=[lnb])
        k.dma("sp", bsb[:], d["gm_bs"].partition_broadcast(128), writes=[bsb])
        uT = k.sb("guT", [128, 8, 512], BF16, es=ph)
        gm2 = [k.sb("ggmT%d" % t, [128, 8, 512], BF16, es=ph) for t in range(1)]
        gmT = [gm2[0], gm2[0], gm2[0], gm2[0]]
        vt = k.sb("gvt", [128, 1024], F32, es=ph)
        vsq = k.sb("gvsq", [128, 1024], F32, es=ph)
        vnb = k.sb("gvnb", [128, 4, 1024], BF16, es=ph)
        st = k.sb("gst", [128, 8], F32, es=ph)
        tmpa = [(k.sb("gt1%d" % i, [128, 512], F32, es=ph), k.sb("gt3%d" % i, [128, 512], F32, es=ph)) for i in range(2)]
        zz = [k.sb("gzz%d" % i, [128, 512], F32, es=ph) for i in range(1)] * 2
        ng = 0
        for t in range(TT):
            for fc in range(8):
                pu = self.psum()
                k.mm(pu, [(pu[:], Wu[:, kc * 1024 + fc * 128:kc * 1024 + fc * 128 + 128], hT[t][:, kc, :], kc == 0, kc == 7) for kc in range(8)], [Wu, hT[t]])
                self.gelu_tanh(uT[:, fc, :], uT, pu, tmpa[ng % 2], 512)
                ng += 1
            for q4 in range(4):
                for half in range(2):
                    pv = self.psum()
                    k.mm(pv, [(pv[:], hT[t][:, kc, q4 * 128:(q4 + 1) * 128], Wv[:, kc * 1024 + half * 512:kc * 1024 + half * 512 + 512], kc == 0, kc == 7)
                              for kc in range(8)], [Wv, hT[t]])
                    self.gelu_tanh(vt[:, half * 512:(half + 1) * 512], vt, pv, tmpa[ng % 2], 512)
                    ng += 1
                k.op("dve", lambda e: e.tensor_reduce(st[:, 0:1], vt[:], AX.X, ALU.add), [vt], [st])
                k.op("pool", lambda e: e.tensor_tensor(vsq[:], vt[:], vt[:], ALU.mult), [vt], [vsq])
                k.op("dve", lambda e: e.tensor_reduce(st[:, 1:2], vsq[:], AX.X, ALU.add), [vsq], [st])
                k.op("dve", lambda e: e.tensor_scalar(st[:, 2:3], st[:, 0:1], 1.0 / 1024, None, ALU.mult), [st], [st])
                k.op("dve", lambda e: e.tensor_tensor(st[:, 3:4], st[:, 2:3], st[:, 2:3], ALU.mult), [st], [st])
                k.op("dve", lambda e: e.scalar_tensor_tensor(st[:, 4:5], st[:, 1:2], 1.0 / 1024, st[:, 3:4], ALU.mult, ALU.subtract), [st], [st])
                k.op("act", lambda e: e.activation(st[:, 5:6], st[:, 4:5], AF.Sqrt, bias=self.eps_t[:, 0:1]), [st, self.eps_t], [st])
                k.op("dve", lambda e: e.reciprocal(st[:, 6:7], st[:, 5:6]), [st], [st])
                k.op("dve", lambda e: e.tensor_scalar(vsq[:], vt[:], st[:, 2:3], st[:, 6:7], ALU.subtract, ALU.mult), [vt, st], [vsq])
                k.op("pool", lambda e: e.tensor_tensor(vsq[:], vsq[:], lng[:], ALU.mult), [vsq, lng], [vsq])
                k.op("pool", lambda e: e.tensor_tensor(vnb[:, q4, :], vsq[:], lnb[:], ALU.add), [vsq, lnb], [vnb])
            for g in range(8):
                pz = self.psum()
                for q4 in range(4):
                    k.mm(pz, [(pz[:, q4 * 128:(q4 + 1) * 128], vnb[:, q4, g * 128:(g + 1) * 128], wsT[:, g * 128:(g + 1) * 128], True, True)], [vnb, wsT])
                z = zz[g % 2]
                bb = bass.AP(bsb.t, g * 128, [[1024, 128], [0, 4], [1, 128]])
                k.op("dve", lambda e: e.tensor_tensor(z[:].rearrange("p (a b) -> p a b", a=4), pz[:].rearrange("p (a b) -> p a b", a=4), bb, ALU.add), [pz, bsb], [z])
                k.op("dve", lambda e: e.tensor_tensor(gmT[t][:, g, :], z[:], uT[:, g, :], ALU.mult), [z, uT], [gmT[t]])
            self.out_proj(l, Wo, gmT, only=t)

    def mix_nat(self, l, hT, ph):
        k = self.k
        d = self.din
        Wo = k.sb("nWo", [128, 8192], BF16, es=ph)
        for h in range(2):
            k.dma("pool", Wo[:, h * 4096:(h + 1) * 4096], d["nat_wo"][:, h * 4096:(h + 1) * 4096], writes=[Wo])
        oT = [k.sb("noT%d" % t, [128, 8, 512], BF16, es=ph) for t in range(TT)]
        Wp = [k.sb("nWp%d" % i, [128, 3072], BF16, es=ph) for i in range(2)]
        maskT = k.sb("nmask", [128, 64], F32, es=ph)
        if not (NATV & 64):
            k.dma("sp", maskT[:], d["nat_maskT"], writes=[maskT])
        tbr = [k.sb("ntbr%d" % i, [128, 960], F32, es=ph) for i in range(1)] * 2
        tbT = [k.sb("ntbT%d" % i, [128, 15, 64], BF16, es=ph) for i in range(2)]
        v2 = k.sb("nv2", [128, 7, 128], BF16, es=ph)
        ckT = [k.sb("nckT%d" % i, [128, 256], BF16, es=ph) for i in range(2)]
        cv = [k.sb("ncv%d" % i, [128, 256], BF16, es=ph) for i in range(2)]
        qT = [k.sb("nqT%d" % i, [128, 1024], BF16, es=ph) for i in range(2)]
        kT = [k.sb("nkT%d" % i, [128, 1024], BF16, es=ph) for i in range(2)]
        vtk = [k.sb("nvt%d" % i, [128, 8, 128], BF16, es=ph) for i in range(2)]
        k32 = [k.sb("nk32%d" % i, [128, 256], F32, es=ph) for i in range(2)]
        v32 = [k.sb("nv32%d" % i, [128, 2, 128], F32, es=ph) for i in range(2)]
        PT = [k.sb("nPT%d" % i, [128, 512], BF16, es=ph) for i in range(4)]
        rdl = [k.sb("nrdl%d" % i, [128, 64], F32, es=ph) for i in range(4)]
        rd = [k.sb("nrd%d" % i, [128, 256], F32, es=ph) for i in range(2)]
        kc_out, vc_out = self.dout["kc_out"], self.dout["vc_out"]
        nseq = 0
        npt = 0
        for pr in range(8):
            W = Wp[pr % 2]
            k.dma("pool", W[:], d["nat_wqkv"][pr], writes=[W])
            tr_, tb = tbr[pr % 2], tbT[pr % 2]
            if not (NATV & 64):
                k.dma("sp", tr_[:], d["nat_tbT"][pr], writes=[tr_])
            mb = bass.AP(maskT.t, 0, [[64, 128], [0, 15], [1, 64]])
            if not (NATV & 16):
                k.op("dve", lambda e: e.tensor_tensor(tb[:], tr_[:].rearrange("p (b c) -> p b c", b=15), mb, ALU.add), [tr_, maskT], [tb])
            ck, cvv = ckT[pr % 2], cv[pr % 2]
            if not (NATV & 128):
                k.dma("pool", ck[:], d["nat_ckT"][pr], writes=[ck])
                k.dma("pool", cvv[:], d["nat_cv"][pr], writes=[cvv])
            for sq in range(5):
                if NATV & 32:
                    continue
                T = 256 if sq < 4 else 1024
                t0 = sq * 256
                q_, k_, v_ = qT[nseq % 2], kT[nseq % 2], vtk[nseq % 2]
                kk32, vv32 = k32[nseq % 2], v32[nseq % 2]
                nseq += 1
                for c0 in range(0, T, 512):
                    n = min(512, T - c0)
                    tt, off = (t0 + c0) // 512, (t0 + c0) % 512
                    pq = self.psum()
                    if not (NATV & 256):
                        k.mm(pq, [(pq[:, 0:n], W[:, kc * 128:(kc + 1) * 128], hT[tt][:, kc, off:off + n], kc == 0, kc == 7) for kc in range(8)], [W, hT[tt]])
                        k.op("act", lambda e: e.mul(q_[:, c0:c0 + n], pq[:, 0:n], 0.125), [pq], [q_])
                    if NATV & 512:
                        continue
                    pk = self.psum()
                    k.mm(pk, [(pk[:, 0:n], W[:, 1024 + kc * 128:1024 + (kc + 1) * 128], hT[tt][:, kc, off:off + n], kc == 0, kc == 7) for kc in range(8)], [W, hT[tt]])
                    if not (NATV & 2048):
                        k.op("dve", lambda e: e.tensor_copy(k_[:, c0:c0 + n], pk[:, 0:n]), [pk], [k_])
                    if sq < 4 and not (NATV & 4096):
                        k.op("act", lambda e: e.copy(kk32[:, 0:n], pk[:, 0:n]), [pk], [kk32])
                        if not (NATV & 8):
                            k.dma("sp", kc_out[:, pr, t0:t0 + 256], kk32[:], reads=[kk32], is_output=True)
                for ch in range(T // 128):
                    if NATV & 1024:
                        continue
                    tt, off = (t0 + ch * 128) // 512, (t0 + ch * 128) % 512
                    pv = self.psum()
                    k.mm(pv, [(pv[:, 0:128], hT[tt][:, kc, off:off + 128], W[:, 2048 + kc * 128:2048 + (kc + 1) * 128], kc == 0, kc == 7) for kc in range(8)], [W, hT[tt]])
                    k.op("dve", lambda e: e.tensor_copy(v_[:, ch, :], pv[:, 0:128]), [pv], [v_])
                    if sq < 4:
                        k.op("act", lambda e: e.copy(vv32[:, ch, :], pv[:, 0:128]), [pv], [vv32])
                if sq < 4:
                    if not (NATV & 8):
                        k.dma("sp", vc_out[:, pr, sq * 2:sq * 2 + 2, :], vv32[:], reads=[vv32], is_output=True)
                elif not (NATV & 1):
                    k.dma("sp", v2[0:64, :, :], v_[64:128, 0:7, :], reads=[v_], writes=[v2])
                    k.dma("sp", v2[64:128, :, :], v_[0:64, 1:8, :], reads=[v_], writes=[v2])
                def drive(gens):
                    while gens:
                        for gn in list(gens):
                            try:
                                next(gn)
                            except StopIteration:
                                gens.remove(gn)

                def unit_ctx(hh, ci_):
                    hb = hh * 64
                    pS = self.psum()
                    for ch in range(2):
                        k.mm(pS, [(pS[:, ch * 256:(ch + 1) * 256], k_[hb:hb + 64, ch * 128:(ch + 1) * 128], q_[hb:hb + 64, 0:256], True, True)], [k_, q_])
                    P = PT[ci_]
                    yield
                    k.op("act", lambda e: e.activation(P[:], pS[:], AF.Exp), [pS], [P])
                    yield
                    po = self.psum()
                    k.mm(po, [(po[:, 0:256], v_[:, ch, :], P[:, ch * 256:(ch + 1) * 256], ch == 0, ch == 1) for ch in range(2)], [v_, P])
                    k.mm(po, [(po[:, 256:512], self.ones_bf[:], P[:, ch * 256:(ch + 1) * 256], ch == 0, ch == 1) for ch in range(2)], [self.ones_bf, P])
                    r_ = rd[ci_ % 2]
                    k.op("dve", lambda e: e.reciprocal(r_[hb:hb + 64, 0:256], po[hb:hb + 64, 256:512]), [po], [r_])
                    tt, off = t0 // 512, t0 % 512
                    k.op("dve", lambda e: e.tensor_tensor(oT[tt][hb:hb + 64, pr, off:off + 256], po[hb:hb + 64, 0:256], r_[hb:hb + 64, 0:256], ALU.mult), [po, r_], [oT[tt]])

                def unit_lat(hh, r, ci_):
                    hb = hh * 64
                    row0 = min(max(r - 4, 0), 8)
                    nch = 6
                    pS = self.psum()
                    qs = q_[hb:hb + 64, r * 64:(r + 1) * 64]
                    for ci in range(4):
                        sl = row0 + 2 * ci - r + 7
                        lb = tb[hb:hb + 64, sl:sl + 2, :].rearrange("p a b -> p (a b)")
                        kc0 = 64 * row0 + 128 * ci
                        k.mm(pS, [(pS[:, ci * 64:(ci + 1) * 64], k_[hb:hb + 64, kc0:kc0 + 128], qs, True, False),
                                  (pS[:, ci * 64:(ci + 1) * 64], lb, self.ident_bf[hb:hb + 64, hb:hb + 64], False, True)], [k_, q_, tb, self.ident_bf])
                    for cj in range(2):
                        ci = 4 + cj
                        k.mm(pS, [(pS[:, ci * 64:(ci + 1) * 64], ck[hb:hb + 64, cj * 128:(cj + 1) * 128], qs, True, True)], [ck, q_])
                    P = PT[ci_]
                    yield
                    k.op("act", lambda e: e.activation(P[:, 0:nch * 64], pS[:, 0:nch * 64], AF.Exp), [pS], [P])
                    yield
                    po = self.psum()
                    if row0 % 2 == 0:
                        vs = [v_[:, row0 // 2 + ci, :] for ci in range(4)]
                    else:
                        vs = [v2[:, (row0 - 1) // 2 + ci, :] for ci in range(4)]
                    vs += [cvv[:, cj * 128:(cj + 1) * 128] for cj in range(2)]
                    k.mm(po, [(po[:, 0:64], vs[ci], P[:, ci * 64:(ci + 1) * 64], ci == 0, ci == nch - 1) for ci in range(nch)], [v_, v2, cvv, P])
                    k.mm(po, [(po[:, 64:128], self.ones_bf[:], P[:, ci * 64:(ci + 1) * 64], ci == 0, ci == nch - 1) for ci in range(nch)], [self.ones_bf, P])
                    r_ = rdl[ci_]
                    k.op("dve", lambda e: e.reciprocal(r_[hb:hb + 64, 0:64], po[hb:hb + 64, 64:128]), [po], [r_])
                    tt, off = (t0 + r * 64) // 512, (t0 + r * 64) % 512
                    k.op("dve", lambda e: e.tensor_tensor(oT[tt][hb:hb + 64, pr, off:off + 64], po[hb:hb + 64, 0:64], r_[hb:hb + 64, 0:64], ALU.mult), [po, r_], [oT[tt]])

                if sq < 4:
                    drive([unit_ctx(0, 0), unit_ctx(1, 1)])
                else:
                    for r2 in range(0, 16, 2):
                        drive([unit_lat(0, r2, 0), unit_lat(1, r2, 1), unit_lat(0, r2 + 1, 2), unit_lat(1, r2 + 1, 3)])
        self.out_proj(l, Wo, oT)

    def mix_gla(self, l, hT, ph):
        k = self.k
        d = self.din
        SC = 128 ** -0.5
        W = k.sb("aW", [128, 8, 1024], BF16, es=ph)
        Wo = k.sb("aWo", [128, 2, 1024], BF16, es=ph)
        wa1 = k.sb("awa1", [128, 8, 32], BF16, es=ph)
        wa2 = k.sb("awa2", [33, 4, 256], BF16, es=ph)
        z1a = k.sb("az1a", [33, NTOK], BF16, es=ph)
        tri = k.sb("atri", [128, 4, 128], F32, es=ph)
        msk = k.sb("amsk", [128, 2, 128], F32, es=ph)
        cos = k.sb("acos", [128, 1024], F32, es=ph)
        sin = k.sb("asin", [128, 1024], F32, es=ph)
        ng = k.sb("ang", [128, 8], F32, es=ph)
        q32 = k.sb("aq32", [128, 1024], F32, es=ph)
        k32 = k.sb("ak32", [128, 1024], F32, es=ph)
        kt32 = k.sb("akt32", [128, 8, 128], F32, es=ph)
        vtk = k.sb("avtk", [128, 8, 256], BF16, es=ph)
        srT = k.sb("asrT", [128, 2, 1024], BF16, es=ph)
        Ltok = k.sb("aLtok", [128, 8, 256], F32, es=ph)
        oF = k.sb("aoF", [128, 2, 1024], F32, es=ph)
        GT = k.sb("aGT", [128, 2, 1024], BF16, es=ph)
        sqn = k.sb("asqn", [128, 2, 512], BF16, es=ph)
        S32s = [k.sb("aS32%d" % i, [128, 256], F32, es=ph) for i in range(2)]
        Sbfs = [k.sb("aSbf%d" % i, [128, 256], BF16, es=ph) for i in range(2)]
        Ebt = [k.sb("aEb%d" % i, [128, 128], F32, es=ph) for i in range(2)]
        Ent = [k.sb("aEn%d" % i, [128, 128], F32, es=ph) for i in range(2)]
        EDt = [k.sb("aED%d" % i, [128, 128], F32, es=ph) for i in range(2)]
        qd = [k.sb("aqd%d" % i, [128, 128], BF16, es=ph) for i in range(2)]
        kd = [k.sb("akd%d" % i, [128, 128], BF16, es=ph) for i in range(2)]
        Am = [k.sb("aAm%d" % i, [128, 128], BF16, es=ph) for i in range(2)]
        kl = [k.sb("akl%d" % i, [128, 128], BF16, es=ph) for i in range(2)]
        t1 = k.sb("at1", [128, 512], F32, es=ph)
        t2 = k.sb("at2", [128, 512], F32, es=ph)
        rsd = k.sb("arsd", [128, 512], F32, es=ph)
        k.dma("pool", wa1[:], d["gla_wa1"], writes=[wa1])
        k.dma("pool", wa2[:], d["gla_wa2"], writes=[wa2])
        k.dma("sp", tri[:], d["gla_tri"], writes=[tri])
        k.dma("sp", msk[:], d["gla_msk"], writes=[msk])
        k.dma("sp", cos[:], d["gla_cos"], writes=[cos])
        k.dma("sp", sin[:], d["gla_sin"], writes=[sin])
        k.dma("sp", ng[:], d["gla_ng"], writes=[ng])
        gla_out = self.dout["gla_out"]
        k.op("pool", lambda e: e.memset(z1a[:], 1.0), [], [z1a])
        for t in range(TT):
            pz = self.psum()
            k.mm(pz, [(pz[0:32, :], wa1[:, kc, :], hT[t][:, kc, :], kc == 0, kc == 7) for kc in range(8)], [wa1, hT[t]])
            k.op("act", lambda e: e.copy(z1a[0:32, t * 512:(t + 1) * 512], pz[0:32, :]), [pz], [z1a])
        Gm = self.Gm[l]
        n2 = 0
        for h in range(4):
            for hf in range(2):
                k.dma("pool", W[:, hf * 4:(hf + 1) * 4, :], d["gla_win"][h, :, hf * 4:(hf + 1) * 4, :], writes=[W])
            k.dma("pool", Wo[:], d["gla_wo"][h], writes=[Wo])
            for sq in range(5):
                T = 256 if sq < 4 else 1024
                t0 = sq * 256
                nch = T // 128
                rope = sq == 4
                for c0 in range(0, T, 512):
                    n = min(512, T - c0)
                    tt, off = (t0 + c0) // 512, (t0 + c0) % 512
                    for (dst, cb) in ((q32, 0), (k32, 128)):
                        pq = self.psum()
                        k.mm(pq, [(pq[:, 0:n], W[:, kc, cb:cb + 128], hT[tt][:, kc, off:off + n], kc == 0, kc == 7) for kc in range(8)], [W, hT[tt]])
                        if rope:
                            ps2 = self.psum()
                            k.mm(ps2, [(ps2[:, 0:n], W[:, kc, 256 + cb:256 + cb + 128], hT[tt][:, kc, off:off + n], kc == 0, kc == 7) for kc in range(8)], [W, hT[tt]])
                            k.op("dve", lambda e: e.tensor_tensor(t1[:, 0:n], pq[:, 0:n], cos[:, c0:c0 + n], ALU.mult), [pq, cos], [t1])
                            k.op("dve", lambda e: e.tensor_tensor(t2[:, 0:n], ps2[:, 0:n], sin[:, c0:c0 + n], ALU.mult), [ps2, sin], [t2])
                            k.op("pool", lambda e: e.tensor_tensor(dst[:, c0:c0 + n], t1[:, 0:n], t2[:, 0:n], ALU.add), [t1, t2], [dst])
                        else:
                            k.op("act", lambda e: e.copy(dst[:, c0:c0 + n], pq[:, 0:n]), [pq], [dst])
                    for dvc in range(2):
                        pr = self.psum()
                        k.mm(pr, [(pr[:, 0:n], W[:, kc, 768 + dvc * 128:768 + (dvc + 1) * 128], hT[tt][:, kc, off:off + n], kc == 0, kc == 7) for kc in range(8)], [W, hT[tt]])
                        k.op("act", lambda e: e.activation(srT[:, dvc, c0:c0 + n], pr[:, 0:n], AF.Silu), [pr], [srT])
                for ch in range(nch):
                    tt, off = (t0 + ch * 128) // 512, (t0 + ch * 128) % 512
                    pv = self.psum()
                    k.mm(pv, [(pv[:, 0:256], hT[tt][:, kc, off:off + 128], W[:, kc, 512:768], kc == 0, kc == 7) for kc in range(8)], [W, hT[tt]])
                    k.op("act", lambda e: e.copy(vtk[:, ch, :], pv[:, 0:256]), [pv], [vtk])
                    pl = self.psum()
                    k.mm(pl, [(pl[:, 0:256], z1a[0:33, t0 + ch * 128:t0 + (ch + 1) * 128], wa2[0:33, h, :], True, True)], [z1a, wa2])
                    k.op("act", lambda e: e.activation(Ltok[:, ch, :], pl[:, 0:256], AF.Exp, scale=-1.0), [pl], [Ltok])
                    k.op("act", lambda e: e.activation(Ltok[:, ch, :], Ltok[:, ch, :], AF.Ln, bias=1.0), [Ltok], [Ltok])
                    pt = self.psum()
                    k.tr(pt, pt[:, 0:128], k32[:, ch * 128:(ch + 1) * 128], self.ident_f[:], [k32, self.ident_f])
                    k.op("dve", lambda e: e.tensor_copy(kt32[:, ch, :], pt[:, 0:128]), [pt], [kt32])
                for e_ in range(2):
                    if rope:
                        k.dma("sp", S32s[e_][:], d["gla_s0"][:, e_, h, :], writes=[S32s[e_]])
                    else:
                        k.op("pool", lambda e: e.memset(S32s[e_][:], 0.0), [], [S32s[e_]])
                    k.op("act", lambda e: e.copy(Sbfs[e_][:], S32s[e_][:]), [S32s[e_]], [Sbfs[e_]])
                def unit(e_, ch, first):
                    S32, Sbf = S32s[e_], Sbfs[e_]
                    gcol = 127 if e_ == 0 else 0
                    i2 = e_
                    cs = slice(ch * 128, (ch + 1) * 128)
                    Lc = Ltok[:, ch, e_ * 128:(e_ + 1) * 128]
                    pb = self.psum()
                    k.mm(pb, [(pb[:, 0:128], Lc, tri[:, 2 * e_, :], True, True)], [Ltok, tri])
                    Eb, En, ED = Ebt[i2], Ent[i2], EDt[i2]
                    pD = self.psum()
                    k.mm(pD, [(pD[:, 0:128], tri[:, 2 * e_ + 1, :], Lc, True, True)], [Ltok, tri])
                    yield
                    k.op("act", lambda e: e.activation(Eb[:], pb[:, 0:128], AF.Exp), [pb], [Eb])
                    k.op("act", lambda e: e.activation(En[:], pb[:, 0:128], AF.Exp, scale=-1.0), [pb], [En])
                    k.op("dve", lambda e: e.scalar_tensor_tensor(qd[i2][:], q32[:, cs], SC, Eb[:], ALU.mult, ALU.mult), [q32, Eb], [qd[i2]])
                    k.op("dve", lambda e: e.tensor_tensor(kd[i2][:], k32[:, cs], En[:], ALU.mult), [k32, En], [kd[i2]])
                    yield
                    pA = self.psum()
                    k.mm(pA, [(pA[:, 0:128], kd[i2][:], qd[i2][:], True, True)], [kd[i2], qd[i2]])
                    k.op("dve", lambda e: e.tensor_tensor(Am[i2][:], pA[:, 0:128], msk[:, e_, :], ALU.mult), [pA, msk], [Am[i2]])
                    k.op("act", lambda e: e.activation(ED[:], pD[:, 0:128], AF.Exp), [pD], [ED])
                    k.op("dve", lambda e: e.tensor_tensor(kl[i2][:], kt32[:, ch, :], ED[:], ALU.mult), [kt32, ED], [kl[i2]])
                    yield
                    po = self.psum()
                    for dvc in range(2):
                        k.mm(po, [(po[:, dvc * 128:(dvc + 1) * 128], vtk[:, ch, dvc * 128:(dvc + 1) * 128], Am[i2][:], True, False),
                                  (po[:, dvc * 128:(dvc + 1) * 128], Sbf[:, dvc * 128:(dvc + 1) * 128], qd[i2][:], False, True)], [vtk, Am[i2], Sbf, qd[i2]])
                    if first:
                        k.op("act", lambda e: e.copy(oF[:, :, cs], po[:, 0:256].rearrange("p (a b) -> p a b", a=2)), [po], [oF])
                    else:
                        k.op("dve", lambda e: e.tensor_tensor(oF[:, :, cs], oF[:, :, cs], po[:, 0:256].rearrange("p (a b) -> p a b", a=2), ALU.add), [po, oF], [oF])
                    yield
                    pU = self.psum()
                    k.mm(pU, [(pU[:, 0:256], kl[i2][:], vtk[:, ch, :], True, True)], [kl[i2], vtk])
                    k.op("dve", lambda e: e.scalar_tensor_tensor(S32[:], S32[:], Eb[:, gcol:gcol + 1], pU[:, 0:256], ALU.mult, ALU.add), [S32, Eb, pU], [S32])
                    k.op("act", lambda e: e.copy(Sbf[:], S32[:]), [S32], [Sbf])

                written = set()
                for step in range(nch):
                    gens = []
                    for e_ in range(2):
                        ch = step if e_ == 0 else nch - 1 - step
                        gens.append(unit(e_, ch, ch not in written))
                        written.add(ch)
                    while gens:
                        for gn in list(gens):
                            try:
                                next(gn)
                            except StopIteration:
                                gens.remove(gn)
                if not rope:
                    for e_ in range(2):
                        k.dma("sp", gla_out[:, sq, e_, h, :], S32s[e_][:], reads=[S32s[e_]], is_output=True)
                for c0 in range(0, T, 512):
                    n = min(512, T - c0)
                    tt, off = (t0 + c0) // 512, (t0 + c0) % 512
                    k.op("act", lambda e: e.activation(sqn[:, :, 0:n], oF[:, :, c0:c0 + n], AF.Square), [oF], [sqn])
                    pn = self.psum()
                    k.mm(pn, [(pn[:, 0:n], self.ones_bf[:], sqn[:, dvc, 0:n], dvc == 0, dvc == 1) for dvc in range(2)], [self.ones_bf, sqn])
                    k.op("act", lambda e: e.activation(rsd[:, 0:n], pn[:, 0:n], AF.Sqrt, bias=self.eps_t[:, 0:1], scale=1.0 / 256), [pn, self.eps_t], [rsd])
                    k.op("dve", lambda e: e.reciprocal(rsd[:, 0:n], rsd[:, 0:n]), [rsd], [rsd])
                    for dvc in range(2):
                        k.op("dve", lambda e: e.scalar_tensor_tensor(t1[:, 0:n], oF[:, dvc, c0:c0 + n], ng[:, h * 2 + dvc:h * 2 + dvc + 1], rsd[:, 0:n], ALU.mult, ALU.mult), [oF, ng, rsd], [t1])
                        k.op("pool", lambda e: e.tensor_tensor(GT[:, dvc, c0:c0 + n], t1[:, 0:n], srT[:, dvc, c0:c0 + n], ALU.mult), [t1, srT], [GT])
                    j = 0 if tt < 2 else 1
                    for dc in range(8):
                        py = self.psum()
                        k.mm(py, [(py[:, 0:n], Wo[:, dvc, dc * 128:(dc + 1) * 128], GT[:, dvc, c0:c0 + n], dvc == 0, dvc == 1) for dvc in range(2)], [Wo, GT])
                        x = self.x[tt]
                        k.op("dve", lambda e: e.scalar_tensor_tensor(x[:, dc, off:off + n], py[:, 0:n], Gm[:, 1, dc, j:j + 1], x[:, dc, off:off + n], ALU.mult, ALU.add),
                             [py, Gm, x], [x])

    def mix_ssd(self, l, hT_unused, ph):
        k = self.k
        d = self.din
        hTb = [k.sb("shT%d" % i, [128, 8, 512], BF16, es=ph) for i in range(2)]
        yT = k.sb("syT", [128, 16, 1024], BF16, es=ph)
        Wp = k.sb("sWp", [128, 8, 512], BF16, es=ph)
        Wo = View(Wp, Wp[:, 0:4, :].rearrange("p a (b c) -> p (a b) c", c=128))
        Wdt = k.sb("sWdt", [128, 8, 64], BF16, es=ph)
        dt_t = k.sb("sdt", [128, 8, 64], F32, es=ph)
        dtA = None
        ecum = k.sb("secum", [128, 8, 64], F32, es=ph)
        edec = k.sb("sedec", [128, 8, 64], F32, es=ph)
        gdec = k.sb("sgdec", [128, 8, 64], F32, es=ph)
        raw = k.sb("sraw", [128, 1024], F32, es=ph)
        dtA = View(raw, raw[:, 0:512].rearrange("p (a b) -> p a b", a=8))
        cv = k.sb("scv", [128, 1024], F32, es=ph)
        xtok = k.sb("sxtok", [128, 8, 512], BF16, es=ph)
        dtx = [k.sb("sdtx%d" % i, [128, 512], BF16, es=ph) for i in range(2)]
        dtxd = [k.sb("sdtxd%d" % i, [128, 512], BF16, es=ph) for i in range(2)]
        BcT = k.sb("sBcT", [128, 1024], BF16, es=ph)
        CcT = k.sb("sCcT", [128, 1024], BF16, es=ph)
        Btok = k.sb("sBtok", [128, 8, 128], BF16, es=ph)
        yacc = k.sb("syacc", [128, 8, 512], F32, es=ph)
        STs = [k.sb("sST32%d" % i, [128, 512], F32, es=ph) for i in range(2)]
        STb = [k.sb("sSTbf%d" % i, [128, 512], BF16, es=ph) for i in range(2)]
        CBm = [k.sb("sCBm%d" % i, [128, 128], F32, es=ph) for i in range(2)]
        lhs = [(k.sb("slhh%d" % i, [128, 8, 128], BF16, es=ph), k.sb("slhl%d" % i, [128, 8, 128], BF16, es=ph)) for i in range(2)]
        dtAh = k.sb("sdtAh", [128, 8, 64], BF16, es=ph)
        dtAl = k.sb("sdtAl", [128, 8, 64], BF16, es=ph)
        trib = k.sb("strib", [128, 4, 128], BF16, es=ph)
        Dec = [k.sb("sDec%d" % i, [128, 512], F32, es=ph) for i in range(2)]
        Mt = [k.sb("sMt%d" % i, [128, 512], BF16, es=ph) for i in range(2)]
        tA = [k.sb("stA%d" % i, [128, 512], F32, es=ph) for i in range(1)] + [View(cv, cv[:, 512:1024]), View(cv, cv[:, 0:512])]
        rbc = raw
        tri = k.sb("stri", [128, 4, 128], F32, es=ph)
        ones_f = k.sb("sones", [128, 128], F32, es=ph)
        cw = k.sb("scw", [128, 24, 3], F32, es=ph)
        cb = k.sb("scb", [128, 24], F32, es=ph)
        dtb = k.sb("sdtb", [128, 64], F32, es=ph)
        abc = k.sb("sabc", [128, 64], F32, es=ph)
        dsk = k.sb("sdsk", [128, 32], F32, es=ph)
        ng = k.sb("sng", [128, 16], F32, es=ph)
        ssqp = k.sb("sssqp", [128, 8, 4], F32, es=ph)
        ssq = k.sb("sssq", [128, 8], F32, es=ph)
        k.dma("pool", Wdt[:], d["ssd_wdt"], writes=[Wdt])
        k.dma("sp", tri[:], d["ssd_tri"], writes=[tri])
        k.dma("pool", trib[:], d["ssd_tri"], writes=[trib])
        k.dma("sp", cw[:], d["ssd_cw"], writes=[cw])
        k.dma("sp", cb[:], d["ssd_cb"], writes=[cb])
        k.dma("sp", dtb[:], d["ssd_dtb"].partition_broadcast(128), writes=[dtb])
        k.dma("sp", abc[:], d["ssd_alog"].partition_broadcast(128), writes=[abc])
        k.dma("sp", dsk[:], d["ssd_dsk"].partition_broadcast(128), writes=[dsk])
        k.dma("sp", ng[:], d["ssd_ng"], writes=[ng])
        k.op("pool", lambda e: e.memset(ones_f[:], 1.0), [], [ones_f])
        k.op("act", lambda e: e.activation(abc[:], abc[:], AF.Exp), [abc], [abc])
        k.op("dve", lambda e: e.tensor_scalar(abc[:], abc[:], -1.0, None, ALU.mult), [abc], [abc])
        ssd_out = self.dout["ssd_out"]
        Gm = self.Gm[l]
        bf_view = yacc.t.bitcast(BF16)
        n3 = 0
        ntr = 0

        def bc8(buf, c, col0):
            return bass.AP(buf.t, c * 64 + col0, [[8 * 64, 128], [1, 8], [0, 64]])

        for blk in range(2):
            tiles = [2 * blk, 2 * blk + 1]
            seqs = [(i * 256, 256) for i in range(4)] if blk == 0 else [(0, 1024)]
            ns = len(seqs)
            temps = ([(yacc, bf_view[:, :, 0:512])], [(raw, raw[:, 0:512])], [(raw, raw[:, 512:1024]), (cv, cv[:, 0:512]), (cv, cv[:, 512:1024])])
            self.norm_mod(l, 1, hTb, ph, tiles=tiles, temps=temps)
            for c in range(8):
                hsl = hTb[c // 4]
                o_ = (c % 4) * 128
                pd = self.psum()
                k.mm(pd, [(pd[:, 0:64], hsl[:, kc, o_:o_ + 128], Wdt[:, kc, :], kc == 0, kc == 7) for kc in range(8)], [hsl, Wdt])
                k.op("dve", lambda e: e.tensor_tensor(dt_t[:, c, :], pd[:, 0:64], dtb[:], ALU.add), [pd, dtb], [dt_t])
                k.op("act", lambda e: e.activation(dt_t[:, c, :], dt_t[:, c, :], AF.Exp), [dt_t], [dt_t])
                k.op("act", lambda e: e.activation(dt_t[:, c, :], dt_t[:, c, :], AF.Ln, bias=1.0), [dt_t], [dt_t])
                k.op("dve", lambda e: e.tensor_tensor(dtA[:, c, :], dt_t[:, c, :], abc[:], ALU.mult), [dt_t, abc], [dtA])
                k.op("act", lambda e: e.copy(dtAh[:, c, :], dtA[:, c, :]), [dtA], [dtAh])
                k.op("dve", lambda e: e.tensor_tensor(dtAl[:, c, :], dtA[:, c, :], dtAh[:, c, :], ALU.subtract), [dtA, dtAh], [dtAl])
                pc = self.psum()
                k.mm(pc, [(pc[:, 0:32], tri[:, 0, :], dtA[:, c, 0:32], True, True)], [tri, dtA])
                k.mm(pc, [(pc[:, 32:64], tri[:, 1, :], dtA[:, c, 32:64], True, True)], [tri, dtA])
                k.op("act", lambda e: e.activation(ecum[:, c, :], pc[:, 0:64], AF.Exp), [pc], [ecum])
                pe_ = self.psum()
                k.mm(pe_, [(pe_[:, 0:32], tri[:, 2, :], dtA[:, c, 0:32], True, True)], [tri, dtA])
                k.mm(pe_, [(pe_[:, 32:64], tri[:, 3, :], dtA[:, c, 32:64], True, True)], [tri, dtA])
                k.op("act", lambda e: e.activation(edec[:, c, :], pe_[:, 0:64], AF.Exp), [pe_], [edec])
                pg = self.psum()
                k.mm(pg, [(pg[:, 0:64], ones_f[:], dtA[:, c, :], True, True)], [ones_f, dtA])
                k.op("act", lambda e: e.activation(gdec[:, c, :], pg[:, 0:64], AF.Exp), [pg], [gdec])

            def conv_chunk(Wt, col0, cidx, dst):
                pps = []
                for ti in range(2):
                    pp = self.psum()
                    k.mm(pp, [(pp[:], Wt[:, kc, col0:col0 + 128], hTb[ti][:, kc, :], kc == 0, kc == 7) for kc in range(8)], [Wt, hTb[ti]])
                    pps.append(pp)
                    k.op("act", lambda e: e.activation(dst[:, ti * 512:(ti + 1) * 512], pp[:], AF.Identity, bias=cb[:, cidx:cidx + 1], scale=cw[:, cidx, 1:2]),
                         [pp, cw, cb], [dst])
                nseg = ns // 2 if ns > 1 else 1
                T = 512 // nseg
                for ti in range(2):
                    pp = pps[ti]
                    d3 = dst[:, ti * 512:(ti + 1) * 512].rearrange("p (a b) -> p a b", a=nseg)
                    p3 = pp[:].rearrange("p (a b) -> p a b", a=nseg)
                    k.op("dve", lambda e: e.scalar_tensor_tensor(d3[:, :, 1:T], p3[:, :, 0:T - 1], cw[:, cidx, 0:1], d3[:, :, 1:T], ALU.mult, ALU.add), [pp, cw, dst], [dst])
                    k.op("dve", lambda e: e.scalar_tensor_tensor(d3[:, :, 0:T - 1], p3[:, :, 1:T], cw[:, cidx, 2:3], d3[:, :, 0:T - 1], ALU.mult, ALU.add), [pp, cw, dst], [dst])
                if ns == 1:
                    k.op("dve", lambda e: e.scalar_tensor_tensor(dst[:, 512:513], pps[0][:, 511:512], cw[:, cidx, 0:1], dst[:, 512:513], ALU.mult, ALU.add), [pps[0], cw, dst], [dst])
                    k.op("dve", lambda e: e.scalar_tensor_tensor(dst[:, 511:512], pps[1][:, 0:1], cw[:, cidx, 2:3], dst[:, 511:512], ALU.mult, ALU.add), [pps[1], cw, dst], [dst])
                k.op("act", lambda e: e.activation(dst[:], dst[:], AF.Silu), [dst], [dst])

            for g in range(4):
                for hf in range(2):
                    k.dma("pool", Wp[:, hf * 4:(hf + 1) * 4, :], d["ssd_wx"][g, :, hf * 4:(hf + 1) * 4, :], writes=[Wp])
                for xc in range(4):
                    cvb = (raw, cv)[xc % 2]
                    conv_chunk(Wp, xc * 128, 4 * g + xc, cvb)
                    for c in range(8):
                        pt = self.psum()
                        k.tr(pt, pt[:, 0:128], cvb[:, c * 128:(c + 1) * 128], self.ident_f[:], [cvb, self.ident_f])
                        ntr += 1
                        if ntr % 2:
                            k.op("act", lambda e: e.copy(xtok[:, c, xc * 128:(xc + 1) * 128], pt[:, 0:128]), [pt], [xtok])
                        else:
                            k.op("dve", lambda e: e.tensor_copy(xtok[:, c, xc * 128:(xc + 1) * 128], pt[:, 0:128]), [pt], [xtok])
                k.dma("pool", Wp[:, :, 0:256], d["ssd_wbc"][g], writes=[Wp])
                conv_chunk(Wp, 0, 16 + g, raw)
                k.op("pool", lambda e: e.tensor_copy(BcT[:], raw[:]), [raw], [BcT])
                for c in range(8):
                    pt = self.psum()
                    k.tr(pt, pt[:, 0:128], raw[:, c * 128:(c + 1) * 128], self.ident_f[:], [raw, self.ident_f])
                    k.op("act", lambda e: e.copy(Btok[:, c, :], pt[:, 0:128]), [pt], [Btok])
                conv_chunk(Wp, 128, 20 + g, cv)
                k.op("pool", lambda e: e.tensor_copy(CcT[:], cv[:]), [cv], [CcT])
                for hf in range(2):
                    k.dma("pool", Wp[:, hf * 4:(hf + 1) * 4, :], d["ssd_wz"][g, :, hf * 4:(hf + 1) * 4, :], writes=[Wp])
                for si, (s0, T) in enumerate(seqs):
                    nch = T // 128
                    c_first = s0 // 128
                    for e_ in range(2):
                        if blk == 1:
                            k.dma("sp", STs[e_][:].rearrange("p (a b) -> p a b", a=8), d["ssd_s0"][:, e_, 8 * g:8 * g + 8, :], writes=[STs[e_]])
                        else:
                            k.op("pool", lambda e: e.memset(STs[e_][:], 0.0), [], [STs[e_]])
                        k.op("act", lambda e: e.copy(STb[e_][:], STs[e_][:]), [STs[e_]], [STb[e_]])
                    def unit(e_, c, first, u):
                        ST32, STbf = STs[e_], STb[e_]
                        hc0 = e_ * 32 + 8 * g
                        cs = slice(c * 128, (c + 1) * 128)
                        hc0 = e_ * 32 + 8 * g
                        pcb = self.psum()
                        k.mm(pcb, [(pcb[:, 0:128], BcT[:, cs], CcT[:, cs], True, True)], [BcT, CcT])
                        cbm = CBm[e_]
                        yield
                        k.op("dve", lambda e: e.tensor_tensor(cbm[:], pcb[:, 0:128], tri[:, e_, :], ALU.mult), [pcb, tri], [cbm])
                        dx = dtx[e_]
                        k.op("pool", lambda e: e.tensor_tensor(dx[:].rearrange("p (a b) -> p a b", a=8), xtok[:, c, :].rearrange("p (a b) -> p a b", a=8),
                                                               bc8(dt_t, c, hc0), ALU.mult), [xtok, dt_t], [dx])
                        pyd = self.pyd[e_]
                        tri_b = bass.AP(trib.t, e_ * 128, [[512, 128], [0, 8], [1, 128]])
                        lhh, lhl = lhs[e_]
                        k.op("dve", lambda e: e.tensor_tensor(lhh[:], tri_b, bass.AP(dtAh.t, c * 64 + hc0, [[512, 128], [1, 8], [0, 128]]), ALU.mult), [trib, dtAh], [lhh])
                        k.op("dve", lambda e: e.tensor_tensor(lhl[:], tri_b, bass.AP(dtAl.t, c * 64 + hc0, [[512, 128], [1, 8], [0, 128]]), ALU.mult), [trib, dtAl], [lhl])
                        cbm_b = bass.AP(cbm.t, 0, [[128, 128], [0, 4], [1, 128]])
                        for half in range(2):
                            i3 = e_
                            psg = self.psum()
                            mms = [(psg[:], trib[:, 2 + e_, :], lhh[:, half * 4:(half + 1) * 4, :].rearrange("p a b -> p (a b)"), True, False),
                                   (psg[:], trib[:, 2 + e_, :], lhl[:, half * 4:(half + 1) * 4, :].rearrange("p a b -> p (a b)"), False, True)]
                            yield
                            k.mm(psg, mms, [lhh, lhl, trib])
                            yield
                            k.op("act", lambda e: e.activation(Dec[i3][:], psg[:], AF.Exp), [psg], [Dec[i3]])
                            k.op("dve", lambda e: e.tensor_tensor(Mt[i3][:].rearrange("p (a b) -> p a b", a=4), Dec[i3][:].rearrange("p (a b) -> p a b", a=4), cbm_b, ALU.mult),
                                 [Dec[i3], cbm], [Mt[i3]])
                            yield
                            k.mm(pyd, [(pyd[:, (half * 4 + q) * 64:(half * 4 + q + 1) * 64], Mt[i3][:, q * 128:(q + 1) * 128], dx[:, (half * 4 + q) * 64:(half * 4 + q + 1) * 64], True, True)
                                       for q in range(4)], [Mt[i3], dx])
                        yield
                        pyo = self.psum()
                        k.mm(pyo, [(pyo[:], CcT[:, cs], STbf[:], True, True)], [CcT, STbf])
                        t_ = tA[(2 * u) % 3]
                        k.op("dve", lambda e: e.tensor_tensor(t_[:].rearrange("p (a b) -> p a b", a=8), pyo[:].rearrange("p (a b) -> p a b", a=8),
                                                              bc8(ecum, c, hc0), ALU.mult), [pyo, ecum], [t_])
                        if first:
                            k.op("dve", lambda e: e.tensor_tensor(yacc[:, c, :], t_[:], pyd[:], ALU.add), [t_, pyd], [yacc])
                        else:
                            k.op("pool", lambda e: e.tensor_tensor(yacc[:, c, :], yacc[:, c, :], t_[:], ALU.add), [t_, yacc], [yacc])
                            k.op("dve", lambda e: e.tensor_tensor(yacc[:, c, :], yacc[:, c, :], pyd[:], ALU.add), [pyd, yacc], [yacc])
                        dxd = dtxd[e_]
                        k.op("pool", lambda e: e.tensor_tensor(dxd[:].rearrange("p (a b) -> p a b", a=8), dx[:].rearrange("p (a b) -> p a b", a=8),
                                                               bc8(edec, c, hc0), ALU.mult), [dx, edec], [dxd])
                        yield
                        pU = self.psum()
                        k.mm(pU, [(pU[:], Btok[:, c, :], dxd[:], True, True)], [Btok, dxd])
                        t2 = tA[(2 * u + 1) % 3]
                        k.op("dve", lambda e: e.tensor_tensor(t2[:].rearrange("p (a b) -> p a b", a=8), ST32[:].rearrange("p (a b) -> p a b", a=8),
                                                              bc8(gdec, c, hc0), ALU.mult), [ST32, gdec], [t2])
                        k.op("dve", lambda e: e.tensor_tensor(ST32[:], t2[:], pU[:], ALU.add), [t2, pU], [ST32])
                        k.op("act", lambda e: e.copy(STbf[:], ST32[:]), [ST32], [STbf])

                    written = set()
                    for step in range(nch):
                        gens = []
                        for e_ in range(2):
                            c = c_first + step if e_ == 0 else c_first + nch - 1 - step
                            gens.append(unit(e_, c, c not in written, n3))
                            written.add(c)
                            n3 += 1
                        while gens:
                            for gn in list(gens):
                                try:
                                    next(gn)
                                except StopIteration:
                                    gens.remove(gn)
                    if blk == 0:
                        for e_ in range(2):
                            k.dma("sp", ssd_out[:, si, e_, 8 * g:8 * g + 8, :], STs[e_][:].rearrange("p (a b) -> p a b", a=8), reads=[STs[e_]], is_output=True)
                dskb = bass.AP(dsk.t, 8 * g, [[32, 128], [1, 8], [0, 64]])
                tsets = [(tA[0], View(cv, cv[:, 0:512]), View(cv, cv[:, 512:1024])),
                         (Dec[0], View(raw, raw[:, 0:512]), View(raw, raw[:, 512:1024]))]

                def unit_d(c, ts):
                    t_, t2, t3 = ts
                    k.op("pool", lambda e: e.tensor_tensor(t_[:].rearrange("p (a b) -> p a b", a=8), xtok[:, c, :].rearrange("p (a b) -> p a b", a=8), dskb, ALU.mult), [xtok, dsk], [t_])
                    k.op("pool", lambda e: e.tensor_tensor(yacc[:, c, :], yacc[:, c, :], t_[:], ALU.add), [t_, yacc], [yacc])
                    hsl = hTb[c // 4]
                    o_ = (c % 4) * 128
                    pz = self.psum()
                    k.mm(pz, [(pz[:], hsl[:, kc, o_:o_ + 128], Wp[:, kc, :], kc == 0, kc == 7) for kc in range(8)], [hsl, Wp])
                    yield
                    k.op("act", lambda e: e.activation(t2[:], pz[:], AF.Silu), [pz], [t2])
                    k.op("dve", lambda e: e.tensor_tensor(t2[:], t2[:], yacc[:, c, :], ALU.mult), [t2, yacc], [t2])
                    k.op("pool", lambda e: e.tensor_tensor(t3[:], t2[:], t2[:], ALU.mult), [t2], [t3])
                    k.op("dve", lambda e: e.tensor_reduce(ssqp[:, c, g:g + 1], t3[:], AX.X, ALU.add), [t3], [ssqp])
                    yield
                    for q in range(4):
                        pt = self.psum()
                        k.tr(pt, pt[:, 0:128], t2[:, q * 128:(q + 1) * 128], self.ident_f[:], [t2, self.ident_f])
                        fc = 4 * g + q
                        if q % 2:
                            k.op("act", lambda e: e.activation(yT[:, fc, c * 128:(c + 1) * 128], pt[:, 0:128], AF.Identity, scale=ng[:, fc:fc + 1]), [pt, ng], [yT])
                        else:
                            k.op("dve", lambda e: e.tensor_scalar(yT[:, fc, c * 128:(c + 1) * 128], pt[:, 0:128], ng[:, fc:fc + 1], None, ALU.mult), [pt, ng], [yT])
                        yield

                for c2 in range(0, 8, 2):
                    gens = [unit_d(c2, tsets[0]), unit_d(c2 + 1, tsets[1])]
                    while gens:
                        for gn in list(gens):
                            try:
                                next(gn)
                            except StopIteration:
                                gens.remove(gn)
            k.op("dve", lambda e: e.tensor_reduce(ssq[:], ssqp[:], AX.X, ALU.add), [ssqp], [ssq])
            k.op("act", lambda e: e.activation(ssq[:], ssq[:], AF.Sqrt, bias=self.eps_t[:, 0:1], scale=1.0 / 2048), [ssq, self.eps_t], [ssq])
            k.op("dve", lambda e: e.reciprocal(ssq[:], ssq[:]), [ssq], [ssq])
            for c in range(8):
                i3 = c % 3
                k.op("pool", lambda e: e.tensor_scalar(Dec[0][:, i3 * 128:(i3 + 1) * 128], ones_f[:], ssq[:, c:c + 1], None, ALU.mult), [ones_f, ssq], [Dec[0]])
                pr_ = self.psum()
                k.mm(pr_, [(pr_[:, 0:128], Dec[0][:, i3 * 128:(i3 + 1) * 128], self.ident_f[:], True, True)], [Dec[0], self.ident_f])
                k.op("act", lambda e: e.copy(rbc[:, c * 128:(c + 1) * 128], pr_[:, 0:128]), [pr_], [rbc])
            for dp in range(8):
                k.dma("pool", Wo[:], d["ssd_wo"][:, :, dp * 128:(dp + 1) * 128], writes=[Wo])
                for d2 in range(1):
                    dc = dp
                    for ti, t in enumerate(tiles):
                        j = 0 if t < 2 else 1
                        py = self.psum()
                        k.mm(py, [(py[:], Wo[:, kc, d2 * 128:(d2 + 1) * 128], yT[:, kc, ti * 512:(ti + 1) * 512], kc == 0, kc == 15) for kc in range(16)], [Wo, yT])
                        t_ = tA[(dc + ti) % 3]
                        k.op("dve", lambda e: e.tensor_tensor(t_[:], py[:], rbc[:, ti * 512:(ti + 1) * 512], ALU.mult), [py, rbc], [t_])
                        x = self.x[t]
                        k.op("dve", lambda e: e.scalar_tensor_tensor(x[:, dc, :], t_[:], Gm[:, 1, dc, j:j + 1], x[:, dc, :], ALU.mult, ALU.add), [t_, Gm, x], [x])


def host_layout(inp):
    f = lambda a: np.ascontiguousarray(a, dtype=np.float32)
    sh = {}
    w = inp["w_ffn_in"].reshape(8, 8, 128, 2, NFC, 128)
    sh["w1"] = f(w.transpose(0, 4, 2, 3, 1, 5).reshape(8 * NFC, 128, 2048))
    w = inp["w_ffn_out"].reshape(8, NFC, 128, 1024)
    sh["w2"] = f(w.transpose(0, 2, 1, 3).reshape(8, 128, NFC * 1024))
    w = inp["w_ada"].reshape(4, 8, 128, 18, 512)
    sh["wada"] = f(w.transpose(0, 3, 2, 1, 4).reshape(4, 18, 128, 4096))
    sh["bada"] = f(inp["b_ada"].reshape(4, 72, 128).transpose(2, 0, 1))
    sh["ng"] = f(inp["norm_g"].reshape(4, 3, 8, 128).transpose(3, 0, 1, 2))
    sh["fg"] = f(inp["final_g"].reshape(8, 128).T)
    sh["ident"] = np.eye(128, dtype=np.float32)
    kcm = lambda w: f(w.reshape(8, 128, -1).transpose(1, 0, 2).reshape(128, -1))
    sh["gm_wu"] = kcm(inp["gm_w_in"][0][:, :1024]); sh["gm_wv"] = kcm(inp["gm_w_in"][0][:, 1024:]); sh["gm_wo"] = kcm(inp["gm_w_out"][0])
    sh["gm_wsT"] = f(inp["gm_w_s"][0].transpose(2, 0, 1).reshape(128, 1024))
    sh["gm_lng"] = f(inp["gm_ln_g"][0].reshape(1, 1024)); sh["gm_lnb"] = f(inp["gm_ln_b"][0].reshape(1, 1024))
    sh["gm_bs"] = f(inp["gm_b_s"][0].reshape(1, 1024))
    w = inp["nat_w_qkv"][0].reshape(8, 128, 3, 8, 128)
    sh["nat_wqkv"] = f(w.transpose(3, 1, 2, 0, 4).reshape(8, 128, 3072))
    sh["nat_wo"] = kcm(inp["nat_w_out"][0])
    kc_ = np.arange(64)[:, None]; qc_ = np.arange(64)[None, :]
    idx = np.clip(kc_ - qc_ + 15, 0, 30)
    tb = inp["nat_rpb"][0][:, :, idx]
    tb = tb.reshape(8, 2, 15, 64, 64).transpose(0, 1, 4, 2, 3)
    sh["nat_tbT"] = f(tb.reshape(8, 128, 960))
    cs = np.clip(np.arange(64) - 8, 0, 48)[None, :]
    valid = (kc_ >= cs) & (kc_ < cs + 16)
    mk = np.where(valid.T, 0.0, -30000.0).astype(np.float32)
    sh["nat_maskT"] = np.ascontiguousarray(np.concatenate([mk, mk], 0))
    wi = inp["gla_w_in"][0]
    dk = np.arange(128)
    partner = np.where(dk % 64 < 32, dk + 32, dk - 32)
    gw = np.zeros((4, 1024, 1024), np.float32)
    for h in range(4):
        q = wi[:, h * 128:(h + 1) * 128]; kk = wi[:, 512 + h * 128:512 + (h + 1) * 128]
        gw[h, :, 0:128] = q; gw[h, :, 128:256] = kk; gw[h, :, 256:384] = q[:, partner]; gw[h, :, 384:512] = kk[:, partner]
        gw[h, :, 512:768] = wi[:, 1024 + h * 256:1024 + (h + 1) * 256]
        gw[h, :, 768:1024] = wi[:, 2048 + h * 256:2048 + (h + 1) * 256]
    sh["gla_win"] = f(gw.reshape(4, 8, 128, 1024).transpose(0, 2, 1, 3))
    sh["gla_wo"] = f(inp["gla_w_out"][0].reshape(4, 2, 128, 1024).transpose(0, 2, 1, 3))
    sh["gla_wa1"] = f(inp["gla_w_a1"][0].transpose(1, 0, 2).reshape(8, 128, 32).transpose(1, 0, 2))
    a2 = np.zeros((33, 4, 2, 128), np.float32)
    for e in range(2):
        a2[e * 16:(e + 1) * 16, :, e, :] = inp["gla_w_a2"][0][e].reshape(16, 4, 128)
        a2[32, :, e, :] = inp["gla_b_a"][0][e].reshape(4, 128)
    sh["gla_wa2"] = f(a2.reshape(33, 4, 256))
    jj = np.arange(128)[:, None]; ii = np.arange(128)[None, :]
    c16 = -1.0 / 16.0
    tri = np.stack([(jj <= ii), (jj > ii), (jj >= ii), (jj < ii)], 1).astype(np.float32) * c16
    sh["gla_tri"] = f(tri)
    sh["gla_msk"] = f(np.stack([(jj <= ii), (jj >= ii)], 1).astype(np.float32))
    tpos = np.arange(1024)
    inv = 10000.0 ** (-np.arange(0, 64, 2, dtype=np.float64) / 64.0)
    pos = np.where((dk < 64)[:, None], (tpos // 64)[None, :], (tpos % 64)[None, :]).astype(np.float64)
    ang = pos * inv[dk % 32][:, None]
    sgn = np.where(dk % 64 < 32, -1.0, 1.0)[:, None]
    sh["gla_cos"] = f(np.cos(ang)); sh["gla_sin"] = f(np.sin(ang) * sgn)
    sh["gla_ng"] = f(inp["gla_norm_g"][0].reshape(8, 128).T)
    wi = inp["ssd_w_in"][0]
    kc3 = lambda w: f(w.reshape(8, 128, -1).transpose(1, 0, 2))
    sh["ssd_wz"] = f(np.stack([kc3(wi[:, 512 * g:512 * (g + 1)]) for g in range(4)]))
    sh["ssd_wx"] = f(np.stack([kc3(wi[:, 2048 + 512 * g:2048 + 512 * (g + 1)]) for g in range(4)]))
    sh["ssd_wbc"] = f(np.stack([kc3(np.concatenate([wi[:, 4096 + 128 * g:4096 + 128 * (g + 1)], wi[:, 4608 + 128 * g:4608 + 128 * (g + 1)]], 1)) for g in range(4)]))
    sh["ssd_wdt"] = kc3(wi[:, 5120:5184])
    sh["ssd_wo"] = f(inp["ssd_w_out"][0].reshape(16, 128, 1024).transpose(1, 0, 2))
    sh["ssd_tri"] = f(np.stack([(jj <= ii), (jj >= ii), (jj > ii), (jj < ii)], 1).astype(np.float32))
    sh["ssd_cw"] = f(inp["ssd_conv_w"][0].reshape(3, 24, 128).transpose(2, 1, 0))
    sh["ssd_cb"] = f(inp["ssd_conv_b"][0].reshape(24, 128).T)
    sh["ssd_dtb"] = f(inp["ssd_dt_bias"][0].reshape(1, 64)); sh["ssd_alog"] = f(inp["ssd_a_log"][0].reshape(1, 64))
    sh["ssd_dsk"] = f(inp["ssd_d"][0].reshape(1, 32)); sh["ssd_ng"] = f(inp["ssd_norm_g"][0].reshape(16, 128).T)
    return sh


def core_inputs(c, inp, sh):
    b = c // 4
    xp = inp["x_prompt"][4 * c:4 * c + 4].reshape(1024, D)
    xs = inp["x_sample"][b]
    xall = np.concatenate([xp, xs], 0)
    m = dict(sh)
    m["xT"] = np.ascontiguousarray(xall.T.reshape(8, 128, NTOK).transpose(1, 0, 2), dtype=np.float32)
    cond = np.stack([inp["c_ctx"], inp["c"][b]], 1)
    m["cond"] = np.ascontiguousarray(cond.reshape(8, 128, 2).transpose(1, 0, 2), dtype=np.float32)
    ck = inp["cache_nat_k"][b, 0].reshape(8, 2, 256, 64)
    m["nat_ckT"] = np.ascontiguousarray(ck.transpose(0, 1, 3, 2).reshape(8, 128, 256), dtype=np.float32)
    m["ssd_s0"] = np.ascontiguousarray(inp["state_ssd"][b, 0].transpose(3, 0, 1, 2), dtype=np.float32)
    m["gla_s0"] = np.ascontiguousarray(inp["state_gla"][b, 0].transpose(2, 0, 1, 3), dtype=np.float32)
    cv = inp["cache_nat_v"][b, 0].reshape(8, 2, 2, 128, 64)
    m["nat_cv"] = np.ascontiguousarray(cv.transpose(0, 3, 2, 1, 4).reshape(8, 128, 256), dtype=np.float32)
    return m


_CACHE = {}


def run(inputs, stop=None, debug=False, start=None, xover=None):
    inp = {k: np.asarray(v) for k, v in inputs.items()}
    key = (stop, debug, start)
    if key not in _CACHE:
        p = Prog(stop=stop, debug=debug, start=start)
        p.build()
        _CACHE[key] = p
    p = _CACHE[key]
    sh = host_layout(inp)
    in_maps = [core_inputs(c, inp, sh) for c in range(NCORES)]
    if xover is not None:
        for c in range(NCORES):
            in_maps[c]["xT"] = xover[c]
    if p.lite:
        for c in range(NCORES):
            in_maps[c]["w1"] = sh["w1"][:1]; in_maps[c]["w2"] = sh["w2"][:1]
            in_maps[c]["wada"] = sh["wada"][start[1]:start[1] + 1]
    res = run_bass_kernel_spmd(p.nc, in_maps, core_ids=list(range(NCORES)))
    return p, res.results


def assemble_y(results):
    yp = np.zeros((32, 256, D), np.float32)
    ys = np.zeros((2, 1024, D), np.float32)
    for c in range(NCORES):
        yT = results[c]["yT"]
        y = yT.transpose(2, 1, 0).reshape(NTOK, D)
        yp[4 * c:4 * c + 4] = y[:1024].reshape(4, 256, D)
        q = c % 4
        ys[c // 4, q * 256:(q + 1) * 256] = y[1024 + q * 256:1024 + (q + 1) * 256]
    return yp, ys


def kernel(**inputs):
    p, results = run(inputs)
    yp, ys = assemble_y(results)
    gla = np.zeros((32, 1, 2, 4, 128, 256), np.float32)
    ck = np.zeros((32, 1, 16, 256, 64), np.float32)
    cvv = np.zeros((32, 1, 16, 256, 64), np.float32)
    ssd = np.zeros((32, 1, 2, 32, 64, 128), np.float32)
    for c in range(NCORES):
        r = results[c]
        gla[4 * c:4 * c + 4, 0] = r["gla_out"].transpose(1, 2, 3, 0, 4)
        kc = r["kc_out"].reshape(2, 64, 8, 4, 256)
        ck[4 * c:4 * c + 4, 0] = kc.transpose(3, 2, 0, 4, 1).reshape(4, 16, 256, 64)
        vc = r["vc_out"].reshape(128, 8, 4, 2, 2, 64)
        cvv[4 * c:4 * c + 4, 0] = vc.transpose(2, 1, 4, 3, 0, 5).reshape(4, 16, 256, 64)
        ssd[4 * c:4 * c + 4, 0] = r["ssd_out"].transpose(1, 2, 3, 4, 0)
    return yp, ys, gla, ck, cvv, ssd
```

```python
import numpy as np
from contextlib import ExitStack
import concourse.bass as bass
import concourse.mybir as mybir
from concourse.bass_utils import run_bass_kernel_spmd

F32 = mybir.dt.float32
BF16 = mybir.dt.bfloat16
AF = mybir.ActivationFunctionType
ALU = mybir.AluOpType
AX = mybir.AxisListType

NCORES = 8
D = 1024
NTOK = 2048
TT = 4
DFF = 2816
NFC = 22
FGROUPS = [(0, 6), (6, 6), (12, 5), (17, 5)]
EPS = 1e-6


LAST_NAMES = []
NATV = 0


class Buf:
    __slots__ = ("t", "lw", "rd", "name", "psum")

    def __init__(self, t, name=""):
        self.psum = False
        self.t = t
        self.lw = None
        self.rd = {}
        self.name = name

    def __getitem__(self, k):
        return self.t[k]


class View:
    def __init__(self, parent, ap):
        self.parent = parent
        self.ap = ap
        self.psum = parent.psum
        self.name = parent.name

    lw = property(lambda self: self.parent.lw, lambda self, v: setattr(self.parent, "lw", v))
    rd = property(lambda self: self.parent.rd, lambda self, v: setattr(self.parent, "rd", v))
    t = property(lambda self: self.ap.tensor)

    def __getitem__(self, k):
        return self.ap[k]


class Eng:
    def __init__(self, name, e, sem):
        self.name = name
        self.e = e
        self.sem = sem
        self.cnt = 0
        self.seen = {}


class K:
    def __init__(self, nc, es):
        self.nc = nc
        self.es = es
        self.sems = {}
        self.engs = {}
        for name, e in (("pe", nc.tensor), ("act", nc.scalar), ("dve", nc.vector),
                        ("pool", nc.gpsimd), ("sp", nc.sync)):
            s = es.enter_context(nc.semaphore("s_" + name))
            self.sems[name] = s
            self.engs[name] = Eng(name, e, s)
        self.dma_sems = []
        self.pools = {}
        for pname, cnt in (("hw", 28), ("sw", 60)):
            lst = []
            for i in range(cnt):
                nm = "%s%d" % (pname, i)
                s = es.enter_context(nc.semaphore(nm))
                self.sems[nm] = s
                ent = [nm, 0]
                lst.append(ent)
                self.dma_sems.append(ent)
            self.pools[pname] = [lst, 0]
        self.n_inst = 0
        self.out_tokens = []
        self.uid = 0

    def sb(self, name, shape, dt=F32, es=None):
        self.uid += 1
        LAST_NAMES.append("%s_%d" % (name, self.uid + 0))
        t = (es or self.es).enter_context(self.nc.sbuf_tensor("%s_%d" % (name, self.uid), list(shape), dt))
        return Buf(t, name)

    def ps(self, name, shape, dt=F32, es=None):
        t = (es or self.es).enter_context(self.nc.psum_tensor(name, list(shape), dt))
        b = Buf(t, name)
        b.psum = True
        return b

    def _wait(self, eng, deps):
        need = {}
        for (k, v) in deps:
            if v > need.get(k, 0):
                need[k] = v
        for k, v in need.items():
            if eng.seen.get(k, 0) >= v:
                continue
            eng.e.wait_ge(self.sems[k], v)
            eng.seen[k] = v
            self.n_inst += 1

    def _deps(self, reads, writes, en=None):
        deps = []
        for b in reads:
            if b.lw is not None:
                deps.append(b.lw)
            if b.psum:
                for k_, v_ in b.rd.items():
                    if k_ != en:
                        deps.append((k_, v_))
        for b in writes:
            if b.lw is not None:
                deps.append(b.lw)
            for k, v in b.rd.items():
                deps.append((k, v))
        return deps

    def _mark(self, tok, reads, writes):
        for b in reads:
            if b.rd.get(tok[0], 0) < tok[1]:
                b.rd[tok[0]] = tok[1]
        for b in writes:
            b.lw = tok
            b.rd = {}

    def op(self, en, fn, reads=(), writes=()):
        eng = self.engs[en]
        self._wait(eng, self._deps(reads, writes, en))
        ins = fn(eng.e)
        eng.cnt += 1
        ins.then_inc(eng.sem, 1)
        tok = (en, eng.cnt)
        self._mark(tok, reads, writes)
        self.n_inst += 1
        return tok

    def mm(self, out_buf, mms, reads):
        eng = self.engs["pe"]
        self._wait(eng, self._deps(reads, [out_buf]))
        n = len(mms)
        for i, m in enumerate(mms):
            ins = eng.e.matmul(m[0], m[1], m[2], start=m[3], stop=m[4])
            self.n_inst += 1
            if i == n - 1:
                eng.cnt += 1
                ins.then_inc(eng.sem, 1)
        tok = ("pe", eng.cnt)
        self._mark(tok, reads, [out_buf])
        return tok

    def tr(self, out_buf, out_ap, in_ap, ident_ap, reads):
        eng = self.engs["pe"]
        self._wait(eng, self._deps(reads, [out_buf]))
        ins = eng.e.transpose(out_ap, in_ap, ident_ap)
        eng.cnt += 1
        ins.then_inc(eng.sem, 1)
        tok = ("pe", eng.cnt)
        self._mark(tok, reads, [out_buf])
        self.n_inst += 1
        return tok

    def dma(self, qn, out_ap, in_ap, reads=(), writes=(), is_output=False):
        eng = self.engs[qn]
        pl = self.pools["sw" if qn == "pool" else "hw"]
        slot = pl[0][pl[1]]
        pl[1] = (pl[1] + 1) % len(pl[0])
        deps = self._deps(reads, writes)
        if slot[1] > 0:
            deps.append((slot[0], slot[1]))
        self._wait(eng, deps)
        ins = eng.e.dma_start(out=out_ap, in_=in_ap)
        slot[1] += 16
        ins.then_inc(self.sems[slot[0]], 16)
        tok = (slot[0], slot[1])
        self._mark(tok, reads, writes)
        self.n_inst += 1
        if is_output:
            self.out_tokens.append(tok)
        return tok

    def all_tokens(self):
        deps = []
        for s in self.dma_sems:
            if s[1] > 0:
                deps.append((s[0], s[1]))
        for n in ("pe", "act", "dve", "pool"):
            if self.engs[n].cnt > 0:
                deps.append((n, self.engs[n].cnt))
        return deps

    def barrier(self):
        deps = self.all_tokens()
        for n in ("pe", "act", "dve", "pool", "sp"):
            self._wait(self.engs[n], deps)

    def finish(self):
        self._wait(self.engs["sp"], self.all_tokens() + list(self.out_tokens))


class Prog:
    def __init__(self, stop=None, debug=False, start=None):
        self.start = start
        self.lite = start is not None and start[0] == "mix" and stop == start
        self.stop = stop
        self.debug = debug
        self.nc = bass.Bass("TRN2", target_bir_lowering=False)
        self.din = {}
        self.dout = {}

    def I(self, name, shape):
        self.din[name] = self.nc.dram_tensor(name, list(shape), F32, kind="ExternalInput").ap()
        return self.din[name]

    def O(self, name, shape):
        self.dout[name] = self.nc.dram_tensor(name, list(shape), F32, kind="ExternalOutput").ap()
        return self.dout[name]

    def psum(self):
        b = self.pbanks[self.prr]
        self.prr = (self.prr + 1) % len(self.pbanks)
        return b

    def build(self):
        nc = self.nc
        xT_d = self.I("xT", [128, 8, NTOK])
        cond_d = self.I("cond", [128, 8, 2])
        w1_d = self.I("w1", [1 if self.lite else 8 * NFC, 128, 2048])
        w2_d = self.I("w2", [1 if self.lite else 8, 128, NFC * 1024])
        wada_d = self.I("wada", [1 if self.lite else 4, 18, 128, 4096])
        bada_d = self.I("bada", [128, 4, 72])
        ng_d = self.I("ng", [128, 4, 3, 8])
        fg_d = self.I("fg", [128, 8])
        yT_d = self.O("yT", [128, 8, NTOK])
        self.declare_mixer_io()

        with ExitStack() as es:
            k = K(nc, es)
            self.k = k
            self.pbanks = [k.ps("pb%d" % i, [128, 512]) for i in range(6)]
            self.psm = k.ps("psm", [128, 512])
            self.pyd = [self.psm, k.ps("psd2", [128, 512])]
            self.prr = 0
            self.x = [k.sb("x%d" % t, [128, 8, 512]) for t in range(TT)]
            self.ones_bf = k.sb("ones_bf", [128, 128], BF16)
            self.scT = k.sb("scT", [128, 8, 2], BF16)
            self.condt = k.sb("condt", [128, 8, 2])
            self.bada = k.sb("bada", [128, 4, 72])
            self.ng = k.sb("ng", [128, 4, 3, 8])
            self.fg = k.sb("fg", [128, 8])
            self.mod = [k.sb("mod%d" % l, [128, 72, 2]) for l in range(4)]
            self.Am = [k.sb("Am%d" % l, [128, 3, 8, 2]) for l in range(4)]
            self.Gm = [k.sb("Gm%d" % l, [128, 3, 8, 2]) for l in range(4)]
            self.alloc_mixer_persistent()

            for t in range(TT):
                k.dma("sp", self.x[t][:], xT_d[:, :, t * 512:(t + 1) * 512], writes=[self.x[t]])
            k.dma("sp", self.condt[:], cond_d, writes=[self.condt])
            k.dma("sp", self.bada[:], bada_d, writes=[self.bada])
            k.dma("sp", self.ng[:], ng_d, writes=[self.ng])
            k.dma("sp", self.fg[:], fg_d, writes=[self.fg])
            k.op("pool", lambda e: e.memset(self.ones_bf[:], 1.0), [], [self.ones_bf])
            k.op("act", lambda e: e.activation(self.scT[:], self.condt[:], AF.Silu), [self.condt], [self.scT])
            self.load_mixer_consts()

            stages = []
            for l in range(4):
                stages += [("mod", l), ("ffn", l, 0), ("mix", l), ("ffn", l, 1)]
            done = False
            if self.start is not None:
                i0 = stages.index(self.start)
                stages = [("mod", self.start[1])] + stages[i0:]
            for st in stages:
                if st[0] == "mod":
                    self.modulation(st[1])
                elif st[0] == "ffn":
                    self.ffn(st[1], st[2])
                    if self.stop == ("ffn", st[1], st[2]):
                        done = True
                else:
                    self.mixer(st[1])
                    if self.stop == ("mix", st[1]):
                        done = True
                if done:
                    break
            self.final(yT_d, raw=self.debug)
            k.finish()
            self.n_inst = k.n_inst
        return nc

    def modulation(self, l):
        k = self.k
        wada_d = self.din["wada"]
        k.barrier()
        with ExitStack() as ph:
            wa = [k.sb("wa%d" % i, [128, 4096], BF16, es=ph) for i in range(4)]
            psm = self.psm
            for t in range(18):
                w = wa[t % 4]
                k.dma("pool", w[:], wada_d[0 if self.lite else l, t], writes=[w])
                for cc in range(4):
                    ci = t * 4 + cc
                    mms = []
                    for kc in range(8):
                        mms.append((psm[:, ci * 2:ci * 2 + 2], w[:, kc * 512 + cc * 128: kc * 512 + cc * 128 + 128],
                                    self.scT[:, kc, :], kc == 0, kc == 7))
                    k.mm(psm, mms, [w, self.scT])
            mod = self.mod[l]
            bb = bass.AP(self.bada.t, l * 72, [[4 * 72, 128], [1, 72], [0, 2]])
            k.op("dve", lambda e: e.tensor_tensor(mod[:], psm[:, 0:144].rearrange("p (a b) -> p a b", b=2), bb, ALU.add),
                 [psm, self.bada], [mod])
            Am, Gm = self.Am[l], self.Gm[l]
            for s in range(3):
                sc = mod[:, (3 * s + 1) * 8:(3 * s + 2) * 8, :]
                gt = mod[:, (3 * s + 2) * 8:(3 * s + 3) * 8, :]
                ngb = bass.AP(self.ng.t, l * 24 + s * 8, [[96, 128], [1, 8], [0, 2]])
                k.op("dve", lambda e: e.scalar_tensor_tensor(Am[:, s], sc, 1.0, ngb, ALU.add, ALU.mult), [mod, self.ng], [Am])
                k.op("dve", lambda e: e.tensor_scalar(Gm[:, s], gt, 0.5 if s != 1 else 1.0, None, ALU.mult), [mod], [Gm])
            k.barrier()

    def norm_mod(self, l, s, hT, ph, tiles=None, temps=None):
        k = self.k
        tiles = list(range(TT)) if tiles is None else tiles
        if temps is None:
            rsb = [k.sb("rs%d" % i, [128, 512], F32, es=ph) for i in range(len(tiles))]
            t3b = [k.sb("t32%d" % i, [128, 512], F32, es=ph) for i in range(3)]
            temps = ([(b, b[:]) for b in rsb], [(b, b[:]) for b in t3b])
        rsl, t3l = temps
        Am, mod = self.Am[l], self.mod[l]
        for ti, t in enumerate(tiles):
            x = self.x[t]
            q = hT[ti]
            rb, r = rsl[ti]
            k.op("act", lambda e: e.activation(q[:], x[:], AF.Square), [x], [q])
            pb = self.psum()
            k.mm(pb, [(pb[:], self.ones_bf[:], q[:, c, :], c == 0, c == 7) for c in range(8)], [self.ones_bf, q])
            k.op("act", lambda e: e.activation(r, pb[:], AF.Sqrt, bias=self.eps_t[:, 0:1], scale=1.0 / D), [pb, self.eps_t], [rb])
            k.op("dve", lambda e: e.reciprocal(r, r), [rb], [rb])
        n = 0
        for ti, t in enumerate(tiles):
            j = 0 if t < 2 else 1
            x = self.x[t]
            rb, r = rsl[ti]
            for c in range(8):
                tbb, tb = t3l[n % len(t3l)]
                n += 1
                k.op("dve", lambda e: e.scalar_tensor_tensor(tb, x[:, c, :], Am[:, s, c, j:j + 1], r, ALU.mult, ALU.mult),
                     [x, Am, rb], [tbb])
                k.op("act", lambda e: e.activation(hT[ti][:, c, :], tb, AF.Identity, bias=mod[:, 3 * s * 8 + c, j:j + 1]),
                     [tbb, mod], [hT[ti]])

    def ffn(self, l, i):
        k = self.k
        w1_d, w2_d = self.din["w1"], self.din["w2"]
        s = 0 if i == 0 else 2
        li = l * 2 + i
        k.barrier()
        with ExitStack() as ph:
            hT = [k.sb("hT%d" % t, [128, 8, 512], BF16, es=ph) for t in range(TT)]
            gT = [k.sb("gT%d" % t, [128, 6, 512], BF16, es=ph) for t in range(TT)]
            win = [k.sb("win%d" % t, [128, 2048], BF16, es=ph) for t in range(4)]
            wout = [k.sb("wout%d" % t, [128, 6 * 1024], BF16, es=ph) for t in range(2)]
            sl = [k.sb("sl%d" % t, [128, 512], F32, es=ph) for t in range(3)]
            self.norm_mod(l, s, hT, ph)
            Gm = self.Gm[l]
            nw = 0
            ns = 0
            for g, (kc0, n) in enumerate(FGROUPS):
                wo = wout[g % 2]
                if g > 0:
                    k.dma("pool", wo[:, 0:n * 1024], w2_d[li, :, kc0 * 1024:(kc0 + n) * 1024], writes=[wo])
                for fi in range(n):
                    fc = kc0 + fi
                    w = win[nw % 4]
                    nw += 1
                    k.dma("pool", w[:], w1_d[li * NFC + fc], writes=[w])
                    if g == 0 and fi == 1:
                        k.dma("pool", wo[:, 0:n * 1024], w2_d[li, :, kc0 * 1024:(kc0 + n) * 1024], writes=[wo])
                    for t in range(TT):
                        pa = self.psum()
                        k.mm(pa, [(pa[:], w[:, kc * 128:(kc + 1) * 128], hT[t][:, kc, :], kc == 0, kc == 7) for kc in range(8)], [w, hT[t]])
                        pu = self.psum()
                        k.mm(pu, [(pu[:], w[:, 1024 + kc * 128:1024 + (kc + 1) * 128], hT[t][:, kc, :], kc == 0, kc == 7) for kc in range(8)], [w, hT[t]])
                        st = sl[ns % 3]
                        ns += 1
                        k.op("act", lambda e: e.activation(st[:], pa[:], AF.Silu), [pa], [st])
                        k.op("dve", lambda e: e.tensor_tensor(gT[t][:, fi, :], st[:], pu[:], ALU.mult), [st, pu], [gT[t]])
                for dc in range(8):
                    for t in range(TT):
                        j = 0 if t < 2 else 1
                        py = self.psum()
                        k.mm(py, [(py[:], wo[:, q * 1024 + dc * 128:q * 1024 + dc * 128 + 128], gT[t][:, q, :], q == 0, q == n - 1)
                                  for q in range(n)], [wo, gT[t]])
                        x = self.x[t]
                        k.op("dve", lambda e: e.scalar_tensor_tensor(x[:, dc, :], py[:], Gm[:, s, dc, j:j + 1], x[:, dc, :], ALU.mult, ALU.add),
                             [py, Gm, x], [x])
            k.barrier()

    def final(self, yT_d, raw=False):
        k = self.k
        k.barrier()
        with ExitStack() as ph:
            if raw:
                for t in range(TT):
                    k.dma("sp", yT_d[:, :, t * 512:(t + 1) * 512], self.x[t][:], reads=[self.x[t]], is_output=True)
                return
            sq = [k.sb("fsq%d" % i, [128, 8, 512], BF16, es=ph) for i in range(2)]
            rs = [k.sb("frs%d" % i, [128, 512], F32, es=ph) for i in range(2)]
            yo = [k.sb("fyo%d" % i, [128, 8, 512], F32, es=ph) for i in range(2)]
            for t in range(TT):
                x = self.x[t]
                q, r, y = sq[t % 2], rs[t % 2], yo[t % 2]
                k.op("act", lambda e: e.activation(q[:], x[:], AF.Square), [x], [q])
                pb = self.psum()
                k.mm(pb, [(pb[:], self.ones_bf[:], q[:, c, :], c == 0, c == 7) for c in range(8)], [self.ones_bf, q])
                k.op("act", lambda e: e.activation(r[:], pb[:], AF.Sqrt, bias=self.eps_t[:, 0:1], scale=1.0 / D), [pb, self.eps_t], [r])
                k.op("dve", lambda e: e.reciprocal(r[:], r[:]), [r], [r])
                for c in range(8):
                    k.op("dve", lambda e: e.scalar_tensor_tensor(y[:, c, :], x[:, c, :], self.fg[:, c:c + 1], r[:], ALU.mult, ALU.mult),
                         [x, self.fg, r], [y])
                k.dma("sp", yT_d[:, :, t * 512:(t + 1) * 512], y[:], reads=[y], is_output=True)

    def declare_mixer_io(self):
        I, O = self.I, self.O
        I("ident", [128, 128])
        I("gm_wu", [128, 8192]); I("gm_wv", [128, 8192]); I("gm_wo", [128, 8192])
        I("gm_wsT", [128, 1024]); I("gm_lng", [1, 1024]); I("gm_lnb", [1, 1024]); I("gm_bs", [1, 1024])
        I("nat_wqkv", [8, 128, 3072]); I("nat_wo", [128, 8192])
        I("nat_tbT", [8, 128, 960]); I("nat_maskT", [128, 64])
        I("nat_ckT", [8, 128, 256]); I("nat_cv", [8, 128, 256])
        O("kc_out", [128, 8, 1024]); O("vc_out", [128, 8, 8, 128])
        I("gla_win", [4, 128, 8, 1024]); I("gla_wo", [4, 128, 2, 1024]); I("gla_wa1", [128, 8, 32]); I("gla_wa2", [33, 4, 256])
        I("gla_tri", [128, 4, 128]); I("gla_msk", [128, 2, 128]); I("gla_cos", [128, 1024]); I("gla_sin", [128, 1024])
        I("gla_ng", [128, 8]); I("gla_s0", [128, 2, 4, 256])
        O("gla_out", [128, 4, 2, 4, 256])
        I("ssd_wx", [4, 128, 8, 512]); I("ssd_wz", [4, 128, 8, 512]); I("ssd_wbc", [4, 128, 8, 256]); I("ssd_wdt", [128, 8, 64])
        I("ssd_wo", [128, 16, 1024]); I("ssd_tri", [128, 4, 128]); I("ssd_cw", [128, 24, 3]); I("ssd_cb", [128, 24])
        I("ssd_dtb", [1, 64]); I("ssd_alog", [1, 64]); I("ssd_dsk", [1, 32]); I("ssd_ng", [128, 16]); I("ssd_s0", [128, 2, 32, 64])
        O("ssd_out", [128, 4, 2, 32, 64])

    def alloc_mixer_persistent(self):
        k = self.k
        self.eps_t = k.sb("eps_t", [128, 1])
        k.op("pool", lambda e: e.memset(self.eps_t[:], EPS), [], [self.eps_t])
        self.ident_bf = k.sb("ident_bf", [128, 128], BF16)
        self.ident_f = k.sb("ident_f", [128, 128], F32)

    def load_mixer_consts(self):
        k = self.k
        k.dma("pool", self.ident_bf[:], self.din["ident"], writes=[self.ident_bf])
        k.dma("sp", self.ident_f[:], self.din["ident"], writes=[self.ident_f])

    def mixer(self, l):
        k = self.k
        k.barrier()
        with ExitStack() as ph:
            if l == 3:
                self.mix_ssd(l, None, ph)
            else:
                hT = [k.sb("mhT%d" % t, [128, 8, 512], BF16, es=ph) for t in range(TT)]
                with ExitStack() as sub:
                    self.norm_mod(l, 1, hT, sub)
                    k.barrier()
                [self.mix_gla, self.mix_nat, self.mix_gmlp][l](l, hT, ph)
            k.barrier()

    def out_proj(self, l, Wo, srcT, only=None):
        k = self.k
        Gm = self.Gm[l]
        for dc in range(8):
            for t in (range(TT) if only is None else [only]):
                j = 0 if t < 2 else 1
                py = self.psum()
                k.mm(py, [(py[:], Wo[:, q * 1024 + dc * 128:q * 1024 + dc * 128 + 128], srcT[t][:, q, :], q == 0, q == 7)
                          for q in range(8)], [Wo, srcT[t]])
                x = self.x[t]
                k.op("dve", lambda e: e.scalar_tensor_tensor(x[:, dc, :], py[:], Gm[:, 1, dc, j:j + 1], x[:, dc, :], ALU.mult, ALU.add),
                     [py, Gm, x], [x])

    def gelu_tanh(self, out_ap, out_buf, p, tmps, n):
        k = self.k
        t1, t3 = tmps
        k.op("act", lambda e: e.activation(t1[:, 0:n], p[:, 0:n], AF.Square, scale=0.044715 ** 0.5), [p], [t1])
        k.op("dve", lambda e: e.scalar_tensor_tensor(t3[:, 0:n], t1[:, 0:n], 1.0, p[:, 0:n], ALU.add, ALU.mult), [t1, p], [t3])
        k.op("act", lambda e: e.activation(t3[:, 0:n], t3[:, 0:n], AF.Sigmoid, scale=1.5957691216), [t3], [t3])
        k.op("dve", lambda e: e.tensor_tensor(out_ap, t3[:, 0:n], p[:, 0:n], ALU.mult), [t3, p], [out_buf])

    def mix_gmlp(self, l, hT, ph):
        k = self.k
        d = self.din
        Wu = k.sb("gWu", [128, 8192], BF16, es=ph)
        Wv = k.sb("gWv", [128, 8192], BF16, es=ph)
        Wo = k.sb("gWo", [128, 8192], BF16, es=ph)
        wsT = k.sb("gwsT", [128, 1024], BF16, es=ph)
        lng = k.sb("glng", [128, 1024], F32, es=ph)
        lnb = k.sb("glnb", [128, 1024], F32, es=ph)
        bsb = k.sb("gbsb", [128, 1024], F32, es=ph)
        for (w, nm) in ((Wu, "gm_wu"), (Wv, "gm_wv"), (Wo, "gm_wo")):
            for h in range(2):
                k.dma("pool", w[:, h * 4096:(h + 1) * 4096], d[nm][:, h * 4096:(h + 1) * 4096], writes=[w])
        k.dma("pool", wsT[:], d["gm_wsT"], writes=[wsT])
        k.dma("sp", lng[:], d["gm_lng"].partition_broadcast(128), writes=[lng])
        k.dma("sp", lnb[:], d["gm_lnb"].partition_broadcast(128), writes=[lnb])
        k.dma("sp", bsb[:], d["gm_bs"].partition_broadcast(128), writes=[bsb])
        uT = k.sb("guT", [128, 8, 512], BF16, es=ph)
        gm2 = [k.sb("ggmT%d" % t, [128, 8, 512], BF16, es=ph) for t in range(1)]
        gmT = [gm2[0], gm2[0], gm2[0], gm2[0]]
        vt = k.sb("gvt", [128, 1024], F32, es=ph)
        vsq = k.sb("gvsq", [128, 1024], F32, es=ph)
        vnb = k.sb("gvnb", [128, 4, 1024], BF16, es=ph)
        st = k.sb("gst", [128, 8], F32, es=ph)
        tmpa = [(k.sb("gt1%d" % i, [128, 512], F32, es=ph), k.sb("gt3%d" % i, [128, 512], F32, es=ph)) for i in range(2)]
        zz = [k.sb("gzz%d" % i, [128, 512], F32, es=ph) for i in range(1)] * 2
        ng = 0
        for t in range(TT):
            for fc in range(8):
                pu = self.psum()
                k.mm(pu, [(pu[:], Wu[:, kc * 1024 + fc * 128:kc * 1024 + fc * 128 + 128], hT[t][:, kc, :], kc == 0, kc == 7) for kc in range(8)], [Wu, hT[t]])
                self.gelu_tanh(uT[:, fc, :], uT, pu, tmpa[ng % 2], 512)
                ng += 1
            for q4 in range(4):
                for half in range(2):
                    pv = self.psum()
                    k.mm(pv, [(pv[:], hT[t][:, kc, q4 * 128:(q4 + 1) * 128], Wv[:, kc * 1024 + half * 512:kc * 1024 + half * 512 + 512], kc == 0, kc == 7)
                              for kc in range(8)], [Wv, hT[t]])
                    self.gelu_tanh(vt[:, half * 512:(half + 1) * 512], vt, pv, tmpa[ng % 2], 512)
                    ng += 1
                k.op("dve", lambda e: e.tensor_reduce(st[:, 0:1], vt[:], AX.X, ALU.add), [vt], [st])
                k.op("pool", lambda e: e.tensor_tensor(vsq[:], vt[:], vt[:], ALU.mult), [vt], [vsq])
                k.op("dve", lambda e: e.tensor_reduce(st[:, 1:2], vsq[:], AX.X, ALU.add), [vsq], [st])
                k.op("dve", lambda e: e.tensor_scalar(st[:, 2:3], st[:, 0:1], 1.0 / 1024, None, ALU.mult), [st], [st])
                k.op("dve", lambda e: e.tensor_tensor(st[:, 3:4], st[:, 2:3], st[:, 2:3], ALU.mult), [st], [st])
                k.op("dve", lambda e: e.scalar_tensor_tensor(st[:, 4:5], st[:, 1:2], 1.0 / 1024, st[:, 3:4], ALU.mult, ALU.subtract), [st], [st])
                k.op("act", lambda e: e.activation(st[:, 5:6], st[:, 4:5], AF.Sqrt, bias=self.eps_t[:, 0:1]), [st, self.eps_t], [st])
                k.op("dve", lambda e: e.reciprocal(st[:, 6:7], st[:, 5:6]), [st], [st])
                k.op("dve", lambda e: e.tensor_scalar(vsq[:], vt[:], st[:, 2:3], st[:, 6:7], ALU.subtract, ALU.mult), [vt, st], [vsq])
                k.op("pool", lambda e: e.tensor_tensor(vsq[:], vsq[:], lng[:], ALU.mult), [vsq, lng], [vsq])
                k.op("pool", lambda e: e.tensor_tensor(vnb[:, q4, :], vsq[:], lnb[:], ALU.add), [vsq, lnb], [vnb])
            for g in range(8):
                pz = self.psum()
                for q4 in range(4):
                    k.mm(pz, [(pz[:, q4 * 128:(q4 + 1) * 128], vnb[:, q4, g * 128:(g + 1) * 128], wsT[:, g * 128:(g + 1) * 128], True, True)], [vnb, wsT])
                z = zz[g % 2]
                bb = bass.AP(bsb.t, g * 128, [[1024, 128], [0, 4], [1, 128]])
                k.op("dve", lambda e: e.tensor_tensor(z[:].rearrange("p (a b) -> p a b", a=4), pz[:].rearrange("p (a b) -> p a b", a=4), bb, ALU.add), [pz, bsb], [z])
                k.op("dve", lambda e: e.tensor_tensor(gmT[t][:, g, :], z[:], uT[:, g, :], ALU.mult), [z, uT], [gmT[t]])
            self.out_proj(l, Wo, gmT, only=t)

    def mix_nat(self, l, hT, ph):
        k = self.k
        d = self.din
        Wo = k.sb("nWo", [128, 8192], BF16, es=ph)
        for h in range(2):
            k.dma("pool", Wo[:, h * 4096:(h + 1) * 4096], d["nat_wo"][:, h * 4096:(h + 1) * 4096], writes=[Wo])
        oT = [k.sb("noT%d" % t, [128, 8, 512], BF16, es=ph) for t in range(TT)]
        Wp = [k.sb("nWp%d" % i, [128, 3072], BF16, es=ph) for i in range(2)]
        maskT = k.sb("nmask", [128, 64], F32, es=ph)
        if not (NATV & 64):
            k.dma("sp", maskT[:], d["nat_maskT"], writes=[maskT])
        tbr = [k.sb("ntbr%d" % i, [128, 960], F32, es=ph) for i in range(1)] * 2
        tbT = [k.sb("ntbT%d" % i, [128, 15, 64], BF16, es=ph) for i in range(2)]
        v2 = k.sb("nv2", [128, 7, 128], BF16, es=ph)
        ckT = [k.sb("nckT%d" % i, [128, 256], BF16, es=ph) for i in range(2)]
        cv = [k.sb("ncv%d" % i, [128, 256], BF16, es=ph) for i in range(2)]
        qT = [k.sb("nqT%d" % i, [128, 1024], BF16, es=ph) for i in range(2)]
        kT = [k.sb("nkT%d" % i, [128, 1024], BF16, es=ph) for i in range(2)]
        vtk = [k.sb("nvt%d" % i, [128, 8, 128], BF16, es=ph) for i in range(2)]
        k32 = [k.sb("nk32%d" % i, [128, 256], F32, es=ph) for i in range(2)]
        v32 = [k.sb("nv32%d" % i, [128, 2, 128], F32, es=ph) for i in range(2)]
        PT = [k.sb("nPT%d" % i, [128, 512], BF16, es=ph) for i in range(4)]
        rdl = [k.sb("nrdl%d" % i, [128, 64], F32, es=ph) for i in range(4)]
        rd = [k.sb("nrd%d" % i, [128, 256], F32, es=ph) for i in range(2)]
        kc_out, vc_out = self.dout["kc_out"], self.dout["vc_out"]
        nseq = 0
        npt = 0
        for pr in range(8):
            W = Wp[pr % 2]
            k.dma("pool", W[:], d["nat_wqkv"][pr], writes=[W])
            tr_, tb = tbr[pr % 2], tbT[pr % 2]
            if not (NATV & 64):
                k.dma("sp", tr_[:], d["nat_tbT"][pr], writes=[tr_])
            mb = bass.AP(maskT.t, 0, [[64, 128], [0, 15], [1, 64]])
            if not (NATV & 16):
                k.op("dve", lambda e: e.tensor_tensor(tb[:], tr_[:].rearrange("p (b c) -> p b c", b=15), mb, ALU.add), [tr_, maskT], [tb])
            ck, cvv = ckT[pr % 2], cv[pr % 2]
            if not (NATV & 128):
                k.dma("pool", ck[:], d["nat_ckT"][pr], writes=[ck])
                k.dma("pool", cvv[:], d["nat_cv"][pr], writes=[cvv])
            for sq in range(5):
                if NATV & 32:
                    continue
                T = 256 if sq < 4 else 1024
                t0 = sq * 256
                q_, k_, v_ = qT[nseq % 2], kT[nseq % 2], vtk[nseq % 2]
                kk32, vv32 = k32[nseq % 2], v32[nseq % 2]
                nseq += 1
                for c0 in range(0, T, 512):
                    n = min(512, T - c0)
                    tt, off = (t0 + c0) // 512, (t0 + c0) % 512
                    pq = self.psum()
                    if not (NATV & 256):
                        k.mm(pq, [(pq[:, 0:n], W[:, kc * 128:(kc + 1) * 128], hT[tt][:, kc, off:off + n], kc == 0, kc == 7) for kc in range(8)], [W, hT[tt]])
                        k.op("act", lambda e: e.mul(q_[:, c0:c0 + n], pq[:, 0:n], 0.125), [pq], [q_])
                    if NATV & 512:
                        continue
                    pk = self.psum()
                    k.mm(pk, [(pk[:, 0:n], W[:, 1024 + kc * 128:1024 + (kc + 1) * 128], hT[tt][:, kc, off:off + n], kc == 0, kc == 7) for kc in range(8)], [W, hT[tt]])
                    if not (NATV & 2048):
                        k.op("dve", lambda e: e.tensor_copy(k_[:, c0:c0 + n], pk[:, 0:n]), [pk], [k_])
                    if sq < 4 and not (NATV & 4096):
                        k.op("act", lambda e: e.copy(kk32[:, 0:n], pk[:, 0:n]), [pk], [kk32])
                        if not (NATV & 8):
                            k.dma("sp", kc_out[:, pr, t0:t0 + 256], kk32[:], reads=[kk32], is_output=True)
                for ch in range(T // 128):
                    if NATV & 1024:
                        continue
                    tt, off = (t0 + ch * 128) // 512, (t0 + ch * 128) % 512
                    pv = self.psum()
                    k.mm(pv, [(pv[:, 0:128], hT[tt][:, kc, off:off + 128], W[:, 2048 + kc * 128:2048 + (kc + 1) * 128], kc == 0, kc == 7) for kc in range(8)], [W, hT[tt]])
                    k.op("dve", lambda e: e.tensor_copy(v_[:, ch, :], pv[:, 0:128]), [pv], [v_])
                    if sq < 4:
                        k.op("act", lambda e: e.copy(vv32[:, ch, :], pv[:, 0:128]), [pv], [vv32])
                if sq < 4:
                    if not (NATV & 8):
                        k.dma("sp", vc_out[:, pr, sq * 2:sq * 2 + 2, :], vv32[:], reads=[vv32], is_output=True)
                elif not (NATV & 1):
                    k.dma("sp", v2[0:64, :, :], v_[64:128, 0:7, :], reads=[v_], writes=[v2])
                    k.dma("sp", v2[64:128, :, :], v_[0:64, 1:8, :], reads=[v_], writes=[v2])
                def drive(gens):
                    while gens:
                        for gn in list(gens):
                            try:
                                next(gn)
                            except StopIteration:
                                gens.remove(gn)

                def unit_ctx(hh, ci_):
                    hb = hh * 64
                    pS = self.psum()
                    for ch in range(2):
                        k.mm(pS, [(pS[:, ch * 256:(ch + 1) * 256], k_[hb:hb + 64, ch * 128:(ch + 1) * 128], q_[hb:hb + 64, 0:256], True, True)], [k_, q_])
                    P = PT[ci_]
                    yield
                    k.op("act", lambda e: e.activation(P[:], pS[:], AF.Exp), [pS], [P])
                    yield
                    po = self.psum()
                    k.mm(po, [(po[:, 0:256], v_[:, ch, :], P[:, ch * 256:(ch + 1) * 256], ch == 0, ch == 1) for ch in range(2)], [v_, P])
                    k.mm(po, [(po[:, 256:512], self.ones_bf[:], P[:, ch * 256:(ch + 1) * 256], ch == 0, ch == 1) for ch in range(2)], [self.ones_bf, P])
                    r_ = rd[ci_ % 2]
                    k.op("dve", lambda e: e.reciprocal(r_[hb:hb + 64, 0:256], po[hb:hb + 64, 256:512]), [po], [r_])
                    tt, off = t0 // 512, t0 % 512
                    k.op("dve", lambda e: e.tensor_tensor(oT[tt][hb:hb + 64, pr, off:off + 256], po[hb:hb + 64, 0:256], r_[hb:hb + 64, 0:256], ALU.mult), [po, r_], [oT[tt]])

                def unit_lat(hh, r, ci_):
                    hb = hh * 64
                    row0 = min(max(r - 4, 0), 8)
                    nch = 6
                    pS = self.psum()
                    qs = q_[hb:hb + 64, r * 64:(r + 1) * 64]
                    for ci in range(4):
                        sl = row0 + 2 * ci - r + 7
                        lb = tb[hb:hb + 64, sl:sl + 2, :].rearrange("p a b -> p (a b)")
                        kc0 = 64 * row0 + 128 * ci
                        k.mm(pS, [(pS[:, ci * 64:(ci + 1) * 64], k_[hb:hb + 64, kc0:kc0 + 128], qs, True, False),
                                  (pS[:, ci * 64:(ci + 1) * 64], lb, self.ident_bf[hb:hb + 64, hb:hb + 64], False, True)], [k_, q_, tb, self.ident_bf])
                    for cj in range(2):
                        ci = 4 + cj
                        k.mm(pS, [(pS[:, ci * 64:(ci + 1) * 64], ck[hb:hb + 64, cj * 128:(cj + 1) * 128], qs, True, True)], [ck, q_])
                    P = PT[ci_]
                    yield
                    k.op("act", lambda e: e.activation(P[:, 0:nch * 64], pS[:, 0:nch * 64], AF.Exp), [pS], [P])
                    yield
                    po = self.psum()
                    if row0 % 2 == 0:
                        vs = [v_[:, row0 // 2 + ci, :] for ci in range(4)]
                    else:
                        vs = [v2[:, (row0 - 1) // 2 + ci, :] for ci in range(4)]
                    vs += [cvv[:, cj * 128:(cj + 1) * 128] for cj in range(2)]
                    k.mm(po, [(po[:, 0:64], vs[ci], P[:, ci * 64:(ci + 1) * 64], ci == 0, ci == nch - 1) for ci in range(nch)], [v_, v2, cvv, P])
                    k.mm(po, [(po[:, 64:128], self.ones_bf[:], P[:, ci * 64:(ci + 1) * 64], ci == 0, ci == nch - 1) for ci in range(nch)], [self.ones_bf, P])
                    r_ = rdl[ci_]
                    k.op("dve", lambda e: e.reciprocal(r_[hb:hb + 64, 0:64], po[hb:hb + 64, 64:128]), [po], [r_])
                    tt, off = (t0 + r * 64) // 512, (t0 + r * 64) % 512
                    k.op("dve", lambda e: e.tensor_tensor(oT[tt][hb:hb + 64, pr, off:off + 64], po[hb:hb + 64, 0:64], r_[hb:hb + 64, 0:64], ALU.mult), [po, r_], [oT[tt]])

                if sq < 4:
                    drive([unit_ctx(0, 0), unit_ctx(1, 1)])
                else:
                    for r2 in range(0, 16, 2):
                        drive([unit_lat(0, r2, 0), unit_lat(1, r2, 1), unit_lat(0, r2 + 1, 2), unit_lat(1, r2 + 1, 3)])
        self.out_proj(l, Wo, oT)

    def mix_gla(self, l, hT, ph):
        k = self.k
        d = self.din
        SC = 128 ** -0.5
        W = k.sb("aW", [128, 8, 1024], BF16, es=ph)
        Wo = k.sb("aWo", [128, 2, 1024], BF16, es=ph)
        wa1 = k.sb("awa1", [128, 8, 32], BF16, es=ph)
        wa2 = k.sb("awa2", [33, 4, 256], BF16, es=ph)
        z1a = k.sb("az1a", [33, NTOK], BF16, es=ph)
        tri = k.sb("atri", [128, 4, 128], F32, es=ph)
        msk = k.sb("amsk", [128, 2, 128], F32, es=ph)
        cos = k.sb("acos", [128, 1024], F32, es=ph)
        sin = k.sb("asin", [128, 1024], F32, es=ph)
        ng = k.sb("ang", [128, 8], F32, es=ph)
        q32 = k.sb("aq32", [128, 1024], F32, es=ph)
        k32 = k.sb("ak32", [128, 1024], F32, es=ph)
        kt32 = k.sb("akt32", [128, 8, 128], F32, es=ph)
        vtk = k.sb("avtk", [128, 8, 256], BF16, es=ph)
        srT = k.sb("asrT", [128, 2, 1024], BF16, es=ph)
        Ltok = k.sb("aLtok", [128, 8, 256], F32, es=ph)
        oF = k.sb("aoF", [128, 2, 1024], F32, es=ph)
        GT = k.sb("aGT", [128, 2, 1024], BF16, es=ph)
        sqn = k.sb("asqn", [128, 2, 512], BF16, es=ph)
        S32s = [k.sb("aS32%d" % i, [128, 256], F32, es=ph) for i in range(2)]
        Sbfs = [k.sb("aSbf%d" % i, [128, 256], BF16, es=ph) for i in range(2)]
        Ebt = [k.sb("aEb%d" % i, [128, 128], F32, es=ph) for i in range(2)]
        Ent = [k.sb("aEn%d" % i, [128, 128], F32, es=ph) for i in range(2)]
        EDt = [k.sb("aED%d" % i, [128, 128], F32, es=ph) for i in range(2)]
        qd = [k.sb("aqd%d" % i, [128, 128], BF16, es=ph) for i in range(2)]
        kd = [k.sb("akd%d" % i, [128, 128], BF16, es=ph) for i in range(2)]
        Am = [k.sb("aAm%d" % i, [128, 128], BF16, es=ph) for i in range(2)]
        kl = [k.sb("akl%d" % i, [128, 128], BF16, es=ph) for i in range(2)]
        t1 = k.sb("at1", [128, 512], F32, es=ph)
        t2 = k.sb("at2", [128, 512], F32, es=ph)
        rsd = k.sb("arsd", [128, 512], F32, es=ph)
        k.dma("pool", wa1[:], d["gla_wa1"], writes=[wa1])
        k.dma("pool", wa2[:], d["gla_wa2"], writes=[wa2])
        k.dma("sp", tri[:], d["gla_tri"], writes=[tri])
        k.dma("sp", msk[:], d["gla_msk"], writes=[msk])
        k.dma("sp", cos[:], d["gla_cos"], writes=[cos])
        k.dma("sp", sin[:], d["gla_sin"], writes=[sin])
        k.dma("sp", ng[:], d["gla_ng"], writes=[ng])
        gla_out = self.dout["gla_out"]
        k.op("pool", lambda e: e.memset(z1a[:], 1.0), [], [z1a])
        for t in range(TT):
            pz = self.psum()
            k.mm(pz, [(pz[0:32, :], wa1[:, kc, :], hT[t][:, kc, :], kc == 0, kc == 7) for kc in range(8)], [wa1, hT[t]])
            k.op("act", lambda e: e.copy(z1a[0:32, t * 512:(t + 1) * 512], pz[0:32, :]), [pz], [z1a])
        Gm = self.Gm[l]
        n2 = 0
        for h in range(4):
            for hf in range(2):
                k.dma("pool", W[:, hf * 4:(hf + 1) * 4, :], d["gla_win"][h, :, hf * 4:(hf + 1) * 4, :], writes=[W])
            k.dma("pool", Wo[:], d["gla_wo"][h], writes=[Wo])
            for sq in range(5):
                T = 256 if sq < 4 else 1024
                t0 = sq * 256
                nch = T // 128
                rope = sq == 4
                for c0 in range(0, T, 512):
                    n = min(512, T - c0)
                    tt, off = (t0 + c0) // 512, (t0 + c0) % 512
                    for (dst, cb) in ((q32, 0), (k32, 128)):
                        pq = self.psum()
                        k.mm(pq, [(pq[:, 0:n], W[:, kc, cb:cb + 128], hT[tt][:, kc, off:off + n], kc == 0, kc == 7) for kc in range(8)], [W, hT[tt]])
                        if rope:
                            ps2 = self.psum()
                            k.mm(ps2, [(ps2[:, 0:n], W[:, kc, 256 + cb:256 + cb + 128], hT[tt][:, kc, off:off + n], kc == 0, kc == 7) for kc in range(8)], [W, hT[tt]])
                            k.op("dve", lambda e: e.tensor_tensor(t1[:, 0:n], pq[:, 0:n], cos[:, c0:c0 + n], ALU.mult), [pq, cos], [t1])
                            k.op("dve", lambda e: e.tensor_tensor(t2[:, 0:n], ps2[:, 0:n], sin[:, c0:c0 + n], ALU.mult), [ps2, sin], [t2])
                            k.op("pool", lambda e: e.tensor_tensor(dst[:, c0:c0 + n], t1[:, 0:n], t2[:, 0:n], ALU.add), [t1, t2], [dst])
                        else:
                            k.op("act", lambda e: e.copy(dst[:, c0:c0 + n], pq[:, 0:n]), [pq], [dst])
                    for dvc in range(2):
                        pr = self.psum()
                        k.mm(pr, [(pr[:, 0:n], W[:, kc, 768 + dvc * 128:768 + (dvc + 1) * 128], hT[tt][:, kc, off:off + n], kc == 0, kc == 7) for kc in range(8)], [W, hT[tt]])
                        k.op("act", lambda e: e.activation(srT[:, dvc, c0:c0 + n], pr[:, 0:n], AF.Silu), [pr], [srT])
                for ch in range(nch):
                    tt, off = (t0 + ch * 128) // 512, (t0 + ch * 128) % 512
                    pv = self.psum()
                    k.mm(pv, [(pv[:, 0:256], hT[tt][:, kc, off:off + 128], W[:, kc, 512:768], kc == 0, kc == 7) for kc in range(8)], [W, hT[tt]])
                    k.op("act", lambda e: e.copy(vtk[:, ch, :], pv[:, 0:256]), [pv], [vtk])
                    pl = self.psum()
                    k.mm(pl, [(pl[:, 0:256], z1a[0:33, t0 + ch * 128:t0 + (ch + 1) * 128], wa2[0:33, h, :], True, True)], [z1a, wa2])
                    k.op("act", lambda e: e.activation(Ltok[:, ch, :], pl[:, 0:256], AF.Exp, scale=-1.0), [pl], [Ltok])
                    k.op("act", lambda e: e.activation(Ltok[:, ch, :], Ltok[:, ch, :], AF.Ln, bias=1.0), [Ltok], [Ltok])
                    pt = self.psum()
                    k.tr(pt, pt[:, 0:128], k32[:, ch * 128:(ch + 1) * 128], self.ident_f[:], [k32, self.ident_f])
                    k.op("dve", lambda e: e.tensor_copy(kt32[:, ch, :], pt[:, 0:128]), [pt], [kt32])
                for e_ in range(2):
                    if rope:
                        k.dma("sp", S32s[e_][:], d["gla_s0"][:, e_, h, :], writes=[S32s[e_]])
                    else:
                        k.op("pool", lambda e: e.memset(S32s[e_][:], 0.0), [], [S32s[e_]])
                    k.op("act", lambda e: e.copy(Sbfs[e_][:], S32s[e_][:]), [S32s[e_]], [Sbfs[e_]])
                def unit(e_, ch, first):
                    S32, Sbf = S32s[e_], Sbfs[e_]
                    gcol = 127 if e_ == 0 else 0
                    i2 = e_
                    cs = slice(ch * 128, (ch + 1) * 128)
                    Lc = Ltok[:, ch, e_ * 128:(e_ + 1) * 128]
                    pb = self.psum()
                    k.mm(pb, [(pb[:, 0:128], Lc, tri[:, 2 * e_, :], True, True)], [Ltok, tri])
                    Eb, En, ED = Ebt[i2], Ent[i2], EDt[i2]
                    pD = self.psum()
                    k.mm(pD, [(pD[:, 0:128], tri[:, 2 * e_ + 1, :], Lc, True, True)], [Ltok, tri])
                    yield
                    k.op("act", lambda e: e.activation(Eb[:], pb[:, 0:128], AF.Exp), [pb], [Eb])
                    k.op("act", lambda e: e.activation(En[:], pb[:, 0:128], AF.Exp, scale=-1.0), [pb], [En])
                    k.op("dve", lambda e: e.scalar_tensor_tensor(qd[i2][:], q32[:, cs], SC, Eb[:], ALU.mult, ALU.mult), [q32, Eb], [qd[i2]])
                    k.op("dve", lambda e: e.tensor_tensor(kd[i2][:], k32[:, cs], En[:], ALU.mult), [k32, En], [kd[i2]])
                    yield
                    pA = self.psum()
                    k.mm(pA, [(pA[:, 0:128], kd[i2][:], qd[i2][:], True, True)], [kd[i2], qd[i2]])
                    k.op("dve", lambda e: e.tensor_tensor(Am[i2][:], pA[:, 0:128], msk[:, e_, :], ALU.mult), [pA, msk], [Am[i2]])
                    k.op("act", lambda e: e.activation(ED[:], pD[:, 0:128], AF.Exp), [pD], [ED])
                    k.op("dve", lambda e: e.tensor_tensor(kl[i2][:], kt32[:, ch, :], ED[:], ALU.mult), [kt32, ED], [kl[i2]])
                    yield
                    po = self.psum()
                    for dvc in range(2):
                        k.mm(po, [(po[:, dvc * 128:(dvc + 1) * 128], vtk[:, ch, dvc * 128:(dvc + 1) * 128], Am[i2][:], True, False),
                                  (po[:, dvc * 128:(dvc + 1) * 128], Sbf[:, dvc * 128:(dvc + 1) * 128], qd[i2][:], False, True)], [vtk, Am[i2], Sbf, qd[i2]])
                    if first:
                        k.op("act", lambda e: e.copy(oF[:, :, cs], po[:, 0:256].rearrange("p (a b) -> p a b", a=2)), [po], [oF])
                    else:
                        k.op("dve", lambda e: e.tensor_tensor(oF[:, :, cs], oF[:, :, cs], po[:, 0:256].rearrange("p (a b) -> p a b", a=2), ALU.add), [po, oF], [oF])
                    yield
                    pU = self.psum()
                    k.mm(pU, [(pU[:, 0:256], kl[i2][:], vtk[:, ch, :], True, True)], [kl[i2], vtk])
                    k.op("dve", lambda e: e.scalar_tensor_tensor(S32[:], S32[:], Eb[:, gcol:gcol + 1], pU[:, 0:256], ALU.mult, ALU.add), [S32, Eb, pU], [S32])
                    k.op("act", lambda e: e.copy(Sbf[:], S32[:]), [S32], [Sbf])

                written = set()
                for step in range(nch):
                    gens = []
                    for e_ in range(2):
                        ch = step if e_ == 0 else nch - 1 - step
                        gens.append(unit(e_, ch, ch not in written))
                        written.add(ch)
                    while gens:
                        for gn in list(gens):
                            try:
                                next(gn)
                            except StopIteration:
                                gens.remove(gn)
                if not rope:
                    for e_ in range(2):
                        k.dma("sp", gla_out[:, sq, e_, h, :], S32s[e_][:], reads=[S32s[e_]], is_output=True)
                for c0 in range(0, T, 512):
                    n = min(512, T - c0)
                    tt, off = (t0 + c0) // 512, (t0 + c0) % 512
                    k.op("act", lambda e: e.activation(sqn[:, :, 0:n], oF[:, :, c0:c0 + n], AF.Square), [oF], [sqn])
                    pn = self.psum()
                    k.mm(pn, [(pn[:, 0:n], self.ones_bf[:], sqn[:, dvc, 0:n], dvc == 0, dvc == 1) for dvc in range(2)], [self.ones_bf, sqn])
                    k.op("act", lambda e: e.activation(rsd[:, 0:n], pn[:, 0:n], AF.Sqrt, bias=self.eps_t[:, 0:1], scale=1.0 / 256), [pn, self.eps_t], [rsd])
                    k.op("dve", lambda e: e.reciprocal(rsd[:, 0:n], rsd[:, 0:n]), [rsd], [rsd])
                    for dvc in range(2):
                        k.op("dve", lambda e: e.scalar_tensor_tensor(t1[:, 0:n], oF[:, dvc, c0:c0 + n], ng[:, h * 2 + dvc:h * 2 + dvc + 1], rsd[:, 0:n], ALU.mult, ALU.mult), [oF, ng, rsd], [t1])
                        k.op("pool", lambda e: e.tensor_tensor(GT[:, dvc, c0:c0 + n], t1[:, 0:n], srT[:, dvc, c0:c0 + n], ALU.mult), [t1, srT], [GT])
                    j = 0 if tt < 2 else 1
                    for dc in range(8):
                        py = self.psum()
                        k.mm(py, [(py[:, 0:n], Wo[:, dvc, dc * 128:(dc + 1) * 128], GT[:, dvc, c0:c0 + n], dvc == 0, dvc == 1) for dvc in range(2)], [Wo, GT])
                        x = self.x[tt]
                        k.op("dve", lambda e: e.scalar_tensor_tensor(x[:, dc, off:off + n], py[:, 0:n], Gm[:, 1, dc, j:j + 1], x[:, dc, off:off + n], ALU.mult, ALU.add),
                             [py, Gm, x], [x])

    def mix_ssd(self, l, hT_unused, ph):
        k = self.k
        d = self.din
        hTb = [k.sb("shT%d" % i, [128, 8, 512], BF16, es=ph) for i in range(2)]
        yT = k.sb("syT", [128, 16, 1024], BF16, es=ph)
        Wp = k.sb("sWp", [128, 8, 512], BF16, es=ph)
        Wo = View(Wp, Wp[:, 0:4, :].rearrange("p a (b c) -> p (a b) c", c=128))
        Wdt = k.sb("sWdt", [128, 8, 64], BF16, es=ph)
        dt_t = k.sb("sdt", [128, 8, 64], F32, es=ph)
        dtA = None
        ecum = k.sb("secum", [128, 8, 64], F32, es=ph)
        edec = k.sb("sedec", [128, 8, 64], F32, es=ph)
        gdec = k.sb("sgdec", [128, 8, 64], F32, es=ph)
        raw = k.sb("sraw", [128, 1024], F32, es=ph)
        dtA = View(raw, raw[:, 0:512].rearrange("p (a b) -> p a b", a=8))
        cv = k.sb("scv", [128, 1024], F32, es=ph)
        xtok = k.sb("sxtok", [128, 8, 512], BF16, es=ph)
        dtx = [k.sb("sdtx%d" % i, [128, 512], BF16, es=ph) for i in range(2)]
        dtxd = [k.sb("sdtxd%d" % i, [128, 512], BF16, es=ph) for i in range(2)]
        BcT = k.sb("sBcT", [128, 1024], BF16, es=ph)
        CcT = k.sb("sCcT", [128, 1024], BF16, es=ph)
        Btok = k.sb("sBtok", [128, 8, 128], BF16, es=ph)
        yacc = k.sb("syacc", [128, 8, 512], F32, es=ph)
        STs = [k.sb("sST32%d" % i, [128, 512], F32, es=ph) for i in range(2)]
        STb = [k.sb("sSTbf%d" % i, [128, 512], BF16, es=ph) for i in range(2)]
        CBm = [k.sb("sCBm%d" % i, [128, 128], F32, es=ph) for i in range(2)]
        lhs = [(k.sb("slhh%d" % i, [128, 8, 128], BF16, es=ph), k.sb("slhl%d" % i, [128, 8, 128], BF16, es=ph)) for i in range(2)]
        dtAh = k.sb("sdtAh", [128, 8, 64], BF16, es=ph)
        dtAl = k.sb("sdtAl", [128, 8, 64], BF16, es=ph)
        trib = k.sb("strib", [128, 4, 128], BF16, es=ph)
        Dec = [k.sb("sDec%d" % i, [128, 512], F32, es=ph) for i in range(2)]
        Mt = [k.sb("sMt%d" % i, [128, 512], BF16, es=ph) for i in range(2)]
        tA = [k.sb("stA%d" % i, [128, 512], F32, es=ph) for i in range(1)] + [View(cv, cv[:, 512:1024]), View(cv, cv[:, 0:512])]
        rbc = raw
        tri = k.sb("stri", [128, 4, 128], F32, es=ph)
        ones_f = k.sb("sones", [128, 128], F32, es=ph)
        cw = k.sb("scw", [128, 24, 3], F32, es=ph)
        cb = k.sb("scb", [128, 24], F32, es=ph)
        dtb = k.sb("sdtb", [128, 64], F32, es=ph)
        abc = k.sb("sabc", [128, 64], F32, es=ph)
        dsk = k.sb("sdsk", [128, 32], F32, es=ph)
        ng = k.sb("sng", [128, 16], F32, es=ph)
        ssqp = k.sb("sssqp", [128, 8, 4], F32, es=ph)
        ssq = k.sb("sssq", [128, 8], F32, es=ph)
        k.dma("pool", Wdt[:], d["ssd_wdt"], writes=[Wdt])
        k.dma("sp", tri[:], d["ssd_tri"], writes=[tri])
        k.dma("pool", trib[:], d["ssd_tri"], writes=[trib])
        k.dma("sp", cw[:], d["ssd_cw"], writes=[cw])
        k.dma("sp", cb[:], d["ssd_cb"], writes=[cb])
        k.dma("sp", dtb[:], d["ssd_dtb"].partition_broadcast(128), writes=[dtb])
        k.dma("sp", abc[:], d["ssd_alog"].partition_broadcast(128), writes=[abc])
        k.dma("sp", dsk[:], d["ssd_dsk"].partition_broadcast(128), writes=[dsk])
        k.dma("sp", ng[:], d["ssd_ng"], writes=[ng])
        k.op("pool", lambda e: e.memset(ones_f[:], 1.0), [], [ones_f])
        k.op("act", lambda e: e.activation(abc[:], abc[:], AF.Exp), [abc], [abc])
        k.op("dve", lambda e: e.tensor_scalar(abc[:], abc[:], -1.0, None, ALU.mult), [abc], [abc])
        ssd_out = self.dout["ssd_out"]
        Gm = self.Gm[l]
        bf_view = yacc.t.bitcast(BF16)
        n3 = 0
        ntr = 0

        def bc8(buf, c, col0):
            return bass.AP(buf.t, c * 64 + col0, [[8 * 64, 128], [1, 8], [0, 64]])

        for blk in range(2):
            tiles = [2 * blk, 2 * blk + 1]
            seqs = [(i * 256, 256) for i in range(4)] if blk == 0 else [(0, 1024)]
            ns = len(seqs)
            temps = ([(raw, raw[:, 0:512]), (raw, raw[:, 512:1024])], [(cv, cv[:, 0:512]), (cv, cv[:, 512:1024])])
            self.norm_mod(l, 1, hTb, ph, tiles=tiles, temps=temps)
            for c in range(8):
                hsl = hTb[c // 4]
                o_ = (c % 4) * 128
                pd = self.psum()
                k.mm(pd, [(pd[:, 0:64], hsl[:, kc, o_:o_ + 128], Wdt[:, kc, :], kc == 0, kc == 7) for kc in range(8)], [hsl, Wdt])
                k.op("dve", lambda e: e.tensor_tensor(dt_t[:, c, :], pd[:, 0:64], dtb[:], ALU.add), [pd, dtb], [dt_t])
                k.op("act", lambda e: e.activation(dt_t[:, c, :], dt_t[:, c, :], AF.Exp), [dt_t], [dt_t])
                k.op("act", lambda e: e.activation(dt_t[:, c, :], dt_t[:, c, :], AF.Ln, bias=1.0), [dt_t], [dt_t])
                k.op("dve", lambda e: e.tensor_tensor(dtA[:, c, :], dt_t[:, c, :], abc[:], ALU.mult), [dt_t, abc], [dtA])
                k.op("act", lambda e: e.copy(dtAh[:, c, :], dtA[:, c, :]), [dtA], [dtAh])
                k.op("dve", lambda e: e.tensor_tensor(dtAl[:, c, :], dtA[:, c, :], dtAh[:, c, :], ALU.subtract), [dtA, dtAh], [dtAl])
                pc = self.psum()
                k.mm(pc, [(pc[:, 0:32], tri[:, 0, :], dtA[:, c, 0:32], True, True)], [tri, dtA])
                k.mm(pc, [(pc[:, 32:64], tri[:, 1, :], dtA[:, c, 32:64], True, True)], [tri, dtA])
                k.op("act", lambda e: e.activation(ecum[:, c, :], pc[:, 0:64], AF.Exp), [pc], [ecum])
                pe_ = self.psum()
                k.mm(pe_, [(pe_[:, 0:32], tri[:, 2, :], dtA[:, c, 0:32], True, True)], [tri, dtA])
                k.mm(pe_, [(pe_[:, 32:64], tri[:, 3, :], dtA[:, c, 32:64], True, True)], [tri, dtA])
                k.op("act", lambda e: e.activation(edec[:, c, :], pe_[:, 0:64], AF.Exp), [pe_], [edec])
                pg = self.psum()
                k.mm(pg, [(pg[:, 0:64], ones_f[:], dtA[:, c, :], True, True)], [ones_f, dtA])
                k.op("act", lambda e: e.activation(gdec[:, c, :], pg[:, 0:64], AF.Exp), [pg], [gdec])

            def conv_chunk(Wt, col0, cidx, dst):
                pps = []
                for ti in range(2):
                    pp = self.psum()
                    k.mm(pp, [(pp[:], Wt[:, kc, col0:col0 + 128], hTb[ti][:, kc, :], kc == 0, kc == 7) for kc in range(8)], [Wt, hTb[ti]])
                    pps.append(pp)
                    k.op("act", lambda e: e.activation(dst[:, ti * 512:(ti + 1) * 512], pp[:], AF.Identity, bias=cb[:, cidx:cidx + 1], scale=cw[:, cidx, 1:2]),
                         [pp, cw, cb], [dst])
                nseg = ns // 2 if ns > 1 else 1
                T = 512 // nseg
                for ti in range(2):
                    pp = pps[ti]
                    d3 = dst[:, ti * 512:(ti + 1) * 512].rearrange("p (a b) -> p a b", a=nseg)
                    p3 = pp[:].rearrange("p (a b) -> p a b", a=nseg)
                    k.op("dve", lambda e: e.scalar_tensor_tensor(d3[:, :, 1:T], p3[:, :, 0:T - 1], cw[:, cidx, 0:1], d3[:, :, 1:T], ALU.mult, ALU.add), [pp, cw, dst], [dst])
                    k.op("dve", lambda e: e.scalar_tensor_tensor(d3[:, :, 0:T - 1], p3[:, :, 1:T], cw[:, cidx, 2:3], d3[:, :, 0:T - 1], ALU.mult, ALU.add), [pp, cw, dst], [dst])
                if ns == 1:
                    k.op("dve", lambda e: e.scalar_tensor_tensor(dst[:, 512:513], pps[0][:, 511:512], cw[:, cidx, 0:1], dst[:, 512:513], ALU.mult, ALU.add), [pps[0], cw, dst], [dst])
                    k.op("dve", lambda e: e.scalar_tensor_tensor(dst[:, 511:512], pps[1][:, 0:1], cw[:, cidx, 2:3], dst[:, 511:512], ALU.mult, ALU.add), [pps[1], cw, dst], [dst])
                k.op("act", lambda e: e.activation(dst[:], dst[:], AF.Silu), [dst], [dst])

            for g in range(4):
                for hf in range(2):
                    k.dma("pool", Wp[:, hf * 4:(hf + 1) * 4, :], d["ssd_wx"][g, :, hf * 4:(hf + 1) * 4, :], writes=[Wp])
                for xc in range(4):
                    cvb = (raw, cv)[xc % 2]
                    conv_chunk(Wp, xc * 128, 4 * g + xc, cvb)
                    for c in range(8):
                        pt = self.psum()
                        k.tr(pt, pt[:, 0:128], cvb[:, c * 128:(c + 1) * 128], self.ident_f[:], [cvb, self.ident_f])
                        ntr += 1
                        if ntr % 2:
                            k.op("act", lambda e: e.copy(xtok[:, c, xc * 128:(xc + 1) * 128], pt[:, 0:128]), [pt], [xtok])
                        else:
                            k.op("dve", lambda e: e.tensor_copy(xtok[:, c, xc * 128:(xc + 1) * 128], pt[:, 0:128]), [pt], [xtok])
                k.dma("pool", Wp[:, :, 0:256], d["ssd_wbc"][g], writes=[Wp])
                conv_chunk(Wp, 0, 16 + g, raw)
                k.op("pool", lambda e: e.tensor_copy(BcT[:], raw[:]), [raw], [BcT])
                for c in range(8):
                    pt = self.psum()
                    k.tr(pt, pt[:, 0:128], raw[:, c * 128:(c + 1) * 128], self.ident_f[:], [raw, self.ident_f])
                    k.op("act", lambda e: e.copy(Btok[:, c, :], pt[:, 0:128]), [pt], [Btok])
                conv_chunk(Wp, 128, 20 + g, cv)
                k.op("pool", lambda e: e.tensor_copy(CcT[:], cv[:]), [cv], [CcT])
                for hf in range(2):
                    k.dma("pool", Wp[:, hf * 4:(hf + 1) * 4, :], d["ssd_wz"][g, :, hf * 4:(hf + 1) * 4, :], writes=[Wp])
                for si, (s0, T) in enumerate(seqs):
                    nch = T // 128
                    c_first = s0 // 128
                    for e_ in range(2):
                        if blk == 1:
                            k.dma("sp", STs[e_][:].rearrange("p (a b) -> p a b", a=8), d["ssd_s0"][:, e_, 8 * g:8 * g + 8, :], writes=[STs[e_]])
                        else:
                            k.op("pool", lambda e: e.memset(STs[e_][:], 0.0), [], [STs[e_]])
                        k.op("act", lambda e: e.copy(STb[e_][:], STs[e_][:]), [STs[e_]], [STb[e_]])
                    def unit(e_, c, first, u):
                        ST32, STbf = STs[e_], STb[e_]
                        hc0 = e_ * 32 + 8 * g
                        cs = slice(c * 128, (c + 1) * 128)
                        hc0 = e_ * 32 + 8 * g
                        pcb = self.psum()
                        k.mm(pcb, [(pcb[:, 0:128], BcT[:, cs], CcT[:, cs], True, True)], [BcT, CcT])
                        cbm = CBm[e_]
                        yield
                        k.op("dve", lambda e: e.tensor_tensor(cbm[:], pcb[:, 0:128], tri[:, e_, :], ALU.mult), [pcb, tri], [cbm])
                        dx = dtx[e_]
                        k.op("pool", lambda e: e.tensor_tensor(dx[:].rearrange("p (a b) -> p a b", a=8), xtok[:, c, :].rearrange("p (a b) -> p a b", a=8),
                                                               bc8(dt_t, c, hc0), ALU.mult), [xtok, dt_t], [dx])
                        pyd = self.pyd[e_]
                        tri_b = bass.AP(trib.t, e_ * 128, [[512, 128], [0, 8], [1, 128]])
                        lhh, lhl = lhs[e_]
                        k.op("dve", lambda e: e.tensor_tensor(lhh[:], tri_b, bass.AP(dtAh.t, c * 64 + hc0, [[512, 128], [1, 8], [0, 128]]), ALU.mult), [trib, dtAh], [lhh])
                        k.op("dve", lambda e: e.tensor_tensor(lhl[:], tri_b, bass.AP(dtAl.t, c * 64 + hc0, [[512, 128], [1, 8], [0, 128]]), ALU.mult), [trib, dtAl], [lhl])
                        cbm_b = bass.AP(cbm.t, 0, [[128, 128], [0, 4], [1, 128]])
                        for half in range(2):
                            i3 = e_
                            psg = self.psum()
                            mms = [(psg[:], trib[:, 2 + e_, :], lhh[:, half * 4:(half + 1) * 4, :].rearrange("p a b -> p (a b)"), True, False),
                                   (psg[:], trib[:, 2 + e_, :], lhl[:, half * 4:(half + 1) * 4, :].rearrange("p a b -> p (a b)"), False, True)]
                            yield
                            k.mm(psg, mms, [lhh, lhl, trib])
                            yield
                            k.op("act", lambda e: e.activation(Dec[i3][:], psg[:], AF.Exp), [psg], [Dec[i3]])
                            k.op("dve", lambda e: e.tensor_tensor(Mt[i3][:].rearrange("p (a b) -> p a b", a=4), Dec[i3][:].rearrange("p (a b) -> p a b", a=4), cbm_b, ALU.mult),
                                 [Dec[i3], cbm], [Mt[i3]])
                            yield
                            k.mm(pyd, [(pyd[:, (half * 4 + q) * 64:(half * 4 + q + 1) * 64], Mt[i3][:, q * 128:(q + 1) * 128], dx[:, (half * 4 + q) * 64:(half * 4 + q + 1) * 64], True, True)
                                       for q in range(4)], [Mt[i3], dx])
                        yield
                        pyo = self.psum()
                        k.mm(pyo, [(pyo[:], CcT[:, cs], STbf[:], True, True)], [CcT, STbf])
                        t_ = tA[(2 * u) % 3]
                        k.op("dve", lambda e: e.tensor_tensor(t_[:].rearrange("p (a b) -> p a b", a=8), pyo[:].rearrange("p (a b) -> p a b", a=8),
                                                              bc8(ecum, c, hc0), ALU.mult), [pyo, ecum], [t_])
                        if first:
                            k.op("dve", lambda e: e.tensor_tensor(yacc[:, c, :], t_[:], pyd[:], ALU.add), [t_, pyd], [yacc])
                        else:
                            k.op("pool", lambda e: e.tensor_tensor(yacc[:, c, :], yacc[:, c, :], t_[:], ALU.add), [t_, yacc], [yacc])
                            k.op("dve", lambda e: e.tensor_tensor(yacc[:, c, :], yacc[:, c, :], pyd[:], ALU.add), [pyd, yacc], [yacc])
                        dxd = dtxd[e_]
                        k.op("pool", lambda e: e.tensor_tensor(dxd[:].rearrange("p (a b) -> p a b", a=8), dx[:].rearrange("p (a b) -> p a b", a=8),
                                                               bc8(edec, c, hc0), ALU.mult), [dx, edec], [dxd])
                        yield
                        pU = self.psum()
                        k.mm(pU, [(pU[:], Btok[:, c, :], dxd[:], True, True)], [Btok, dxd])
                        t2 = tA[(2 * u + 1) % 3]
                        k.op("dve", lambda e: e.tensor_tensor(t2[:].rearrange("p (a b) -> p a b", a=8), ST32[:].rearrange("p (a b) -> p a b", a=8),
                                                              bc8(gdec, c, hc0), ALU.mult), [ST32, gdec], [t2])
                        k.op("dve", lambda e: e.tensor_tensor(ST32[:], t2[:], pU[:], ALU.add), [t2, pU], [ST32])
                        k.op("act", lambda e: e.copy(STbf[:], ST32[:]), [ST32], [STbf])

                    written = set()
                    for step in range(nch):
                        gens = []
                        for e_ in range(2):
                            c = c_first + step if e_ == 0 else c_first + nch - 1 - step
                            gens.append(unit(e_, c, c not in written, n3))
                            written.add(c)
                            n3 += 1
                        while gens:
                            for gn in list(gens):
                                try:
                                    next(gn)
                                except StopIteration:
                                    gens.remove(gn)
                    if blk == 0:
                        for e_ in range(2):
                            k.dma("sp", ssd_out[:, si, e_, 8 * g:8 * g + 8, :], STs[e_][:].rearrange("p (a b) -> p a b", a=8), reads=[STs[e_]], is_output=True)
                dskb = bass.AP(dsk.t, 8 * g, [[32, 128], [1, 8], [0, 64]])
                tsets = [(tA[0], View(cv, cv[:, 0:512]), View(cv, cv[:, 512:1024])),
                         (Dec[0], View(raw, raw[:, 0:512]), View(raw, raw[:, 512:1024]))]

                def unit_d(c, ts):
                    t_, t2, t3 = ts
                    k.op("pool", lambda e: e.tensor_tensor(t_[:].rearrange("p (a b) -> p a b", a=8), xtok[:, c, :].rearrange("p (a b) -> p a b", a=8), dskb, ALU.mult), [xtok, dsk], [t_])
                    k.op("pool", lambda e: e.tensor_tensor(yacc[:, c, :], yacc[:, c, :], t_[:], ALU.add), [t_, yacc], [yacc])
                    hsl = hTb[c // 4]
                    o_ = (c % 4) * 128
                    pz = self.psum()
                    k.mm(pz, [(pz[:], hsl[:, kc, o_:o_ + 128], Wp[:, kc, :], kc == 0, kc == 7) for kc in range(8)], [hsl, Wp])
                    yield
                    k.op("act", lambda e: e.activation(t2[:], pz[:], AF.Silu), [pz], [t2])
                    k.op("dve", lambda e: e.tensor_tensor(t2[:], t2[:], yacc[:, c, :], ALU.mult), [t2, yacc], [t2])
                    k.op("pool", lambda e: e.tensor_tensor(t3[:], t2[:], t2[:], ALU.mult), [t2], [t3])
                    k.op("dve", lambda e: e.tensor_reduce(ssqp[:, c, g:g + 1], t3[:], AX.X, ALU.add), [t3], [ssqp])
                    yield
                    for q in range(4):
                        pt = self.psum()
                        k.tr(pt, pt[:, 0:128], t2[:, q * 128:(q + 1) * 128], self.ident_f[:], [t2, self.ident_f])
                        fc = 4 * g + q
                        if q % 2:
                            k.op("act", lambda e: e.activation(yT[:, fc, c * 128:(c + 1) * 128], pt[:, 0:128], AF.Identity, scale=ng[:, fc:fc + 1]), [pt, ng], [yT])
                        else:
                            k.op("dve", lambda e: e.tensor_scalar(yT[:, fc, c * 128:(c + 1) * 128], pt[:, 0:128], ng[:, fc:fc + 1], None, ALU.mult), [pt, ng], [yT])
                        yield

                for c2 in range(0, 8, 2):
                    gens = [unit_d(c2, tsets[0]), unit_d(c2 + 1, tsets[1])]
                    while gens:
                        for gn in list(gens):
                            try:
                                next(gn)
                            except StopIteration:
                                gens.remove(gn)
            k.op("dve", lambda e: e.tensor_reduce(ssq[:], ssqp[:], AX.X, ALU.add), [ssqp], [ssq])
            k.op("act", lambda e: e.activation(ssq[:], ssq[:], AF.Sqrt, bias=self.eps_t[:, 0:1], scale=1.0 / 2048), [ssq, self.eps_t], [ssq])
            k.op("dve", lambda e: e.reciprocal(ssq[:], ssq[:]), [ssq], [ssq])
            for c in range(8):
                i3 = c % 3
                k.op("pool", lambda e: e.tensor_scalar(Dec[0][:, i3 * 128:(i3 + 1) * 128], ones_f[:], ssq[:, c:c + 1], None, ALU.mult), [ones_f, ssq], [Dec[0]])
                pr_ = self.psum()
                k.mm(pr_, [(pr_[:, 0:128], Dec[0][:, i3 * 128:(i3 + 1) * 128], self.ident_f[:], True, True)], [Dec[0], self.ident_f])
                k.op("act", lambda e: e.copy(rbc[:, c * 128:(c + 1) * 128], pr_[:, 0:128]), [pr_], [rbc])
            for dp in range(8):
                k.dma("pool", Wo[:], d["ssd_wo"][:, :, dp * 128:(dp + 1) * 128], writes=[Wo])
                for d2 in range(1):
                    dc = dp
                    for ti, t in enumerate(tiles):
                        j = 0 if t < 2 else 1
                        py = self.psum()
                        k.mm(py, [(py[:], Wo[:, kc, d2 * 128:(d2 + 1) * 128], yT[:, kc, ti * 512:(ti + 1) * 512], kc == 0, kc == 15) for kc in range(16)], [Wo, yT])
                        t_ = tA[(dc + ti) % 3]
                        k.op("dve", lambda e: e.tensor_tensor(t_[:], py[:], rbc[:, ti * 512:(ti + 1) * 512], ALU.mult), [py, rbc], [t_])
                        x = self.x[t]
                        k.op("dve", lambda e: e.scalar_tensor_tensor(x[:, dc, :], t_[:], Gm[:, 1, dc, j:j + 1], x[:, dc, :], ALU.mult, ALU.add), [t_, Gm, x], [x])


def host_layout(inp):
    f = lambda a: np.ascontiguousarray(a, dtype=np.float32)
    sh = {}
    w = inp["w_ffn_in"].reshape(8, 8, 128, 2, NFC, 128)
    sh["w1"] = f(w.transpose(0, 4, 2, 3, 1, 5).reshape(8 * NFC, 128, 2048))
    w = inp["w_ffn_out"].reshape(8, NFC, 128, 1024)
    sh["w2"] = f(w.transpose(0, 2, 1, 3).reshape(8, 128, NFC * 1024))
    w = inp["w_ada"].reshape(4, 8, 128, 18, 512)
    sh["wada"] = f(w.transpose(0, 3, 2, 1, 4).reshape(4, 18, 128, 4096))
    sh["bada"] = f(inp["b_ada"].reshape(4, 72, 128).transpose(2, 0, 1))
    sh["ng"] = f(inp["norm_g"].reshape(4, 3, 8, 128).transpose(3, 0, 1, 2))
    sh["fg"] = f(inp["final_g"].reshape(8, 128).T)
    sh["ident"] = np.eye(128, dtype=np.float32)
    kcm = lambda w: f(w.reshape(8, 128, -1).transpose(1, 0, 2).reshape(128, -1))
    sh["gm_wu"] = kcm(inp["gm_w_in"][0][:, :1024]); sh["gm_wv"] = kcm(inp["gm_w_in"][0][:, 1024:]); sh["gm_wo"] = kcm(inp["gm_w_out"][0])
    sh["gm_wsT"] = f(inp["gm_w_s"][0].transpose(2, 0, 1).reshape(128, 1024))
    sh["gm_lng"] = f(inp["gm_ln_g"][0].reshape(1, 1024)); sh["gm_lnb"] = f(inp["gm_ln_b"][0].reshape(1, 1024))
    sh["gm_bs"] = f(inp["gm_b_s"][0].reshape(1, 1024))
    w = inp["nat_w_qkv"][0].reshape(8, 128, 3, 8, 128)
    sh["nat_wqkv"] = f(w.transpose(3, 1, 2, 0, 4).reshape(8, 128, 3072))
    sh["nat_wo"] = kcm(inp["nat_w_out"][0])
    kc_ = np.arange(64)[:, None]; qc_ = np.arange(64)[None, :]
    idx = np.clip(kc_ - qc_ + 15, 0, 30)
    tb = inp["nat_rpb"][0][:, :, idx]
    tb = tb.reshape(8, 2, 15, 64, 64).transpose(0, 1, 4, 2, 3)
    sh["nat_tbT"] = f(tb.reshape(8, 128, 960))
    cs = np.clip(np.arange(64) - 8, 0, 48)[None, :]
    valid = (kc_ >= cs) & (kc_ < cs + 16)
    mk = np.where(valid.T, 0.0, -30000.0).astype(np.float32)
    sh["nat_maskT"] = np.ascontiguousarray(np.concatenate([mk, mk], 0))
    wi = inp["gla_w_in"][0]
    dk = np.arange(128)
    partner = np.where(dk % 64 < 32, dk + 32, dk - 32)
    gw = np.zeros((4, 1024, 1024), np.float32)
    for h in range(4):
        q = wi[:, h * 128:(h + 1) * 128]; kk = wi[:, 512 + h * 128:512 + (h + 1) * 128]
        gw[h, :, 0:128] = q; gw[h, :, 128:256] = kk; gw[h, :, 256:384] = q[:, partner]; gw[h, :, 384:512] = kk[:, partner]
        gw[h, :, 512:768] = wi[:, 1024 + h * 256:1024 + (h + 1) * 256]
        gw[h, :, 768:1024] = wi[:, 2048 + h * 256:2048 + (h + 1) * 256]
    sh["gla_win"] = f(gw.reshape(4, 8, 128, 1024).transpose(0, 2, 1, 3))
    sh["gla_wo"] = f(inp["gla_w_out"][0].reshape(4, 2, 128, 1024).transpose(0, 2, 1, 3))
    sh["gla_wa1"] = f(inp["gla_w_a1"][0].transpose(1, 0, 2).reshape(8, 128, 32).transpose(1, 0, 2))
    a2 = np.zeros((33, 4, 2, 128), np.float32)
    for e in range(2):
        a2[e * 16:(e + 1) * 16, :, e, :] = inp["gla_w_a2"][0][e].reshape(16, 4, 128)
        a2[32, :, e, :] = inp["gla_b_a"][0][e].reshape(4, 128)
    sh["gla_wa2"] = f(a2.reshape(33, 4, 256))
    jj = np.arange(128)[:, None]; ii = np.arange(128)[None, :]
    c16 = -1.0 / 16.0
    tri = np.stack([(jj <= ii), (jj > ii), (jj >= ii), (jj < ii)], 1).astype(np.float32) * c16
    sh["gla_tri"] = f(tri)
    sh["gla_msk"] = f(np.stack([(jj <= ii), (jj >= ii)], 1).astype(np.float32))
    tpos = np.arange(1024)
    inv = 10000.0 ** (-np.arange(0, 64, 2, dtype=np.float64) / 64.0)
    pos = np.where((dk < 64)[:, None], (tpos // 64)[None, :], (tpos % 64)[None, :]).astype(np.float64)
    ang = pos * inv[dk % 32][:, None]
    sgn = np.where(dk % 64 < 32, -1.0, 1.0)[:, None]
    sh["gla_cos"] = f(np.cos(ang)); sh["gla_sin"] = f(np.sin(ang) * sgn)
    sh["gla_ng"] = f(inp["gla_norm_g"][0].reshape(8, 128).T)
    wi = inp["ssd_w_in"][0]
    kc3 = lambda w: f(w.reshape(8, 128, -1).transpose(1, 0, 2))
    sh["ssd_wz"] = f(np.stack([kc3(wi[:, 512 * g:512 * (g + 1)]) for g in range(4)]))
    sh["ssd_wx"] = f(np.stack([kc3(wi[:, 2048 + 512 * g:2048 + 512 * (g + 1)]) for g in range(4)]))
    sh["ssd_wbc"] = f(np.stack([kc3(np.concatenate([wi[:, 4096 + 128 * g:4096 + 128 * (g + 1)], wi[:, 4608 + 128 * g:4608 + 128 * (g + 1)]], 1)) for g in range(4)]))
    sh["ssd_wdt"] = kc3(wi[:, 5120:5184])
    sh["ssd_wo"] = f(inp["ssd_w_out"][0].reshape(16, 128, 1024).transpose(1, 0, 2))
    sh["ssd_tri"] = f(np.stack([(jj <= ii), (jj >= ii), (jj > ii), (jj < ii)], 1).astype(np.float32))
    sh["ssd_cw"] = f(inp["ssd_conv_w"][0].reshape(3, 24, 128).transpose(2, 1, 0))
    sh["ssd_cb"] = f(inp["ssd_conv_b"][0].reshape(24, 128).T)
    sh["ssd_dtb"] = f(inp["ssd_dt_bias"][0].reshape(1, 64)); sh["ssd_alog"] = f(inp["ssd_a_log"][0].reshape(1, 64))
    sh["ssd_dsk"] = f(inp["ssd_d"][0].reshape(1, 32)); sh["ssd_ng"] = f(inp["ssd_norm_g"][0].reshape(16, 128).T)
    return sh


def core_inputs(c, inp, sh):
    b = c // 4
    xp = inp["x_prompt"][4 * c:4 * c + 4].reshape(1024, D)
    xs = inp["x_sample"][b]
    xall = np.concatenate([xp, xs], 0)
    m = dict(sh)
    m["xT"] = np.ascontiguousarray(xall.T.reshape(8, 128, NTOK).transpose(1, 0, 2), dtype=np.float32)
    cond = np.stack([inp["c_ctx"], inp["c"][b]], 1)
    m["cond"] = np.ascontiguousarray(cond.reshape(8, 128, 2).transpose(1, 0, 2), dtype=np.float32)
    ck = inp["cache_nat_k"][b, 0].reshape(8, 2, 256, 64)
    m["nat_ckT"] = np.ascontiguousarray(ck.transpose(0, 1, 3, 2).reshape(8, 128, 256), dtype=np.float32)
    m["ssd_s0"] = np.ascontiguousarray(inp["state_ssd"][b, 0].transpose(3, 0, 1, 2), dtype=np.float32)
    m["gla_s0"] = np.ascontiguousarray(inp["state_gla"][b, 0].transpose(2, 0, 1, 3), dtype=np.float32)
    cv = inp["cache_nat_v"][b, 0].reshape(8, 2, 2, 128, 64)
    m["nat_cv"] = np.ascontiguousarray(cv.transpose(0, 3, 2, 1, 4).reshape(8, 128, 256), dtype=np.float32)
    return m


_CACHE = {}


def run(inputs, stop=None, debug=False, start=None, xover=None):
    inp = {k: np.asarray(v) for k, v in inputs.items()}
    key = (stop, debug, start)
    if key not in _CACHE:
        p = Prog(stop=stop, debug=debug, start=start)
        p.build()
        _CACHE[key] = p
    p = _CACHE[key]
    sh = host_layout(inp)
    in_maps = [core_inputs(c, inp, sh) for c in range(NCORES)]
    if xover is not None:
        for c in range(NCORES):
            in_maps[c]["xT"] = xover[c]
    if p.lite:
        for c in range(NCORES):
            in_maps[c]["w1"] = sh["w1"][:1]; in_maps[c]["w2"] = sh["w2"][:1]
            in_maps[c]["wada"] = sh["wada"][start[1]:start[1] + 1]
    res = run_bass_kernel_spmd(p.nc, in_maps, core_ids=list(range(NCORES)))
    return p, res.results


def assemble_y(results):
    yp = np.zeros((32, 256, D), np.float32)
    ys = np.zeros((2, 1024, D), np.float32)
    for c in range(NCORES):
        yT = results[c]["yT"]
        y = yT.transpose(2, 1, 0).reshape(NTOK, D)
        yp[4 * c:4 * c + 4] = y[:1024].reshape(4, 256, D)
        q = c % 4
        ys[c // 4, q * 256:(q + 1) * 256] = y[1024 + q * 256:1024 + (q + 1) * 256]
    return yp, ys


def kernel(**inputs):
    p, results = run(inputs)
    yp, ys = assemble_y(results)
    gla = np.zeros((32, 1, 2, 4, 128, 256), np.float32)
    ck = np.zeros((32, 1, 16, 256, 64), np.float32)
    cvv = np.zeros((32, 1, 16, 256, 64), np.float32)
    ssd = np.zeros((32, 1, 2, 32, 64, 128), np.float32)
    for c in range(NCORES):
        r = results[c]
        gla[4 * c:4 * c + 4, 0] = r["gla_out"].transpose(1, 2, 3, 0, 4)
        kc = r["kc_out"].reshape(2, 64, 8, 4, 256)
        ck[4 * c:4 * c + 4, 0] = kc.transpose(3, 2, 0, 4, 1).reshape(4, 16, 256, 64)
        vc = r["vc_out"].reshape(128, 8, 4, 2, 2, 64)
        cvv[4 * c:4 * c + 4, 0] = vc.transpose(2, 1, 4, 3, 0, 5).reshape(4, 16, 256, 64)
        ssd[4 * c:4 * c + 4, 0] = r["ssd_out"].transpose(1, 2, 3, 4, 0)
    return yp, ys, gla, ck, cvv, ssd
```
